# Optimizing a Trainium2 kernel written in Bass

```python
import jax, jax.numpy as jnp
from jax import lax
import numpy as np

D_MODEL = 1024
BATCH = 2
SEQ = 8192
DEPTH = 2
DEC_BATCH = 32
DEC_SEQ = 16
PAST_LEN = 4096

CHUNK = 64
EXPAND = 2
D_INNER = EXPAND * D_MODEL
SGU_CHUNK = 128
SGU_GROUPS = 8
SGU_GROUP_DIM = D_INNER // SGU_GROUPS
MLSTM_HEADS = 4
MLSTM_HEAD_DIM = D_INNER // MLSTM_HEADS
QKV_BLOCK = 4
QKV_NBLK = D_INNER // QKV_BLOCK
CONV_W = 4
N_MIXERS = 2
N_A = (DEPTH + 1) // 2
N_B = DEPTH // 2
RMS_EPS = 1e-6
LN_EPS = 1e-5

kernel_name = "chunk_sgu_mlstm_hybrid_step"


def _rmsnorm(x, g):
    xf = x.astype(jnp.float32)
    y = xf * lax.rsqrt(jnp.mean(xf * xf, axis=-1, keepdims=True) + RMS_EPS)
    return (y * g.astype(jnp.float32)).astype(x.dtype)


def _layernorm(x, g, b=None):
    xf = x.astype(jnp.float32)
    mu = jnp.mean(xf, axis=-1, keepdims=True)
    var = jnp.mean(jnp.square(xf - mu), axis=-1, keepdims=True)
    y = (xf - mu) * lax.rsqrt(var + LN_EPS) * g.astype(jnp.float32)
    if b is not None:
        y = y + b.astype(jnp.float32)
    return y.astype(x.dtype)


def _sgu_branch(xn, w_in, ln_g, ln_b, w_s, b_s, w_out, prompt):
    E = D_INNER
    proj = xn @ w_in
    uv = jax.nn.gelu(proj[..., :2 * E])
    u, v = uv[..., :E], uv[..., E:]
    z = proj[..., 2 * E:]
    v = _layernorm(v, ln_g, ln_b)
    bsz, t_len, _ = v.shape
    causal = jnp.tril(jnp.ones((SGU_CHUNK, SGU_CHUNK), dtype=bool))
    w_m = jnp.where(causal, w_s, jnp.zeros_like(w_s))
    if prompt:
        vc = v.reshape(bsz, t_len // SGU_CHUNK, SGU_CHUNK, SGU_GROUPS, SGU_GROUP_DIM)
        s = jnp.einsum('gts,bnsgc->bntgc', w_m, vc) + b_s.T[:, :, None]
    else:
        vc = v.reshape(bsz, t_len, SGU_GROUPS, SGU_GROUP_DIM)
        s = jnp.einsum('gts,bsgc->btgc', w_m[:, :t_len, :t_len], vc) + b_s[:, :t_len].T[:, :, None]
    s = s.reshape(bsz, t_len, E)
    out = u * s * jax.nn.silu(z)
    return out @ w_out, v


def _mlstm_chunk(carry, inp):
    C, n, m = carry
    q, k, v, ig, lf = inp
    L = q.shape[-2]
    b = jnp.cumsum(lf, axis=-1)
    causal = jnp.tril(jnp.ones((L, L), dtype=bool))
    log_d = jnp.where(causal, b[..., :, None] - b[..., None, :] + ig[..., None, :], -jnp.inf)
    m_inter = b + m[..., None]
    m_t = jnp.maximum(m_inter, jnp.max(log_d, axis=-1))
    w_intra = jnp.exp(log_d - m_t[..., None])
    w_inter = jnp.exp(m_inter - m_t)
    s = jnp.einsum('bhtd,bhsd->bhts', q, k) * w_intra
    num = jnp.einsum('bhts,bhse->bhte', s, v) + w_inter[..., None] * jnp.einsum('bhtd,bhde->bhte', q, C)
    den = jnp.sum(s, axis=-1) + w_inter * jnp.einsum('bhtd,bhd->bht', q, n)
    h = num / jnp.maximum(jnp.abs(den), jnp.exp(-m_t))[..., None]
    b_last = b[..., -1]
    log_w = b_last[..., None] - b + ig
    m_new = jnp.maximum(b_last + m, jnp.max(log_w, axis=-1))
    w = jnp.exp(log_w - m_new[..., None])
    decay = jnp.exp(b_last + m - m_new)
    C_new = decay[..., None, None] * C + jnp.einsum('bhs,bhsd,bhse->bhde', w, k, v)
    n_new = decay[..., None] * n + jnp.einsum('bhs,bhsd->bhd', w, k)
    return (C_new, n_new, m_new), h


def _mlstm_branch(xn, w_in, conv_w, conv_b, wq, wk, wv, w_gates, b_gates, hn_g, skip, w_out,
                  C0, n0, m0, conv0):
    E, H, DH = D_INNER, MLSTM_HEADS, MLSTM_HEAD_DIM
    f32 = jnp.float32
    proj = xn @ w_in
    xm, z = proj[..., :E], proj[..., E:]
    bsz, t_len, _ = xm.shape
    xpad = jnp.concatenate([conv0.astype(xm.dtype), xm], axis=1)
    xc = conv_b + sum(xpad[:, j:j + t_len] * conv_w[j] for j in range(CONV_W))
    xc = jax.nn.silu(xc)
    new_conv = xpad[:, -(CONV_W - 1):]

    def blockdiag(a, w):
        a = a.reshape(bsz, t_len, QKV_NBLK, QKV_BLOCK)
        return jnp.einsum('btni,nio->btno', a, w).reshape(bsz, t_len, E)

    q, k, v = blockdiag(xc, wq), blockdiag(xc, wk), blockdiag(xm, wv)
    gates = (jnp.concatenate([q, k, v], axis=-1) @ w_gates + b_gates).astype(f32)
    ig = gates[..., :H]
    lf = jax.nn.log_sigmoid(gates[..., H:])

    def heads(a):
        return a.reshape(bsz, t_len, H, DH).transpose(0, 2, 1, 3).astype(f32)

    qh, kh, vh = heads(q), heads(k) * (DH ** -0.5), heads(v)
    igh, lfh = ig.transpose(0, 2, 1), lf.transpose(0, 2, 1)
    L = min(t_len, CHUNK)
    nc = t_len // L

    def chunks(a):
        return jnp.moveaxis(a.reshape(a.shape[:2] + (nc, L) + a.shape[3:]), 2, 0)

    (C, n, m), h = lax.scan(_mlstm_chunk, (C0.astype(f32), n0.astype(f32), m0.astype(f32)),
                            (chunks(qh), chunks(kh), chunks(vh), chunks(igh), chunks(lfh)))
    h = jnp.moveaxis(h, 0, 2).reshape(bsz, H, t_len, DH).transpose(0, 2, 1, 3)
    h = _layernorm(h, hn_g.reshape(H, DH)).reshape(bsz, t_len, E).astype(xm.dtype)
    out = (h + skip * xc) * jax.nn.silu(z)
    return out @ w_out, (C, n, m, new_conv)


def setup_inputs(seed: int = 0) -> dict:
    key = jax.random.key(seed)
    ks = jax.random.split(key, 32)
    E, H, DH, D = D_INNER, MLSTM_HEADS, MLSTM_HEAD_DIM, D_MODEL

    def nrm(k, shape, scale):
        return scale * jax.random.normal(k, shape, jnp.float32)

    f_bias = jnp.broadcast_to(jnp.linspace(3.0, 6.0, H, dtype=jnp.float32), (N_B, H))
    b_gates = jnp.concatenate([nrm(ks[20], (N_B, H), 0.1), f_bias + nrm(ks[21], (N_B, H), 0.1)], axis=-1)
    return {
        'x_prompt': nrm(ks[0], (BATCH, SEQ, D), 1.0),
        'x_sample': nrm(ks[1], (DEC_BATCH, DEC_SEQ, D), 1.0),
        'state_mlstm_C': nrm(ks[2], (N_B, DEC_BATCH, H, DH, DH), 0.05),
        'state_mlstm_n': nrm(ks[3], (N_B, DEC_BATCH, H, DH), 0.1),
        'state_mlstm_m': nrm(ks[4], (N_B, DEC_BATCH, H), 1.0),
        'state_mlstm_conv': nrm(ks[5], (N_B, DEC_BATCH, CONV_W - 1, E), 1.0),
        'norm_g': 1.0 + nrm(ks[6], (DEPTH, D), 0.05),
        'final_norm_g': 1.0 + nrm(ks[7], (D,), 0.05),
        'a_w_in': nrm(ks[8], (N_A, D, 3 * E), D ** -0.5),
        'a_ln_g': 1.0 + nrm(ks[9], (N_A, E), 0.05),
        'a_ln_b': nrm(ks[10], (N_A, E), 0.02),
        'a_w_s': nrm(ks[11], (N_A, SGU_GROUPS, SGU_CHUNK, SGU_CHUNK), 0.5 * SGU_CHUNK ** -0.5),
        'a_b_s': 1.0 + nrm(ks[12], (N_A, SGU_GROUPS, SGU_CHUNK), 0.1),
        'a_w_out': nrm(ks[13], (N_A, E, D), E ** -0.5),
        'b_w_in': nrm(ks[14], (N_B, D, 2 * E), D ** -0.5),
        'b_conv_w': nrm(ks[15], (N_B, CONV_W, E), CONV_W ** -0.5),
        'b_conv_b': nrm(ks[16], (N_B, E), 0.02),
        'b_wq': nrm(ks[17], (N_B, QKV_NBLK, QKV_BLOCK, QKV_BLOCK), QKV_BLOCK ** -0.5),
        'b_wk': nrm(ks[18], (N_B, QKV_NBLK, QKV_BLOCK, QKV_BLOCK), QKV_BLOCK ** -0.5),
        'b_wv': nrm(ks[19], (N_B, QKV_NBLK, QKV_BLOCK, QKV_BLOCK), QKV_BLOCK ** -0.5),
        'b_w_gates': nrm(ks[22], (N_B, 3 * E, 2 * H), (3 * E) ** -0.5),
        'b_b_gates': b_gates,
        'b_hnorm_g': 1.0 + nrm(ks[23], (N_B, E), 0.05),
        'b_skip': 1.0 + nrm(ks[24], (N_B, E), 0.05),
        'b_w_out': nrm(ks[25], (N_B, E, D), E ** -0.5),
    }


def reference(x_prompt, x_sample, state_mlstm_C, state_mlstm_n, state_mlstm_m, state_mlstm_conv,
              norm_g, final_norm_g, a_w_in, a_ln_g, a_ln_b, a_w_s, a_b_s, a_w_out,
              b_w_in, b_conv_w, b_conv_b, b_wq, b_wk, b_wv, b_w_gates, b_b_gates,
              b_hnorm_g, b_skip, b_w_out):
    E, H, DH = D_INNER, MLSTM_HEADS, MLSTM_HEAD_DIM
    xp, xs = x_prompt, x_sample
    sgu_v = []
    p_C, p_n, p_m, p_conv = [], [], [], []
    s_C, s_n, s_m, s_conv = [], [], [], []
    for i in range(DEPTH):
        j = i // N_MIXERS
        if i % N_MIXERS == 0:
            wa = (a_w_in[j], a_ln_g[j], a_ln_b[j], a_w_s[j], a_b_s[j], a_w_out[j])
            yp, _ = _sgu_branch(_rmsnorm(xp, norm_g[i]), *wa, prompt=True)
            ys, vs = _sgu_branch(_rmsnorm(xs, norm_g[i]), *wa, prompt=False)
            xp = xp + yp
            xs = xs + ys
            sgu_v.append(vs)
        else:
            wb = (b_w_in[j], b_conv_w[j], b_conv_b[j], b_wq[j], b_wk[j], b_wv[j],
                  b_w_gates[j], b_b_gates[j], b_hnorm_g[j], b_skip[j], b_w_out[j])
            zC = jnp.zeros((BATCH, H, DH, DH), jnp.float32)
            zn = jnp.zeros((BATCH, H, DH), jnp.float32)
            zm = jnp.zeros((BATCH, H), jnp.float32)
            zconv = jnp.zeros((BATCH, CONV_W - 1, E), xp.dtype)
            yp, (c1, n1, m1, cv1) = _mlstm_branch(_rmsnorm(xp, norm_g[i]), *wb, zC, zn, zm, zconv)
            ys, (c2, n2, m2, cv2) = _mlstm_branch(_rmsnorm(xs, norm_g[i]), *wb, state_mlstm_C[j],
                                                  state_mlstm_n[j], state_mlstm_m[j], state_mlstm_conv[j])
            xp = xp + yp
            xs = xs + ys
            p_C.append(c1); p_n.append(n1); p_m.append(m1); p_conv.append(cv1)
            s_C.append(c2); s_n.append(n2); s_m.append(m2); s_conv.append(cv2)
    y_prompt = _rmsnorm(xp, final_norm_g)
    y_sample = _rmsnorm(xs, final_norm_g)
    sdt = state_mlstm_C.dtype
    sgu_v_sample = jnp.stack(sgu_v).astype(x_sample.dtype)
    C_prompt = jnp.stack(p_C).astype(sdt)
    n_prompt = jnp.stack(p_n).astype(state_mlstm_n.dtype)
    m_prompt = jnp.stack(p_m).astype(state_mlstm_m.dtype)
    conv_prompt = jnp.stack(p_conv).astype(state_mlstm_conv.dtype)
    C_sample = jnp.stack(s_C).astype(sdt)
    n_sample = jnp.stack(s_n).astype(state_mlstm_n.dtype)
    m_sample = jnp.stack(s_m).astype(state_mlstm_m.dtype)
    conv_sample = jnp.stack(s_conv).astype(state_mlstm_conv.dtype)
    return (y_prompt, y_sample, sgu_v_sample, C_prompt, n_prompt, m_prompt, conv_prompt,
            C_sample, n_sample, m_sample, conv_sample)
```

```python
import numpy as np
import concourse.bass as bass
import concourse.mybir as mybir
from concourse.bass_utils import run_bass_kernel_spmd
from contextlib import ExitStack

F32 = mybir.dt.float32
BF16 = mybir.dt.bfloat16
AF = mybir.ActivationFunctionType
ALU = mybir.AluOpType
AX = mybir.AxisListType

NT = 16
DM = 1024
E = 2048
H = 4
DH = 512
RMS_EPS = 1e-6
LN_EPS = 1e-5
KSCALE = float(DH ** -0.5)


class Buf:
    __slots__ = ("name", "lw", "rd")

    def __init__(self, name):
        self.name = name
        self.lw = None
        self.rd = []


class Op:
    __slots__ = ("eng", "fn", "deps", "dma", "key", "done", "need_inc", "tag", "late")


class Prog:
    ENG = ["pe", "act", "dve", "pool", "sp"]

    def __init__(self, same_engine_sync=False):
        self.ops = {e: [] for e in self.ENG}
        self.same_engine_sync = same_engine_sync
        self.late_bufs = set()
        import os
        self.sync_engs = set(os.environ.get("KSYNC", "dve,act,pool").split(","))
        self.nbuf = 0
        import os
        self.limit = int(os.environ.get("KLIMIT", "100000000"))

    def buf(self, name=None):
        self.nbuf += 1
        return Buf(name or f"b{self.nbuf}")

    def add(self, eng, fn, rd=(), wr=(), dma=0, key=None, tag=None, cc=False):
        self.nadd = getattr(self, "nadd", 0) + 1
        if self.nadd > self.limit:
            return None
        if self.nadd == self.limit:
            import inspect
            fr = inspect.stack()[1]
            print("LAST OP", eng, fr.lineno, flush=True)
        op = Op()
        op.eng = eng
        op.fn = fn
        op.dma = dma
        op.key = key
        op.done = None
        op.need_inc = bool(dma)
        op.tag = tag
        op.late = any(id(b) in self.late_bufs for b in wr)
        deps = []
        for b in rd:
            if b.lw is not None:
                deps.append(b.lw)
        for b in wr:
            if b.lw is not None:
                deps.append(b.lw)
            deps.extend(b.rd)
        seen = set()
        dd = []
        for d in deps:
            if id(d) in seen or d is op:
                continue
            seen.add(id(d))
            if (not d.dma) and d.eng == eng and not dma:
                if eng == "pe" or not (self.same_engine_sync or d.late or eng in self.sync_engs):
                    continue
            dd.append(d)
        op.deps = dd
        for d in dd:
            d.need_inc = True
        for b in rd:
            b.rd.append(op)
        for b in wr:
            b.lw = op
            b.rd = []
        self.ops[eng].append(op)
        return op

    def emit(self, nc, stack):
        esem = {e: stack.enter_context(nc.semaphore(f"s_{e}")) for e in self.ENG}
        dsem = {}
        ecount = {e: 0 for e in self.ENG}
        dcount = {}
        for e in self.ENG:
            for op in self.ops[e]:
                if op.dma:
                    k = op.key
                    if k not in dsem:
                        dsem[k] = stack.enter_context(nc.semaphore(f"d_{len(dsem)}"))
                        dcount[k] = 0
        for e in self.ENG:
            for op in reversed(self.ops[e]):
                if not op.dma:
                    op.need_inc = True
                    break
        for e in self.ENG:
            for op in self.ops[e]:
                if op.dma == "cc":
                    dcount[op.key] += 1
                    op.done = (dsem[op.key], dcount[op.key])
                elif op.dma:
                    dcount[op.key] += 16 * int(op.dma)
                    op.done = (dsem[op.key], dcount[op.key])
                elif op.need_inc:
                    ecount[e] += 1
                    op.done = (esem[e], ecount[e])
        self.final = [(dsem[k], dcount[k]) for k in dsem] + [
            (esem[e], ecount[e]) for e in self.ENG if ecount[e] > 0]
        prog = self

        def run_engine(ename, h, extra_final=False):
            seen = {}
            for op in prog.ops[ename]:
                waits = {}
                for d in op.deps:
                    s, v = d.done
                    key = id(s)
                    if key not in waits or waits[key][1] < v:
                        waits[key] = (s, v)
                for key, (s, v) in waits.items():
                    if seen.get(key, 0) >= v:
                        continue
                    h.wait_ge(s, v)
                    seen[key] = v
                res = op.fn(h)
                if op.dma == "cc":
                    res.then_inc(op.done[0], 1)
                elif op.dma:
                    if not isinstance(res, (list, tuple)):
                        res = [res]
                    assert len(res) == int(op.dma), (op.tag, len(res), op.dma)
                    for r in res:
                        r.then_inc(op.done[0], 16)
                elif op.need_inc:
                    res.then_inc(op.done[0], 1)
            if extra_final:
                for s, v in prog.final:
                    if seen.get(id(s), 0) >= v:
                        continue
                    h.wait_ge(s, v)

        print("TOTAL OPS", getattr(self, "nadd", 0), flush=True)
        with nc.Block() as block:
            @block.tensor
            def _(h):
                run_engine("pe", h)

            @block.scalar
            def _(h):
                run_engine("act", h)

            @block.vector
            def _(h):
                run_engine("dve", h)

            @block.gpsimd
            def _(h):
                run_engine("pool", h)

            @block.sync
            def _(h):
                run_engine("sp", h, extra_final=True)


def build_program():
    nc = bass.Bass("TRN2", target_bir_lowering=False)
    P = Prog(same_engine_sync=False)
    st = ExitStack()

    def din(name, shape):
        return nc.dram_tensor(name, list(shape), F32, kind="ExternalInput").ap()

    def dout(name, shape):
        return nc.dram_tensor(name, list(shape), F32, kind="ExternalOutput").ap()

    xp = din("xp", [(NT + 1) * 128, DM])
    xs = din("xs", [64, DM])
    stC = din("stC", [16 * 512, 512])
    stn = din("stn", [4, 128, 16])
    stm = din("stm", [4, 4])
    stconv = din("stconv", [12, E])
    flags = din("flags", [1, 8])
    ngc = din("ngc", [128, 16])
    fng = din("fng", [1, DM])
    a_w_in = din("a_w_in", [DM, 3 * E])
    a_lng = din("a_lng", [128, 16])
    a_lnb = din("a_lnb", [128, 16])
    a_lng_row = din("a_lng_row", [1, E])
    a_lnb_row = din("a_lnb_row", [1, E])
    a_ws = din("a_ws", [8 * 128, 128])
    a_bs = din("a_bs", [1, 8 * 128])
    a_w_out = din("a_w_out", [E, DM])
    b_w_in = din("b_w_in", [DM, 2 * E])
    b_cw = din("b_cw", [128, 64])
    b_cb = din("b_cb", [128, 16])
    wbd = din("wbd", [128, 48 * 128])
    wbdT = din("wbdT", [128, 48 * 128])
    b_wg = din("b_wg", [128, 48 * 8])
    b_bg = din("b_bg", [1, 8])
    b_hg = din("b_hg", [128, 16])
    b_skip = din("b_skip", [128, 16])
    b_w_out = din("b_w_out", [E, DM])

    yp = dout("yp", [NT * 128, DM])
    ys = dout("ys", [64, DM])
    sguv = dout("sguv", [64, E])
    Cp = dout("Cp", [16 * 128, 512])
    np_o = dout("np_o", [128, 16])
    mp_o = dout("mp_o", [1, 4])
    convp = dout("convp", [3, E])
    Cs = dout("Cs", [4 * 16 * 128, 512])
    ns_o = dout("ns_o", [4, 128, 16])
    ms_o = dout("ms_o", [4, 4])
    convs = dout("convs", [12, E])

    x1s = nc.dram_tensor("x1s", [(NT + 1) * 128 + 64, DM], F32).ap()
    summ_in = [nc.dram_tensor(f"summ_in{i}", [512, 512], F32) for i in range(4)]
    summ_out = [nc.dram_tensor(f"summ_out{i}", [4 * 512, 512], F32) for i in range(4)]
    summ_in_m = nc.dram_tensor("summ_in_m", [128, 512], F32)
    summ_out_m = nc.dram_tensor("summ_out_m", [4 * 128, 512], F32)

    with st:
        def sb(name, shape, dt=F32):
            return st.enter_context(nc.sbuf_tensor(name, list(shape), dt))

        BIGW = sb("BIGW", [128, 65536], BF16)
        F32A = sb("F32A", [128, 8192], F32)
        Win0 = BIGW[:, 0:49152].rearrange("p (a c) -> p a c", a=8)
        Wout0 = BIGW[:, 49152:65536].rearrange("p (a c) -> p a c", a=16)
        Win1 = BIGW[:, 0:32768].rearrange("p (a c) -> p a c", a=8)
        Wout1 = BIGW[:, 32768:49152].rearrange("p (a c) -> p a c", a=16)
        SP = 49152
        xcT = BIGW[:, SP:SP + 2048].rearrange("p (a c) -> p a c", a=16)
        xmT = BIGW[:, SP + 2048:SP + 4096].rearrange("p (a c) -> p a c", a=16)
        qT = BIGW[:, SP + 4096:SP + 6144].rearrange("p (a c) -> p a c", a=16)
        kT = BIGW[:, SP + 6144:SP + 8192].rearrange("p (a c) -> p a c", a=16)
        kw = BIGW[:, SP + 8192:SP + 10240]
        vtok = BIGW[:, SP + 10240:SP + 12288]
        Cb0 = BIGW[:, SP + 12288:SP + 14336].rearrange("p (a c) -> p a c", a=4)
        Cb = [Cb0, Cb0]
        JUNK = BIGW[:, SP + 14336:SP + 16384].bitcast(F32)
        tmpZ_2 = BIGW[:, SP + 14336:SP + 15360].bitcast(F32)
        tmpS_2 = BIGW[:, SP + 15360:SP + 16384].bitcast(F32)
        bWin0, bWout0, bWin1, bWout1 = P.buf("Win0"), P.buf("Wout0"), P.buf("Win1"), P.buf("Wout1")
        bWin0u, bWin0z, bWin1z = P.buf("Win0u"), P.buf("Win0z"), P.buf("Win1z")
        bxcT, bxmT, bqT, bkT, bkw, bvtok = [P.buf(n) for n in ("xcT", "xmT", "qT", "kT", "kw", "vtok")]
        bCb0 = P.buf("Cb0"); bCb = [bCb0, bCb0]
        v_f32 = F32A[:, 0:2048]
        biasT = F32A[:, 2048:4096].rearrange("p (a c) -> p a c", a=16)
        biasTs = F32A[:, 4096:5120].rearrange("p (a c) -> p a c", a=16)
        tmpU = F32A[:, 5120:5632]
        RSb = F32A[:, 6656:7680].rearrange("p (a c) -> p a c", a=8)
        Cst = F32A[:, 0:8192].rearrange("p (a c) -> p a c", a=16)
        bv_f32, bbiasT, bbiasTs, btmpU = P.buf("v_f32"), P.buf("biasT"), P.buf("biasTs"), P.buf("tmpU")
        bC = [P.buf(f"C{h}") for h in range(16)]

        Wbd_t = sb("Wbd", [128, 48 * 128], BF16); bWbd = P.buf("Wbd")
        Wbd = Wbd_t[:, :].rearrange("p (a c) -> p a c", a=48)
        vhat = Wbd_t[:, 0:E]; bvhat = P.buf("vhat")
        xt0 = sb("xt0", [128, DM]); bxt0 = P.buf("xt0")
        xsf = sb("xsf", [128, DM]); bxsf = P.buf("xsf")
        xt = [xt0, xsf]; bxt = [bxt0, bxsf]
        xnT = sb("xnT", [128, 8, 128], BF16); bxnT = P.buf("xnT")
        outT_1 = F32A[:, 5632:6656].bitcast(BF16).rearrange("p (a c) -> p a c", a=16)
        outT_2 = kw.rearrange("p (a c) -> p a c", a=16)
        boutT = bkw
        XH = sb("XH", [128, 16 * 132]); bxpad = P.buf("xpad"); bhn = bxpad
        xpad = XH[:, :].rearrange("p (a c) -> p a c", a=16)
        hn = XH[:, 0:E]
        halo = sb("halo", [128, 16, 12]); bhalo = P.buf("halo")
        halo0 = sb("halo0", [128, 16, 3]); bhalo0 = P.buf("halo0")
        xcf_t = sb("xcf", [128, 16 * 128]); bxcf = P.buf("xcf")
        bxcfc = [P.buf(f"xcf{i}") for i in range(16)]
        ALLX = [bxcf] + bxcfc
        dummy = sb("fence_t", [128, 2]); bdummy = P.buf("dummy")
        xcf = xcf_t[:, :].rearrange("p (a c) -> p a c", a=16)
        crow = xcf_t; bcrow = bxcf
        tmpZ_1 = Wbd_t[:, 2048:3072].bitcast(F32)
        tmpS_1 = Wbd_t[:, 3072:4096].bitcast(F32)
        btmpZ = P.buf("tmpZ"); btmpS = P.buf("tmpS")
        OV2 = sb("OV2", [128, DM]); bOV2 = P.buf("OV2")
        FGb = OV2; bFGb = bOV2
        identf = sb("identf", [128, 128]); bident = P.buf("ident")
        maskT = sb("maskT", [128, 128]); bmaskT = P.buf("maskT")
        onesf = sb("onesf", [128, 128]); bones = P.buf("ones")
        onesb = sb("onesb", [128, 4], BF16); bonesb = P.buf("onesb")
        WmT = OV2[:, 0:512].bitcast(BF16).rearrange("p (a c) -> p a c", a=8); bWmT = P.buf("WmT")
        WmTs = OV2[:, 512:768].bitcast(BF16).rearrange("p (a c) -> p a c", a=8); bWmTs = P.buf("WmTs")
        wtmp = OV2[:, 768:896]; bwtmp = P.buf("wtmp")
        wtmp2 = OV2[:, 896:1024]; bwtmp2 = P.buf("wtmp2")
        BSb = XH[:, 0:1024].rearrange("p (a c) -> p a c", a=8); bBSb = bxpad
        BSs = XH[:, 1024:1536].rearrange("p (a c) -> p a c", a=8); bBSs = bxpad
        cvec = sb("cvec", [128, 160]); bcvec = P.buf("cvec")
        flg = sb("flg", [128, 8]); bflg = P.buf("flg")
        Wg = sb("Wg", [128, 32, 8], BF16); bWg = P.buf("Wg")
        wgf = xcf_t[:, 0:384]; bwgf = bxcf
        bgb = sb("bgb", [128, 8]); bbgb = P.buf("bgb")
        sm = sb("sm", [128, 96]); bsm = P.buf("sm")
        nst = sb("nst", [128, 16]); bnst = P.buf("nst")
        mst = sb("mst", [128, 4]); bmst = P.buf("mst")
        Bacc = sb("Bacc", [128, 4]); bBacc = P.buf("Bacc")
        nbb = sb("nbb", [128, 4], BF16); bnbb = P.buf("nbb")
        Sw = sb("Sw", [128, 128], BF16); bSw = P.buf("Sw")
        diagA = tmpZ_2; bdiagA = btmpZ
        misc = tmpS_2; bmisc = btmpS
        pmisc = sb("pmisc", [128, 64]); bpmisc = P.buf("pmisc")
        stats = sb("stats", [128, 4, 6]); bstats = P.buf("stats")
        mv = sb("mv", [128, 4, 2]); bmv = P.buf("mv")

        for b_ in (bsm, bpmisc, bstats, bmv, bnbb, bmst, bnst, bBacc):
            P.late_bufs.add(id(b_))
        ps = [st.enter_context(nc.psum_tensor(f"ps{i}", [128, 512], F32)) for i in range(8)]
        bps = [P.buf(f"ps{i}") for i in range(8)]
        pctr = [0]

        reserved = set()

        def bank(reserve=False):
            while True:
                i = pctr[0] % 8
                pctr[0] += 1
                if i not in reserved:
                    break
            if reserve:
                reserved.add(i)
            return ps[i], bps[i]

        def release(pb):
            for i in range(8):
                if ps[i] is pb:
                    reserved.discard(i)

        A = P.add
        dq = [0]

        def dmaq():
            dq[0] += 1
            return "sp"

        A("pool", lambda h: h.memset(identf[:], 1.0), wr=[bident])
        A("pool", lambda h: h.affine_select(out=identf[:], in_=identf[:], pattern=[[-1, 128]],
                                            compare_op=ALU.is_equal, fill=0.0, base=0, channel_multiplier=1),
          rd=[bident], wr=[bident])
        A("pool", lambda h: h.memset(maskT[:], 1.0), wr=[bmaskT])
        A("pool", lambda h: h.affine_select(out=maskT[:], in_=maskT[:], pattern=[[1, 128]],
                                            compare_op=ALU.is_ge, fill=0.0, base=0, channel_multiplier=-1),
          rd=[bmaskT], wr=[bmaskT])
        A("pool", lambda h: h.memset(onesf[:], 1.0), wr=[bones])
        A("pool", lambda h: h.memset(onesb[:], 1.0), wr=[bonesb])
        def ld_cvec(h):
            r = []
            for i, (src, n) in enumerate([(ngc, 16), (a_lng, 16), (a_lnb, 16), (b_cb, 16), (b_hg, 16), (b_skip, 16)]):
                r.append(h.dma_start(out=cvec[:, i * 16:(i + 1) * 16], in_=src[:, :]))
            r.append(h.dma_start(out=cvec[:, 96:160], in_=b_cw[:, :]))
            r.append(h.dma_start(out=flg[:], in_=flags.partition_broadcast(128)))
            r.append(h.dma_start(out=BSb[:].rearrange("p a c -> p (a c)"), in_=a_bs.partition_broadcast(128)))
            r.append(h.dma_start(out=wgf[:], in_=b_wg[:, :]))
            return r
        A("sp", ld_cvec, wr=[bcvec, bflg, bBSb, bwgf], dma=10, key="cvec")
        A("sp", lambda h: h.dma_start(out=bgb[:], in_=b_bg.partition_broadcast(128)), wr=[bbgb], dma=1, key="bgb")
        NG0, NG1, LNG, LNB, CB, HG, SKIP, CW = 0, 8, 16, 32, 48, 64, 80, 96

        def ld_w(dst, src, nchunk, cols=None):
            a, ncol = dst.shape[1], dst.shape[2]
            lo, hi = cols if cols is not None else (0, ncol)
            pieces = [(i, c0) for i in range(a) for c0 in range(lo, hi, 2048)]

            def f(h):
                r = []
                sv = src.rearrange("(a p) c -> p a c", p=128)
                for i, c0 in pieces:
                    c1 = min(hi, c0 + 2048)
                    r.append(h.dma_start(out=dst[:, i, c0:c1], in_=sv[:, i, c0:c1]))
                return r
            f.n = len(pieces)
            return f
        f_ = ld_w(Win0, a_w_in, 8, (E, 2 * E)); A("pool", f_, wr=[bWin0], dma=f_.n, key="Win0v")
        f_ = ld_w(Win0, a_w_in, 8, (0, E)); A("pool", f_, wr=[bWin0u], dma=f_.n, key="Win0u")
        f_ = ld_w(Win0, a_w_in, 8, (2 * E, 3 * E)); A("pool", f_, wr=[bWin0z], dma=f_.n, key="Win0z")
        f_ = ld_w(Wout0, a_w_out, 4); A("pool", f_, wr=[bWout0], dma=f_.n, key="Wout0")

        def sgu_consts(Tt, Wdst, bWdst, BSsrc, bBSsrc, bdst, bbdst, sample):
            for g in range(8):
                if not sample:
                    A("sp", lambda h, g=g: h.dma_start(out=wtmp[:], in_=a_ws[g * 128:(g + 1) * 128, :]),
                      wr=[bwtmp], dma=1, key="wtmp")
                else:
                    A("pool", lambda h: h.memset(wtmp[:], 0.0), wr=[bwtmp])
                    A("sp", lambda h, g=g: [h.dma_start(out=wtmp[16 * j:16 * j + 16, 16 * j:16 * j + 16],
                                                        in_=a_ws[g * 128:g * 128 + 16, 0:16]) for j in range(4)],
                      wr=[bwtmp], dma=4, key="wtmp")
                pb, bpb = bank()
                A("pe", lambda h, pb=pb: h.matmul(pb[:Tt, :Tt], lhsT=wtmp[:Tt, :Tt], rhs=identf[:Tt, :Tt],
                                                  start=True, stop=True), rd=[bwtmp, bident], wr=[bpb])
                A("dve", lambda h, pb=pb: h.tensor_tensor(out=wtmp2[:Tt, :Tt], in0=pb[:Tt, :Tt], in1=maskT[:Tt, :Tt],
                                                          op=ALU.mult), rd=[bpb, bmaskT], wr=[bwtmp2])
                A("pool", lambda h, g=g: h.tensor_copy(out=Wdst[:Tt, g, :Tt], in_=wtmp2[:Tt, :Tt]),
                  rd=[bwtmp2], wr=[bWdst])
                pb2, bpb2 = bank()
                A("pe", lambda h, pb2=pb2: h.matmul(pb2[:, :Tt], lhsT=onesf[:Tt, :], rhs=wtmp2[:Tt, :Tt],
                                                    start=True, stop=True), rd=[bwtmp2, bones], wr=[bpb2])
                for i in range(2):
                    ct = 2 * g + i
                    A("dve", lambda h, pb2=pb2, ct=ct, g=g: h.scalar_tensor_tensor(
                        out=bdst[:, ct, :Tt], in0=pb2[:, :Tt], scalar=cvec[:, LNB + ct:LNB + ct + 1],
                        in1=BSsrc[:, g, :Tt], op0=ALU.mult, op1=ALU.add), rd=[bpb2, bcvec, bBSsrc], wr=[bbdst])

        A("pool", lambda h: h.tensor_copy(out=BSs[:].rearrange("p a (j t) -> p a j t", j=4),
                                          in_=BSb[:, :, 0:16].unsqueeze(2).to_broadcast([128, 8, 4, 16])),
          rd=[bBSb], wr=[bBSs])
        sgu_consts(128, WmT, bWmT, BSb, bBSb, biasT, bbiasT, False)
        sgu_consts(64, WmTs, bWmTs, BSs, bBSs, biasTs, bbiasTs, True)

        for ct in range(16):
            pb, bpb = bank()
            for qi in range(3):
                A("sp", lambda h, qi=qi, ct=ct: h.dma_start(
                    out=wtmp[:], in_=wbdT[:, (qi * 16 + ct) * 128:(qi * 16 + ct + 1) * 128]),
                  wr=[bwtmp], dma=1, key="wtmp")
                col = 0 if qi < 2 else 8
                A("pe", lambda h, pb=pb, qi=qi, ct=ct, col=col: h.matmul(
                    pb[:, col:col + 8], lhsT=wtmp[:], rhs=wgf[:, (qi * 16 + ct) * 8:(qi * 16 + ct + 1) * 8],
                    start=(qi != 1), stop=(qi != 0)), rd=[bwtmp, bwgf], wr=[bpb])
            A("dve", lambda h, pb=pb, ct=ct: h.tensor_copy(out=Wg[:, ct, :], in_=pb[:, 0:8]), rd=[bpb], wr=[bWg])
            A("dve", lambda h, pb=pb, ct=ct: h.tensor_copy(out=Wg[:, 16 + ct, :], in_=pb[:, 8:16]), rd=[bpb], wr=[bWg])

        def rms_pre_a(src_ap, Tt, par, rd, J, bJ):
            X, bX = xt[par], bxt[par]
            A("sp", lambda h: h.dma_start(out=X[:Tt, :], in_=src_ap), rd=list(rd), wr=[bX], dma=1, key=f"xt{par}")
            A("act", lambda h: h.activation(out=J[:Tt, :], in_=X[:Tt, :], func=AF.Square, accum_out=sm[:Tt, 0:1]),
              rd=[bX], wr=list(bJ) + [bsm])
            A("act", lambda h: h.activation(out=sm[:Tt, 1:2], in_=sm[:Tt, 0:1], func=AF.Sqrt, scale=1.0 / DM, bias=cvec_eps[:Tt, 0:1]),
              rd=[bsm, bceps], wr=[bsm])
            A("dve", lambda h: h.reciprocal(out=sm[:Tt, 2:3], in_=sm[:Tt, 1:2]), rd=[bsm], wr=[bsm])

        def rms_pre_b(Tt, par, S, bS):
            X, bX = xt[par], bxt[par]
            A("act", lambda h: h.activation(out=S[:Tt, :], in_=X[:Tt, :], func=AF.Copy, scale=sm[:Tt, 2:3]),
              rd=[bX, bsm], wr=[bS])

        def rms_pre(src_ap, Tt, par, rd, S, bS):
            rms_pre_a(src_ap, Tt, par, rd, S, [bS])
            rms_pre_b(Tt, par, S, bS)

        def rms_post(Tt, gcol, S, bS):
            for half in range(2):
                pb, bpb = bank()
                for i in range(4):
                    dt_ = half * 4 + i
                    A("pe", lambda h, pb=pb, i=i, dt_=dt_: h.matmul(
                        pb[:, i * 128:i * 128 + Tt], lhsT=S[:Tt, dt_ * 128:(dt_ + 1) * 128], rhs=identf[:Tt, :Tt],
                        start=True, stop=True), rd=[bS, bident], wr=[bpb])
                for i in range(4):
                    dt_ = half * 4 + i
                    A("act", lambda h, pb=pb, i=i, dt_=dt_: h.activation(
                        out=xnT[:, dt_, :Tt], in_=pb[:, i * 128:i * 128 + Tt], func=AF.Copy,
                        scale=cvec[:, gcol + dt_:gcol + dt_ + 1]), rd=[bpb, bcvec], wr=[bxnT])

        def rms_front(src_ap, Tt, par, gcol, rd=()):
            rms_pre(src_ap, Tt, par, rd, xt[1 - par], bxt[1 - par])
            rms_post(Tt, gcol, xt[1 - par], bxt[1 - par])

        cvec_eps = sb("cvec_eps", [128, 4]); bceps = P.buf("ceps")
        A("pool", lambda h: h.memset(cvec_eps[:, 0:1], RMS_EPS), wr=[bceps])
        A("pool", lambda h: h.memset(cvec_eps[:, 1:2], LN_EPS), wr=[bceps])
        A("pool", lambda h: h.memset(cvec_eps[:, 2:3], 1.0), wr=[bceps])

        def proj_cm(W, bW, col0, cg, Tt):
            pb, bpb = bank()
            for i in range(4):
                c0 = col0 + (cg * 4 + i) * 128
                for dt_ in range(8):
                    A("pe", lambda h, pb=pb, i=i, c0=c0, dt_=dt_: h.matmul(
                        pb[:, i * Tt:(i + 1) * Tt], lhsT=W[:, dt_, c0:c0 + 128], rhs=xnT[:, dt_, :Tt],
                        start=(dt_ == 0), stop=(dt_ == 7)), rd=[bW, bxnT], wr=[bpb])
            return pb, bpb

        def out_proj(W, bW, Tt, par, outT):
            X, bX = xt[par], bxt[par]
            for half in range(2):
                pb, bpb = bank()
                for ct in range(16):
                    A("pe", lambda h, pb=pb, ct=ct, half=half: h.matmul(
                        pb[:Tt, :], lhsT=outT[:, ct, :Tt], rhs=W[:, ct, half * 512:(half + 1) * 512],
                        start=(ct == 0), stop=(ct == 15)), rd=[boutT, bW], wr=[bpb])
                A("dve", lambda h, pb=pb, half=half: h.tensor_tensor(
                    out=X[:Tt, half * 512:(half + 1) * 512], in0=pb[:Tt, :], in1=X[:Tt, half * 512:(half + 1) * 512],
                    op=ALU.add), rd=[bpb, bX], wr=[bX])

        def layer0_tile(src_ap, dst_row, Tt, par, sample, pre=False, nxt=None):
            X, bX = xt[par], bxt[par]
            outT, tmpZ, tmpS = outT_1, tmpZ_1, tmpS_1
            S0 = v_f32[:, 0:1024]
            if not pre:
                rms_pre(src_ap, Tt, par, (), S0, bv_f32)
            rms_post(Tt, NG0, S0, bv_f32)
            Wm_, bWm_ = (WmTs, bWmTs) if sample else (WmT, bWmT)
            bt_, bbt_ = (biasTs, bbiasTs) if sample else (biasT, bbiasT)
            for cb in range(4):
                pb, bpb = bank()
                for dt_ in range(8):
                    A("pe", lambda h, pb=pb, cb=cb, dt_=dt_: h.matmul(
                        pb[:Tt, :], lhsT=xnT[:, dt_, :Tt], rhs=Win0[:, dt_, E + cb * 512:E + (cb + 1) * 512],
                        start=(dt_ == 0), stop=(dt_ == 7)), rd=[bxnT, bWin0], wr=[bpb])
                A("act", lambda h, pb=pb, cb=cb: h.activation(out=v_f32[:Tt, cb * 512:(cb + 1) * 512], in_=pb[:Tt, :],
                                                              func=AF.Gelu_apprx_tanh), rd=[bpb], wr=[bv_f32])
            for cb in range(4):
                A("dve", lambda h, cb=cb: h.bn_stats(out=stats[:Tt, cb, :], in_=v_f32[:Tt, cb * 512:(cb + 1) * 512]),
                  rd=[bv_f32], wr=[bstats])
            A("dve", lambda h: h.bn_aggr(out=sm[:Tt, 82:84], in_=stats[:Tt, :, :].rearrange("p a c -> p (a c)")), rd=[bstats], wr=[bsm])
            A("dve", lambda h: h.tensor_copy(out=sm[:Tt, 84:85], in_=sm[:Tt, 83:84]), rd=[bsm], wr=[bsm])
            A("act", lambda h: h.activation(out=sm[:Tt, 4:5], in_=sm[:Tt, 84:85], func=AF.Sqrt, bias=cvec_eps[:Tt, 1:2]),
              rd=[bsm, bceps], wr=[bsm])
            A("dve", lambda h: h.reciprocal(out=sm[:Tt, 5:6], in_=sm[:Tt, 4:5]), rd=[bsm], wr=[bsm])
            A("dve", lambda h: h.tensor_scalar(out=vhat[:Tt, :], in0=v_f32[:Tt, :], scalar1=sm[:Tt, 82:83],
                                               scalar2=sm[:Tt, 5:6], op0=ALU.subtract, op1=ALU.mult),
              rd=[bv_f32, bsm], wr=[bvhat])
            if sample:
                A("dve", lambda h: h.tensor_scalar(out=v_f32[:Tt, :], in0=v_f32[:Tt, :], scalar1=sm[:Tt, 82:83],
                                                   scalar2=sm[:Tt, 5:6], op0=ALU.subtract, op1=ALU.mult),
                  rd=[bsm], wr=[bv_f32])
                A("sp", lambda h: h.dma_start(out=hn[:Tt, :], in_=a_lng_row.partition_broadcast(Tt)), wr=[bhn], dma=1, key="lnrow")
                A("dve", lambda h: h.tensor_tensor(out=v_f32[:Tt, :], in0=v_f32[:Tt, :], in1=hn[:Tt, :], op=ALU.mult),
                  rd=[bhn, bv_f32], wr=[bv_f32])
                A("sp", lambda h: h.dma_start(out=hn[:Tt, :], in_=a_lnb_row.partition_broadcast(Tt)), wr=[bhn], dma=1, key="lnrow")
                A("dve", lambda h: h.tensor_tensor(out=v_f32[:Tt, :], in0=v_f32[:Tt, :], in1=hn[:Tt, :], op=ALU.add),
                  rd=[bhn, bv_f32], wr=[bv_f32])
                A("sp", lambda h: h.dma_start(out=sguv[:, :], in_=v_f32[:Tt, :]), rd=[bv_f32], dma=1, key="sguv")
            if nxt is not None:
                rms_pre(nxt[0], nxt[1], 1 - par, (), S0, bv_f32)
            for cg in range(4):
                pu, bpu = proj_cm(Win0, bWin0u, 0, cg, Tt)
                pz, bpz = proj_cm(Win0, bWin0z, 2 * E, cg, Tt)
                pS, bpS = bank()
                for i in range(4):
                    ct = cg * 4 + i
                    A("pe", lambda h, pS=pS, i=i, ct=ct: h.matmul(
                        pS[:, i * Tt:(i + 1) * Tt], lhsT=vhat[:Tt, ct * 128:(ct + 1) * 128], rhs=Wm_[:Tt, ct // 2, :Tt],
                        start=True, stop=True), rd=[bvhat, bWm_], wr=[bpS])
                n4 = 4 * Tt
                A("act", lambda h, pu=pu: h.activation(out=tmpU[:, :n4], in_=pu[:, :n4], func=AF.Gelu_apprx_tanh),
                  rd=[bpu], wr=[btmpU])
                A("act", lambda h, pz=pz: h.activation(out=tmpZ[:, :n4], in_=pz[:, :n4], func=AF.Silu),
                  rd=[bpz], wr=[btmpZ])
                for i in range(4):
                    ct = cg * 4 + i
                    A("dve", lambda h, pS=pS, i=i, ct=ct: h.scalar_tensor_tensor(
                        out=tmpS[:, i * Tt:(i + 1) * Tt], in0=pS[:, i * Tt:(i + 1) * Tt],
                        scalar=cvec[:, LNG + ct:LNG + ct + 1], in1=bt_[:, ct, :Tt], op0=ALU.mult, op1=ALU.add),
                      rd=[bpS, bcvec, bbt_], wr=[btmpS])
                A("dve", lambda h: h.tensor_tensor(out=tmpS[:, :n4], in0=tmpS[:, :n4], in1=tmpU[:, :n4], op=ALU.mult),
                  rd=[btmpU, btmpS], wr=[btmpS])
                A("dve", lambda h, cg=cg: h.tensor_tensor(
                    out=outT[:, cg * 4:(cg + 1) * 4, :Tt], in0=tmpS[:, :n4].rearrange("p (a t) -> p a t", a=4),
                    in1=tmpZ[:, :n4].rearrange("p (a t) -> p a t", a=4), op=ALU.mult),
                  rd=[btmpS, btmpZ], wr=[boutT])
            out_proj(Wout0, bWout0, Tt, par, outT)
            A("sp", lambda h: h.dma_start(out=x1s[dst_row:dst_row + Tt, :], in_=X[:Tt, :]), rd=[bX], wr=[bx1s],
              dma=1, key=f"x1st{par}")

        bx1s = P.buf("x1s")

        def l1_front(src_row, Tt, par, nstr, L, need_q, halo_mode, conv_out=None, pre=None, nxt=None, halo_only=False):
            if pre is None:
                rms_front(x1s[src_row:src_row + Tt, :], Tt, par, NG1, rd=[bx1s])
            elif pre == "B0":
                rms_pre(x1s[src_row:src_row + Tt, :], Tt, par, [bx1s], hn[:, 0:1024], bxpad)
                rms_post(Tt, NG1, hn[:, 0:1024], bxpad)
            elif pre == "B1":
                rms_pre_b(Tt, par, hn[:, 0:1024], bxpad)
                rms_post(Tt, NG1, hn[:, 0:1024], bxpad)
            else:
                if not pre:
                    rms_pre(x1s[src_row:src_row + Tt, :], Tt, par, [bx1s], hn[:, 0:1024], bxpad)
                rms_post(Tt, NG1, hn[:, 0:1024], bxpad)
            A("dve", lambda h: h.memset(dummy[:, 0:1], 0.0), wr=ALLX + [bdummy])
            xp4 = xpad[:, :, 0:nstr * (L + 3)].rearrange("p a (j t) -> p a j t", j=nstr)
            if halo_mode == "zero":
                A("pool", lambda h: h.memset(xp4[:, :, :, 0:3], 0.0), wr=[bxpad])
            elif halo_mode == "copy":
                A("act", lambda h: h.activation(out=xp4[:, :, :, 0:3], in_=halo[:, :, 0:3].unsqueeze(2), func=AF.Copy), rd=[bhalo], wr=[bxpad])
            elif halo_mode == "flag":
                A("act", lambda h: h.activation(out=xp4[:, :, :, 0:3], in_=halo[:, :, 0:3].unsqueeze(2), func=AF.Copy,
                                                scale=flg[:, 0:1]), rd=[bhalo, bflg], wr=[bxpad])
            elif halo_mode == "sample":
                A("sp", lambda h: h.dma_start(out=crow[:12, :], in_=stconv[:, :]), wr=[bcrow], dma=1, key="crow")
                pb, bpb = bank()
                for ct in range(16):
                    A("pe", lambda h, pb=pb, ct=ct: h.matmul(pb[:, ct * 12:(ct + 1) * 12], lhsT=crow[:12, ct * 128:(ct + 1) * 128],
                                                             rhs=identf[:12, :12], start=True, stop=True),
                      rd=[bcrow, bident], wr=[bpb])
                A("act", lambda h, pb=pb: h.activation(out=xp4[:, :, :, 0:3],
                                                       in_=pb[:, 0:192].rearrange("p (a j t) -> p a j t", a=16, j=4),
                                                       func=AF.Copy), rd=[bpb], wr=[bxpad])
            for cg in range(4):
                pb, bpb = proj_cm(Win1, bWin1, 0, cg, Tt)
                A("act", lambda h, pb=pb, cg=cg: h.activation(
                    out=xp4[:, cg * 4:(cg + 1) * 4, :, 3:3 + L],
                    in_=pb[:, :4 * Tt].rearrange("p (a j t) -> p a j t", a=4, j=nstr), func=AF.Copy), rd=[bpb], wr=[bxpad])
                A("act", lambda h, cg=cg: h.activation(
                    out=xmT[:, cg * 4:(cg + 1) * 4, :Tt].rearrange("p a (j t) -> p a j t", j=nstr),
                    in_=xp4[:, cg * 4:(cg + 1) * 4, :, 3:3 + L], func=AF.Copy), rd=[bxpad], wr=[bxmT])
            A("act", lambda h: h.activation(out=halo[:, :, 0:3 * nstr].rearrange("p a (j t) -> p a j t", j=nstr),
                                            in_=xp4[:, :, :, L:L + 3], func=AF.Copy), rd=[bxpad], wr=[bhalo])
            if conv_out is not None:
                conv_tail_out(nstr, 3 * nstr, conv_out[0], conv_out[1])
            if halo_only:
                A("act", lambda h: h.activation(out=halo0[:, :, :], in_=halo[:, :, 0:3], func=AF.Copy), rd=[bhalo], wr=[bhalo0])
                if nxt is not None:
                    rms_pre(x1s[nxt:nxt + 128, :], 128, 1 - par, [bx1s], hn[:, 0:1024], bxpad)
                return
            xc4 = xcf[:, :, :Tt].rearrange("p a (j t) -> p a j t", j=nstr)
            A("dve", lambda h: h.memset(dummy[:, 1:2], 0.0), wr=ALLX + [bdummy])
            for ct in range(16):
                A("act", lambda h, ct=ct: h.activation(
                    out=xc4[:, ct], in_=xp4[:, ct, :, 0:L], func=AF.Identity, scale=cvec[:, CW + ct * 4:CW + ct * 4 + 1],
                    bias=cvec[:, CB + ct:CB + ct + 1]), rd=[bxpad, bcvec], wr=[bxcfc[ct]])
            for ct_, j in [(c_, j_) for hf in range(2) for j_ in range(1, 4) for c_ in range(hf * 8, hf * 8 + 8)]:
                for ct in (ct_,):
                    A("dve", lambda h, ct=ct, j=j: h.scalar_tensor_tensor(
                        out=xc4[:, ct], in0=xp4[:, ct, :, j:j + L], scalar=cvec[:, CW + ct * 4 + j:CW + ct * 4 + j + 1],
                        in1=xc4[:, ct], op0=ALU.mult, op1=ALU.add), rd=[bxpad, bcvec, bxcfc[ct]], wr=[bxcfc[ct]])
            for cg in range(4):
                A("act", lambda h, cg=cg: h.activation(out=xcf[:, cg * 4:(cg + 1) * 4, :Tt], in_=xcf[:, cg * 4:(cg + 1) * 4, :Tt],
                                                       func=AF.Silu), rd=bxcfc[cg * 4:(cg + 1) * 4], wr=bxcfc[cg * 4:(cg + 1) * 4])
            for cg in range(4):
                A("act", lambda h, cg=cg: h.activation(out=xcT[:, cg * 4:(cg + 1) * 4, :Tt], in_=xcf[:, cg * 4:(cg + 1) * 4, :Tt],
                                                       func=AF.Copy), rd=bxcfc[cg * 4:(cg + 1) * 4], wr=[bxcT])
            if nxt is not None and pre in ("B0", "B1"):
                rms_pre_a(x1s[nxt:nxt + 128, :], 128, 1 - par, [bx1s], JUNK, [btmpZ, btmpS])
            elif nxt is not None:
                rms_pre(x1s[nxt:nxt + 128, :], 128, 1 - par, [bx1s], hn[:, 0:1024], bxpad)
            if need_q:
                for qi, (dst, bdst, scl) in enumerate([(qT, bqT, 1.0), (kT, bkT, KSCALE)]):
                    for cg in range(4):
                        pb, bpb = bank()
                        for i in range(4):
                            ct = cg * 4 + i
                            A("pe", lambda h, pb=pb, i=i, ct=ct, qi=qi: h.matmul(
                                pb[:, i * Tt:(i + 1) * Tt], lhsT=Wbd[:, qi * 16 + ct, :], rhs=xcT[:, ct, :Tt],
                                start=True, stop=True), rd=[bWbd, bxcT], wr=[bpb])
                        A("act", lambda h, pb=pb, cg=cg, dst=dst, scl=scl: h.activation(
                            out=dst[:, cg * 4:(cg + 1) * 4, :Tt], in_=pb[:, :4 * Tt].rearrange("p (a t) -> p a t", a=4),
                            func=AF.Copy, scale=scl), rd=[bpb], wr=[bdst])
                A("dve", lambda h: h.tensor_tensor(out=xcf[:, :, :Tt], in0=xcf[:, :, :Tt],
                                                    in1=cvec[:, SKIP:SKIP + 16].unsqueeze(2).to_broadcast([128, 16, Tt]),
                                                    op=ALU.mult), rd=bxcfc + [bcvec], wr=bxcfc)

        cbpar = [0]

        def scan_chunk(off, L, full, Cview, nview, bCl, bnl, mview, bml, acc_B=False):
            mview = mview[:, :]
            pg, bpg = bank()
            for ct in range(16):
                A("pe", lambda h, pg=pg, ct=ct: h.matmul(pg[:L, 0:8], lhsT=xcT[:, ct, off:off + L], rhs=Wg[:, ct, :],
                                                         start=(ct == 0), stop=False), rd=[bxcT, bWg], wr=[bpg])
            for ct in range(16):
                A("pe", lambda h, pg=pg, ct=ct: h.matmul(pg[:L, 0:8], lhsT=xmT[:, ct, off:off + L], rhs=Wg[:, 16 + ct, :],
                                                         start=False, stop=(ct == 15)), rd=[bxmT, bWg], wr=[bpg])
            A("dve", lambda h, pg=pg: h.tensor_tensor(out=sm[:L, 8:16], in0=pg[:L, 0:8], in1=bgb[:L, :], op=ALU.add),
              rd=[bpg, bbgb], wr=[bsm])
            A("act", lambda h: h.activation(out=sm[:L, 16:20], in_=sm[:L, 12:16], func=AF.Abs), rd=[bsm], wr=[bsm])
            A("act", lambda h: h.activation(out=sm[:L, 20:24], in_=sm[:L, 16:20], func=AF.Exp, scale=-1.0), rd=[bsm], wr=[bsm])
            A("act", lambda h: h.activation(out=sm[:L, 24:28], in_=sm[:L, 20:24], func=AF.Ln, bias=cvec_eps[:L, 2:3]),
              rd=[bsm, bceps], wr=[bsm])
            A("dve", lambda h: h.tensor_single_scalar(out=sm[:L, 28:32], in_=sm[:L, 12:16], scalar=0.0, op=ALU.min),
              rd=[bsm], wr=[bsm])
            A("dve", lambda h: h.tensor_tensor(out=sm[:L, 28:32], in0=sm[:L, 28:32], in1=sm[:L, 24:28], op=ALU.subtract),
              rd=[bsm], wr=[bsm])
            pc, bpc = bank()
            A("pe", lambda h, pc=pc: h.matmul(pc[:L, 0:4], lhsT=maskT[:L, :L], rhs=sm[:L, 28:32], start=True, stop=True),
              rd=[bmaskT, bsm], wr=[bpc])
            A("pe", lambda h, pc=pc: h.matmul(pc[:, 8:12], lhsT=onesf[:L, :], rhs=sm[:L, 28:32], start=True, stop=True),
              rd=[bones, bsm], wr=[bpc])
            A("dve", lambda h, pc=pc: h.tensor_copy(out=sm[:L, 72:76], in_=pc[:L, 0:4]), rd=[bpc], wr=[bsm])
            A("dve", lambda h: h.tensor_tensor(out=sm[:L, 32:36], in0=sm[:L, 8:12], in1=sm[:L, 72:76], op=ALU.subtract),
              rd=[bsm], wr=[bsm])
            A("dve", lambda h: h.tensor_tensor(
                out=diagA[:L, :4 * L].rearrange("p (a t) -> p a t", a=4),
                in0=identf[:L, :L].unsqueeze(1).to_broadcast([L, 4, L]),
                in1=sm[:L, 32:36].unsqueeze(2).to_broadcast([L, 4, L]), op=ALU.mult), rd=[bident, bsm], wr=[bdiagA])
            pa, bpa = bank()
            A("pe", lambda h, pa=pa: h.matmul(pa[:, :4 * L], lhsT=onesf[:L, :], rhs=diagA[:L, :4 * L], start=True, stop=True),
              rd=[bones, bdiagA], wr=[bpa])
            A("dve", lambda h, pa=pa: h.tensor_reduce(out=pmisc[:, 0:4], in_=pa[:, :4 * L].rearrange("p (a t) -> p a t", a=4),
                                                      axis=AX.X, op=ALU.max), rd=[bpa], wr=[bpmisc])
            A("dve", lambda h: h.tensor_tensor(out=pmisc[:, 4:8], in0=pmisc[:, 0:4], in1=mview, op=ALU.max),
              rd=[bpmisc, bml], wr=[bpmisc])
            A("dve", lambda h: h.tensor_tensor(out=pmisc[:, 12:16], in0=mview, in1=pmisc[:, 4:8], op=ALU.subtract),
              rd=[bpmisc, bml], wr=[bpmisc])
            A("act", lambda h: h.activation(out=pmisc[:, 8:12], in_=pmisc[:, 12:16], func=AF.Exp), rd=[bpmisc], wr=[bpmisc])
            A("dve", lambda h, pc=pc: h.tensor_copy(out=pmisc[:, 16:20], in_=pc[:, 8:12]), rd=[bpc], wr=[bpmisc])
            A("dve", lambda h: h.tensor_tensor(out=mview, in0=pmisc[:, 16:20], in1=pmisc[:, 4:8], op=ALU.add),
              rd=[bpmisc], wr=[bml])
            if acc_B:
                A("dve", lambda h: h.tensor_tensor(out=Bacc[:, :], in0=Bacc[:, :], in1=pmisc[:, 16:20], op=ALU.add),
                  rd=[bpmisc, bBacc], wr=[bBacc])
            A("dve", lambda h: h.tensor_tensor(out=sm[:L, 64:68], in0=sm[:L, 32:36], in1=pmisc[:L, 4:8], op=ALU.subtract),
              rd=[bsm, bpmisc], wr=[bsm])
            A("act", lambda h: h.activation(out=sm[:L, 44:48], in_=sm[:L, 64:68], func=AF.Exp), rd=[bsm], wr=[bsm])
            A("dve", lambda h: h.tensor_single_scalar(out=sm[:L, 52:56], in_=sm[:L, 44:48], scalar=KSCALE, op=ALU.mult),
              rd=[bsm], wr=[bsm])
            if full:
                A("dve", lambda h: h.tensor_tensor(out=sm[:L, 64:68], in0=sm[:L, 72:76], in1=pmisc[:L, 4:8], op=ALU.add),
                  rd=[bsm, bpmisc], wr=[bsm])
                A("act", lambda h: h.activation(out=sm[:L, 48:52], in_=sm[:L, 64:68], func=AF.Exp, scale=-1.0), rd=[bsm], wr=[bsm])
            for hh in range(4):
                pk, bpk = bank()
                for i in range(4):
                    ct = hh * 4 + i
                    A("pe", lambda h, pk=pk, i=i, ct=ct: h.matmul(pk[:L, i * 128:(i + 1) * 128], lhsT=xcT[:, ct, off:off + L],
                                                                  rhs=Wbd[:, 16 + ct, :], start=True, stop=True),
                      rd=[bxcT, bWbd], wr=[bpk])
                A("act", lambda h, pk=pk, hh=hh: h.activation(out=kw[:L, hh * 512:(hh + 1) * 512], in_=pk[:L, :], func=AF.Copy,
                                                              scale=sm[:L, 52 + hh:53 + hh]), rd=[bpk, bsm], wr=[bkw])
                pv, bpv = bank()
                for i in range(4):
                    ct = hh * 4 + i
                    A("pe", lambda h, pv=pv, i=i, ct=ct: h.matmul(pv[:L, i * 128:(i + 1) * 128], lhsT=xmT[:, ct, off:off + L],
                                                                  rhs=Wbd[:, 32 + ct, :], start=True, stop=True),
                      rd=[bxmT, bWbd], wr=[bpv])
                A("dve", lambda h, pv=pv, hh=hh: h.tensor_copy(out=vtok[:L, hh * 512:(hh + 1) * 512], in_=pv[:L, :]),
                  rd=[bpv], wr=[bvtok])
            pden, bpden = (None, None)
            if full:
                pden, bpden = bank(reserve=True)
            pn, bpn = bank(reserve=True)
            for hh in range(4):
                c0col = pmisc[:, 8 + hh:9 + hh]
                if full:
                    par = cbpar[0] % 2
                    cbpar[0] += 1
                    CB_, bCB_ = Cb[par], bCb[par]
                    for dt_ in range(4):
                        A("act", lambda h, hh=hh, dt_=dt_, CB_=CB_, c0col=c0col: h.activation(
                            out=CB_[:, dt_, :], in_=Cview(hh, dt_), func=AF.Copy, scale=c0col),
                          rd=[bCl[hh * 4 + dt_], bpmisc], wr=[bCB_])
                    A("dve", lambda h, hh=hh, c0col=c0col: h.tensor_scalar(out=nbb[:, :], in0=nview[:, hh * 4:(hh + 1) * 4],
                                                                           scalar1=c0col, scalar2=None, op0=ALU.mult),
                      rd=[bnl, bpmisc], wr=[bnbb])
                    pst, bpst = bank()
                    for dt_ in range(4):
                        ct = hh * 4 + dt_
                        A("pe", lambda h, pst=pst, ct=ct, dt_=dt_: h.matmul(
                            pst[:L, :L], lhsT=kT[:, ct, off:off + L], rhs=qT[:, ct, off:off + L],
                            start=(dt_ == 0), stop=(dt_ == 3)), rd=[bkT, bqT], wr=[bpst])
                    A("dve", lambda h, pst=pst, hh=hh: h.scalar_tensor_tensor(
                        out=Sw[:L, :L], in0=pst[:L, :L], scalar=sm[:L, 44 + hh:45 + hh], in1=maskT[:L, :L],
                        op0=ALU.mult, op1=ALU.mult), rd=[bpst, bsm, bmaskT], wr=[bSw])
                    pnum, bpnum = bank()
                    A("pe", lambda h, pnum=pnum, hh=hh: h.matmul(pnum[:L, :], lhsT=Sw[:L, :L], rhs=vtok[:L, hh * 512:(hh + 1) * 512],
                                                                 start=True, stop=False), rd=[bSw, bvtok], wr=[bpnum])
                    for dt_ in range(4):
                        ct = hh * 4 + dt_
                        A("pe", lambda h, pnum=pnum, ct=ct, dt_=dt_, CB_=CB_: h.matmul(
                            pnum[:L, :], lhsT=qT[:, ct, off:off + L], rhs=CB_[:, dt_, :], start=False, stop=(dt_ == 3)),
                          rd=[bqT, bCB_], wr=[bpnum])
                    A("pe", lambda h, hh=hh: h.matmul(pden[:L, hh:hh + 1], lhsT=Sw[:L, :L], rhs=onesb[:L, 0:1],
                                                      start=True, stop=False), rd=[bSw, bonesb], wr=[bpden])
                    for dt_ in range(4):
                        ct = hh * 4 + dt_
                        A("pe", lambda h, hh=hh, ct=ct, dt_=dt_: h.matmul(
                            pden[:L, hh:hh + 1], lhsT=qT[:, ct, off:off + L], rhs=nbb[:, dt_:dt_ + 1],
                            start=False, stop=(dt_ == 3)), rd=[bqT, bnbb], wr=[bpden])
                    A("act", lambda h, pnum=pnum, hh=hh: h.activation(out=hn[:L, hh * 512:(hh + 1) * 512], in_=pnum[:L, :],
                                                                      func=AF.Copy), rd=[bpnum], wr=[bhn])
                for dt_ in range(4):
                    ct = hh * 4 + dt_
                    pu_, bpu_ = bank()
                    A("pe", lambda h, pu_=pu_, ct=ct, hh=hh: h.matmul(
                        pu_[:, :], lhsT=kw[:L, ct * 128:(ct + 1) * 128], rhs=vtok[:L, hh * 512:(hh + 1) * 512],
                        start=True, stop=True), rd=[bkw, bvtok], wr=[bpu_])
                    A("dve", lambda h, pu_=pu_, hh=hh, dt_=dt_, c0col=c0col: h.scalar_tensor_tensor(
                        out=Cview(hh, dt_), in0=Cview(hh, dt_), scalar=c0col, in1=pu_[:, :], op0=ALU.mult, op1=ALU.add),
                      rd=[bpu_, bpmisc, bCl[hh * 4 + dt_]], wr=[bCl[hh * 4 + dt_]])
                    A("pe", lambda h, ct=ct: h.matmul(pn[:, ct:ct + 1], lhsT=kw[:L, ct * 128:(ct + 1) * 128], rhs=onesb[:L, 0:1],
                                                      start=True, stop=True), rd=[bkw, bonesb], wr=[bpn])
                A("dve", lambda h, hh=hh, c0col=c0col: h.scalar_tensor_tensor(
                    out=nview[:, hh * 4:(hh + 1) * 4], in0=nview[:, hh * 4:(hh + 1) * 4], scalar=c0col,
                    in1=pn[:, hh * 4:(hh + 1) * 4], op0=ALU.mult, op1=ALU.add), rd=[bpn, bpmisc, bnl], wr=[bnl])
            release(pn)
            if full:
                release(pden)
                A("act", lambda h: h.activation(out=sm[:L, 56:60], in_=pden[:L, 0:4], func=AF.Abs), rd=[bpden], wr=[bsm])
                A("dve", lambda h: h.tensor_tensor(out=sm[:L, 56:60], in0=sm[:L, 56:60], in1=sm[:L, 48:52], op=ALU.max),
                  rd=[bsm], wr=[bsm])
                A("dve", lambda h: h.reciprocal(out=sm[:L, 60:64], in_=sm[:L, 56:60]), rd=[bsm], wr=[bsm])
                for hh in range(4):
                    A("dve", lambda h, hh=hh: h.bn_stats(out=stats[:L, hh, :], in_=hn[:L, hh * 512:(hh + 1) * 512]),
                      rd=[bhn], wr=[bstats])
                    A("dve", lambda h, hh=hh: h.bn_aggr(out=mv[:L, hh, :], in_=stats[:L, hh, :]), rd=[bstats], wr=[bmv])
                A("dve", lambda h: h.tensor_tensor(out=sm[:L, 64:68], in0=sm[:L, 60:64], in1=sm[:L, 60:64], op=ALU.mult),
                  rd=[bsm], wr=[bsm])
                A("dve", lambda h: h.tensor_tensor(out=sm[:L, 64:68], in0=sm[:L, 64:68], in1=mv[:L, :, 1], op=ALU.mult),
                  rd=[bsm, bmv], wr=[bsm])
                A("act", lambda h: h.activation(out=sm[:L, 64:68], in_=sm[:L, 64:68], func=AF.Sqrt, bias=cvec_eps[:L, 1:2]),
                  rd=[bsm, bceps], wr=[bsm])
                A("dve", lambda h: h.reciprocal(out=sm[:L, 68:72], in_=sm[:L, 64:68]), rd=[bsm], wr=[bsm])
                A("dve", lambda h: h.tensor_tensor(out=sm[:L, 68:72], in0=sm[:L, 68:72], in1=sm[:L, 60:64], op=ALU.mult),
                  rd=[bsm], wr=[bsm])
                for hh in range(4):
                    A("dve", lambda h, hh=hh: h.tensor_scalar(
                        out=hn[:L, hh * 512:(hh + 1) * 512], in0=hn[:L, hh * 512:(hh + 1) * 512], scalar1=mv[:L, hh, 0:1],
                        scalar2=sm[:L, 68 + hh:69 + hh], op0=ALU.subtract, op1=ALU.mult), rd=[bhn, bmv, bsm], wr=[bhn])

        def hn_transpose(banks, off, L, Tt):
            for cg in range(4):
                pb, bpb = banks[cg]
                for i in range(4):
                    ct = cg * 4 + i
                    A("pe", lambda h, pb=pb, i=i, ct=ct: h.matmul(pb[:, i * Tt + off:i * Tt + off + L],
                                                                  lhsT=hn[:L, ct * 128:(ct + 1) * 128], rhs=identf[:L, :L],
                                                                  start=True, stop=True), rd=[bhn, bident], wr=[bpb])

        def l1_tail(banks, Tt, par, out_ap, key):
            X, bX = xt[par], bxt[par]
            xsf, bxsf = xt[1 - par], bxt[1 - par]
            outT, tmpZ, tmpS = outT_2, tmpZ_2, tmpS_2
            n4 = 4 * Tt
            for cg in range(4):
                pz, bpz = proj_cm(Win1, bWin1z, E, cg, Tt)
                A("act", lambda h, pz=pz: h.activation(out=tmpZ[:, :n4], in_=pz[:, :n4], func=AF.Silu), rd=[bpz], wr=[btmpZ])
                pb, bpb = banks[cg]
                for i in range(4):
                    ct = cg * 4 + i
                    A("dve", lambda h, pb=pb, i=i, ct=ct: h.scalar_tensor_tensor(
                        out=tmpS[:, i * Tt:(i + 1) * Tt], in0=pb[:, i * Tt:(i + 1) * Tt], scalar=cvec[:, HG + ct:HG + ct + 1],
                        in1=xcf[:, ct, :Tt], op0=ALU.mult, op1=ALU.add), rd=[bpb, bcvec, bxcfc[ct]], wr=[btmpS])
                A("dve", lambda h, cg=cg: h.tensor_tensor(
                    out=outT[:, cg * 4:(cg + 1) * 4, :Tt], in0=tmpS[:, :n4].rearrange("p (a t) -> p a t", a=4),
                    in1=tmpZ[:, :n4].rearrange("p (a t) -> p a t", a=4), op=ALU.mult), rd=[btmpS, btmpZ], wr=[boutT])
            out_proj(Wout1, bWout1, Tt, par, outT)
            A("act", lambda h: h.activation(out=JUNK[:Tt, :], in_=X[:Tt, :], func=AF.Square, accum_out=sm[:Tt, 86:87]),
              rd=[bX], wr=[btmpZ, btmpS, bsm])
            A("act", lambda h: h.activation(out=sm[:Tt, 87:88], in_=sm[:Tt, 86:87], func=AF.Sqrt, scale=1.0 / DM, bias=cvec_eps[:Tt, 0:1]),
              rd=[bsm, bceps], wr=[bsm])
            A("dve", lambda h: h.reciprocal(out=sm[:Tt, 88:89], in_=sm[:Tt, 87:88]), rd=[bsm], wr=[bsm])
            A("dve", lambda h: h.scalar_tensor_tensor(out=X[:Tt, :], in0=X[:Tt, :], scalar=sm[:Tt, 88:89], in1=FGb[:Tt, :],
                                                      op0=ALU.mult, op1=ALU.mult), rd=[bX, bsm, bFGb], wr=[bX])
            A("sp", lambda h: h.dma_start(out=out_ap, in_=X[:Tt, :]), rd=[bX], dma=1, key=key)

        def conv_tail_out(nstr, nrows, out_ap, key):
            for q in range(4):
                pb, bpb = bank()
                for i in range(4):
                    ct = q * 4 + i
                    A("pe", lambda h, pb=pb, i=i, ct=ct: h.matmul(pb[:nrows, i * 128:(i + 1) * 128], lhsT=halo[:, ct, 0:nrows],
                                                                  rhs=identf[:, :], start=True, stop=True),
                      rd=[bhalo, bident], wr=[bpb])
                A("dve", lambda h, pb=pb, q=q: h.tensor_copy(out=crow[:nrows, q * 512:(q + 1) * 512], in_=pb[:nrows, :]),
                  rd=[bpb], wr=[bcrow])
            A("sp", lambda h: h.dma_start(out=out_ap, in_=crow[:nrows, :]), rd=[bcrow], dma=1, key=key)

        layer0_tile(xs[:, :], (NT + 1) * 128, 64, 0, True, pre=False, nxt=(xp[0:128, :], 128))
        for i in range(NT + 1):
            nx = (xp[(i + 1) * 128:(i + 2) * 128, :], 128) if i < NT else None
            layer0_tile(xp[i * 128:(i + 1) * 128, :], i * 128, 128, (i + 1) % 2, False, pre=True, nxt=nx)

        def zero_state(extra=()):
            for hh in range(4):
                A("pool", lambda h, hh=hh: h.memset(Cst[:, hh * 4:(hh + 1) * 4, :], 0.0), wr=bC[hh * 4:(hh + 1) * 4] + list(extra))
            A("pool", lambda h: h.memset(nst[:], 0.0), wr=[bnst])
            A("pool", lambda h: h.memset(mst[:], 0.0), wr=[bmst])

        def Cv(hh, dt_):
            return Cst[:, hh * 4 + dt_, :]

        bsin = P.buf("summ_in")
        bsout = P.buf("summ_out")
        SROW = (NT + 1) * 128

        def phase_w1():
            W0ALL = [bWin0, bWin0u, bWin0z, bWout0]
            f_ = ld_w(Win1, b_w_in, 8, (0, E)); A("pool", f_, wr=[bWin1] + W0ALL, dma=f_.n, key="Win1")
            A("pool", lambda h: [h.dma_start(out=Wbd_t[:, i * 2048:(i + 1) * 2048], in_=wbd[:, i * 2048:(i + 1) * 2048]) for i in range(3)], wr=[bWbd, bvhat, btmpZ, btmpS], dma=3, key="Wbd")

            zero_state(extra=[bv_f32, bbiasT, bbiasTs, btmpU, btmpZ, btmpS, bWout0, bxcT, bxmT, bqT, bkT, bkw, bvtok, bCb[0], bCb[1]])
            A("pool", lambda h: h.memset(Bacc[:], 0.0), wr=[bBacc])
            A("sp", lambda h: h.dma_start(out=FGb[:, :], in_=fng.partition_broadcast(128)),
              wr=[bFGb, bWmT, bWmTs, bwtmp, bwtmp2], dma=1, key="FGb")


        def phase_A():
            import os
            KA = int(os.environ.get("KA", str(NT)))
            KSCAN = int(os.environ.get("KSCAN", "1"))
            l1_front(0, 128, 0, 1, 128, False, "zero", pre=False, nxt=128, halo_only=True)
            for i in range(KA):
                l1_front((i + 1) * 128, 128, (i + 1) % 2, 1, 128, False, "flag" if i == 0 else "copy",
                         pre=True, nxt=((i + 2) * 128 if i < KA - 1 else None))
                if KSCAN:
                    scan_chunk(0, 128, False, Cv, nst, bC, bnst, mst, bmst, acc_B=True)
                if i == 0:
                    W0ALL = [bWin0, bWin0u, bWin0z, bWout0]
                    f_ = ld_w(Win1, b_w_in, 8, (E, 2 * E)); A("pool", f_, wr=[bWin1z] + W0ALL, dma=f_.n, key="Win1z")
                    f_ = ld_w(Wout1, b_w_out, 4); A("pool", f_, wr=[bWout1] + W0ALL, dma=f_.n, key="Wout1")

        def phase_X():
            A("dve", lambda h: h.memset(misc[:], 0.0), wr=[bmisc])
            A("dve", lambda h: h.tensor_copy(out=misc[:, 0:16], in_=nst[:, :]), rd=[bnst], wr=[bmisc])
            A("dve", lambda h: h.tensor_copy(out=misc[:, 16:20], in_=mst[:, :]), rd=[bmst], wr=[bmisc])
            A("dve", lambda h: h.tensor_copy(out=misc[:, 20:24], in_=Bacc[:, :]), rd=[bBacc], wr=[bmisc])
            A("sp", lambda h: [h.dma_start(out=summ_in[hh][:, :].rearrange("(a p) e -> p a e", p=128), in_=Cst[:, hh * 4:(hh + 1) * 4, :])
                               for hh in range(4)], rd=bC, wr=[bsin], dma=4, key="summC")
            A("sp", lambda h: h.dma_start(out=summ_in_m[:, :], in_=misc[:, :]), rd=[bmisc], wr=[bsin], dma=1, key="summM")
            RG = [[0, 1, 2, 3], [4, 5, 6, 7]]
            for hh in range(4):
                A("pool", lambda h, hh=hh: h.collective_compute("AllGather", ALU.bypass, replica_groups=RG,
                                                                ins=[summ_in[hh].ap().opt()], outs=[summ_out[hh].ap().opt()]),
                  rd=[bsin], wr=[bsout], dma="cc", key="cc")
            A("pool", lambda h: h.collective_compute("AllGather", ALU.bypass, replica_groups=RG,
                                                     ins=[summ_in_m.ap().opt()], outs=[summ_out_m.ap().opt()]),
              rd=[bsin], wr=[bsout], dma="cc", key="cc")

        def phase_S():
            l1_front(SROW, 64, 0, 4, 16, True, "sample", conv_out=(convs[:, :], "convs"))
            sbanks = [bank(reserve=True) for _ in range(4)]
            for j in range(4):
                for hh in range(4):
                    A("sp", lambda h, j=j, hh=hh: h.dma_start(
                        out=Cst[:, hh * 4:(hh + 1) * 4, :],
                        in_=stC[j * 2048 + hh * 512:j * 2048 + (hh + 1) * 512, :].rearrange("(a p) e -> p a e", p=128)),
                      wr=bC[hh * 4:(hh + 1) * 4], dma=1, key=f"Cld{hh}")
                A("sp", lambda h, j=j: h.dma_start(out=nst[:, :], in_=stn[j, :, :]), wr=[bnst], dma=1, key="nld")
                A("sp", lambda h, j=j: h.dma_start(out=mst[:, :], in_=stm[j:j + 1, :].partition_broadcast(128)),
                  wr=[bmst], dma=1, key="mld")
                scan_chunk(16 * j, 16, True, Cv, nst, bC, bnst, mst, bmst)
                hn_transpose(sbanks, 16 * j, 16, 64)
                for hh in range(4):
                    A("sp", lambda h, j=j, hh=hh: h.dma_start(
                        out=Cs[j * 2048 + hh * 512:j * 2048 + (hh + 1) * 512, :].rearrange("(a p) e -> p a e", p=128),
                        in_=Cst[:, hh * 4:(hh + 1) * 4, :]), rd=bC[hh * 4:(hh + 1) * 4], dma=1, key=f"Cst_out{hh}")
                A("sp", lambda h, j=j: h.dma_start(out=ns_o[j, :, :], in_=nst[:, :]), rd=[bnst], dma=1, key="nst_out")
                A("sp", lambda h, j=j: h.dma_start(out=ms_o[j:j + 1, :], in_=mst[0:1, :]), rd=[bmst], dma=1, key="mst_out")
            l1_tail(sbanks, 64, 0, ys[:, :], "ys")
            for pb, _ in sbanks:
                release(pb)

        def phase_C():
            zero_state()
            for r in range(3):
                A("sp", lambda h, r=r: h.dma_start(out=misc[:, :], in_=summ_out_m[r * 128:(r + 1) * 128, :]),
                  rd=[bsout], wr=[bmisc], dma=1, key="miscld")
                pm = flg[:, 1 + r:2 + r]
                A("dve", lambda h, pm=pm: h.tensor_scalar(out=pmisc[:, 20:24], in0=misc[:, 20:24], scalar1=pm, scalar2=None, op0=ALU.mult),
                  rd=[bmisc, bflg], wr=[bpmisc])
                A("dve", lambda h, pm=pm: h.tensor_scalar(out=pmisc[:, 28:29], in0=pm, scalar1=-1.0, scalar2=1e30, op0=ALU.add, op1=ALU.mult),
                  rd=[bflg], wr=[bpmisc])
                A("dve", lambda h, pm=pm: h.tensor_scalar(out=pmisc[:, 24:28], in0=misc[:, 16:20], scalar1=pm, scalar2=pmisc[:, 28:29],
                                                          op0=ALU.mult, op1=ALU.add), rd=[bmisc, bflg, bpmisc], wr=[bpmisc])
                A("dve", lambda h: h.tensor_tensor(out=pmisc[:, 44:48], in0=pmisc[:, 20:24], in1=mst[:, :], op=ALU.add),
                  rd=[bpmisc, bmst], wr=[bpmisc])
                A("dve", lambda h: h.tensor_tensor(out=pmisc[:, 32:36], in0=pmisc[:, 44:48], in1=pmisc[:, 24:28], op=ALU.max),
                  rd=[bpmisc], wr=[bpmisc])
                A("dve", lambda h: h.tensor_tensor(out=pmisc[:, 44:48], in0=pmisc[:, 44:48], in1=pmisc[:, 32:36], op=ALU.subtract),
                  rd=[bpmisc], wr=[bpmisc])
                A("act", lambda h: h.activation(out=pmisc[:, 36:40], in_=pmisc[:, 44:48], func=AF.Exp), rd=[bpmisc], wr=[bpmisc])
                A("dve", lambda h: h.tensor_tensor(out=pmisc[:, 44:48], in0=pmisc[:, 24:28], in1=pmisc[:, 32:36], op=ALU.subtract),
                  rd=[bpmisc], wr=[bpmisc])
                A("act", lambda h: h.activation(out=pmisc[:, 40:44], in_=pmisc[:, 44:48], func=AF.Exp), rd=[bpmisc], wr=[bpmisc])
                A("dve", lambda h: h.tensor_copy(out=mst[:, :], in_=pmisc[:, 32:36]), rd=[bpmisc], wr=[bmst])
                for hh in range(4):
                    A("sp", lambda h, r=r, hh=hh: h.dma_start(
                        out=hn[:, :].rearrange("p (a e) -> p a e", a=4),
                        in_=summ_out[hh][r * 512:(r + 1) * 512, :].rearrange("(a p) e -> p a e", p=128)),
                      rd=[bsout], wr=[bhn], dma=1, key="Crld")
                    for dt_ in range(4):
                        A("act", lambda h, hh=hh, dt_=dt_: h.activation(out=Cv(hh, dt_), in_=Cv(hh, dt_), func=AF.Copy,
                                                                        scale=pmisc[:, 36 + hh:37 + hh]), rd=[bpmisc, bC[hh * 4 + dt_]], wr=[bC[hh * 4 + dt_]])
                        A("dve", lambda h, hh=hh, dt_=dt_: h.scalar_tensor_tensor(
                            out=Cv(hh, dt_), in0=hn[:, dt_ * 512:(dt_ + 1) * 512], scalar=pmisc[:, 40 + hh:41 + hh], in1=Cv(hh, dt_),
                            op0=ALU.mult, op1=ALU.add), rd=[bhn, bpmisc, bC[hh * 4 + dt_]], wr=[bC[hh * 4 + dt_]])
                    A("dve", lambda h, hh=hh: h.tensor_scalar(out=nst[:, hh * 4:(hh + 1) * 4], in0=nst[:, hh * 4:(hh + 1) * 4],
                                                              scalar1=pmisc[:, 36 + hh:37 + hh], scalar2=None, op0=ALU.mult),
                      rd=[bpmisc, bnst], wr=[bnst])
                    A("dve", lambda h, hh=hh: h.scalar_tensor_tensor(
                        out=nst[:, hh * 4:(hh + 1) * 4], in0=misc[:, hh * 4:(hh + 1) * 4], scalar=pmisc[:, 40 + hh:41 + hh],
                        in1=nst[:, hh * 4:(hh + 1) * 4], op0=ALU.mult, op1=ALU.add), rd=[bmisc, bpmisc, bnst], wr=[bnst])

        def phase_B():
            A("act", lambda h: h.activation(out=halo[:, :, 0:3], in_=halo0[:, :, :], func=AF.Copy), rd=[bhalo0], wr=[bhalo])
            for i in range(NT):
                pr = i % 2
                l1_front((i + 1) * 128, 128, pr, 1, 128, True, "flag" if i == 0 else "copy",
                         conv_out=(convp[:, :], "convp") if i == NT - 1 else None,
                         pre=("B0" if i == 0 else "B1"), nxt=((i + 2) * 128 if i < NT - 1 else None))
                scan_chunk(0, 128, True, Cv, nst, bC, bnst, mst, bmst)
                banks = [bank(reserve=True) for _ in range(4)]
                hn_transpose(banks, 0, 128, 128)
                l1_tail(banks, 128, pr, yp[i * 128:(i + 1) * 128, :], "yp")
                for pb, _ in banks:
                    release(pb)
            A("sp", lambda h: h.dma_start(out=Cp[:, :].rearrange("(a p) e -> p a e", p=128), in_=Cst[:, :, :]), rd=bC, dma=1, key="Cp")
            A("sp", lambda h: h.dma_start(out=np_o[:, :], in_=nst[:, :]), rd=[bnst], dma=1, key="np")
            A("sp", lambda h: h.dma_start(out=mp_o[:, :], in_=mst[0:1, :]), rd=[bmst], dma=1, key="mp")

        import os
        KSTOP = int(os.environ.get('KSTOP', '9'))
        for _k, _f in enumerate([phase_w1, phase_A, phase_X, phase_S, phase_C, phase_B]):
            if KSTOP > _k:
                _f()
        P.emit(nc, st)
    return nc


_NC_CACHE = {}


def kernel(**inputs):
    f = lambda k: np.ascontiguousarray(np.asarray(inputs[k], dtype=np.float32))
    x_prompt, x_sample = f("x_prompt"), f("x_sample")
    stC_, stn_, stm_, stcv_ = f("state_mlstm_C"), f("state_mlstm_n"), f("state_mlstm_m"), f("state_mlstm_conv")

    def cols(v, n):
        return np.ascontiguousarray(v.reshape(n, 128).T)

    ngc = np.concatenate([cols(f("norm_g")[0], 8), cols(f("norm_g")[1], 8)], axis=1)
    cw = f("b_conv_w")[0]
    b_cw = np.ascontiguousarray(cw.reshape(4, 16, 128).transpose(2, 1, 0).reshape(128, 64))

    def bdiag(w):
        out = np.zeros((16, 128, 128), np.float32)
        wr = w.reshape(16, 32, 4, 4)
        for n in range(32):
            out[:, 4 * n:4 * n + 4, 4 * n:4 * n + 4] = wr[:, n]
        return out
    bds = [bdiag(f(k)[0]) for k in ("b_wq", "b_wk", "b_wv")]
    wbd = np.ascontiguousarray(np.concatenate(bds, 0).transpose(1, 0, 2).reshape(128, 48 * 128))
    wbdT = np.ascontiguousarray(np.concatenate(bds, 0).transpose(2, 0, 1).reshape(128, 48 * 128))
    wg = f("b_w_gates")[0]
    b_wg = np.ascontiguousarray(wg.reshape(48, 128, 8).transpose(1, 0, 2).reshape(128, 48 * 8))
    shared = {
        "ngc": ngc, "fng": f("final_norm_g").reshape(1, DM),
        "a_w_in": f("a_w_in")[0], "a_lng": cols(f("a_ln_g")[0], 16), "a_lnb": cols(f("a_ln_b")[0], 16),
        "a_lng_row": f("a_ln_g")[0].reshape(1, E), "a_lnb_row": f("a_ln_b")[0].reshape(1, E),
        "a_ws": f("a_w_s")[0].reshape(8 * 128, 128), "a_bs": f("a_b_s")[0].reshape(1, 8 * 128),
        "a_w_out": f("a_w_out")[0], "b_w_in": f("b_w_in")[0], "b_cw": b_cw, "b_cb": cols(f("b_conv_b")[0], 16),
        "wbd": wbd, "wbdT": wbdT, "b_wg": b_wg, "b_bg": f("b_b_gates")[0].reshape(1, 8),
        "b_hg": cols(f("b_hnorm_g")[0], 16), "b_skip": cols(f("b_skip")[0], 16), "b_w_out": f("b_w_out")[0],
    }
    in_maps = []
    for c in range(8):
        b, g = c // 4, c % 4
        xpc = np.zeros(((NT + 1) * 128, DM), np.float32)
        lo = g * 2048 - 128
        if g == 0:
            xpc[128:] = x_prompt[b, 0:2048]
        else:
            xpc[:] = x_prompt[b, lo:lo + (NT + 1) * 128]
        fl = np.zeros((1, 8), np.float32)
        fl[0, 0] = 0.0 if g == 0 else 1.0
        for r in range(3):
            fl[0, 1 + r] = 1.0 if r < g else 0.0
        sl = slice(4 * c, 4 * c + 4)
        m = dict(shared)
        m.update({
            "xp": xpc, "xs": np.ascontiguousarray(x_sample[sl].reshape(64, DM)),
            "stC": np.ascontiguousarray(stC_[0, sl].reshape(16 * 512, 512)),
            "stn": np.ascontiguousarray(stn_[0, sl].reshape(4, 4, 4, 128).transpose(0, 3, 1, 2).reshape(4, 128, 16)),
            "stm": np.ascontiguousarray(stm_[0, sl].reshape(4, 4)),
            "stconv": np.ascontiguousarray(stcv_[0, sl].reshape(12, E)),
            "flags": fl,
        })
        in_maps.append(m)
    if "nc" not in _NC_CACHE:
        _NC_CACHE["nc"] = build_program()
    res = run_bass_kernel_spmd(_NC_CACHE["nc"], in_maps, core_ids=list(range(8)))
    R = res.results
    y_prompt = np.stack([np.concatenate([R[b * 4 + g]["yp"] for g in range(4)], 0) for b in range(2)]).astype(np.float32)
    y_sample = np.concatenate([R[c]["ys"].reshape(4, 16, DM) for c in range(8)], 0).astype(np.float32)
    sgu_v = np.concatenate([R[c]["sguv"].reshape(4, 16, E) for c in range(8)], 0)[None].astype(np.float32)

    def n_from(a):
        return a.reshape(128, 4, 4).transpose(1, 2, 0).reshape(4, 512)
    C_prompt = np.stack([R[b * 4 + 3]["Cp"].reshape(4, 512, 512) for b in range(2)])[None].astype(np.float32)
    n_prompt = np.stack([n_from(R[b * 4 + 3]["np_o"]) for b in range(2)])[None].astype(np.float32)
    m_prompt = np.stack([R[b * 4 + 3]["mp_o"].reshape(4) for b in range(2)])[None].astype(np.float32)
    conv_prompt = np.stack([R[b * 4 + 3]["convp"].reshape(3, E) for b in range(2)])[None].astype(np.float32)
    C_sample = np.concatenate([R[c]["Cs"].reshape(4, 4, 512, 512) for c in range(8)], 0)[None].astype(np.float32)
    n_sample = np.concatenate([np.stack([n_from(R[c]["ns_o"][j]) for j in range(4)]) for c in range(8)], 0)[None].astype(np.float32)
    m_sample = np.concatenate([R[c]["ms_o"].reshape(4, 4) for c in range(8)], 0)[None].astype(np.float32)
    conv_sample = np.concatenate([R[c]["convs"].reshape(4, 3, E) for c in range(8)], 0)[None].astype(np.float32)
    return (y_prompt, y_sample, sgu_v, C_prompt, n_prompt, m_prompt, conv_prompt,
            C_sample, n_sample, m_sample, conv_sample)
```

```python
import numpy as np
import concourse.bass as bass
import concourse.mybir as mybir
from concourse.bass_utils import run_bass_kernel_spmd
from contextlib import ExitStack

F32 = mybir.dt.float32
BF16 = mybir.dt.bfloat16
AF = mybir.ActivationFunctionType
ALU = mybir.AluOpType
AX = mybir.AxisListType

NT = 16
DM = 1024
E = 2048
H = 4
DH = 512
RMS_EPS = 1e-6
LN_EPS = 1e-5
KSCALE = float(DH ** -0.5)


class Buf:
    __slots__ = ("name", "lw", "rd")

    def __init__(self, name):
        self.name = name
        self.lw = None
        self.rd = []


class Op:
    __slots__ = ("eng", "fn", "deps", "dma", "key", "done", "need_inc", "tag", "late")


class Prog:
    ENG = ["pe", "act", "dve", "pool", "sp"]

    def __init__(self, same_engine_sync=False):
        self.ops = {e: [] for e in self.ENG}
        self.same_engine_sync = same_engine_sync
        self.late_bufs = set()
        import os
        self.sync_engs = set(os.environ.get("KSYNC", "dve,act,pool").split(","))
        self.nbuf = 0
        import os
        self.limit = int(os.environ.get("KLIMIT", "100000000"))

    def buf(self, name=None):
        self.nbuf += 1
        return Buf(name or f"b{self.nbuf}")

    def add(self, eng, fn, rd=(), wr=(), dma=0, key=None, tag=None, cc=False):
        self.nadd = getattr(self, "nadd", 0) + 1
        if self.nadd > self.limit:
            return None
        if self.nadd == self.limit:
            import inspect
            fr = inspect.stack()[1]
            print("LAST OP", eng, fr.lineno, flush=True)
        op = Op()
        op.eng = eng
        op.fn = fn
        op.dma = dma
        op.key = key
        op.done = None
        op.need_inc = bool(dma)
        op.tag = tag
        op.late = any(id(b) in self.late_bufs for b in wr)
        deps = []
        for b in rd:
            if b.lw is not None:
                deps.append(b.lw)
        for b in wr:
            if b.lw is not None:
                deps.append(b.lw)
            deps.extend(b.rd)
        seen = set()
        dd = []
        for d in deps:
            if id(d) in seen or d is op:
                continue
            seen.add(id(d))
            if (not d.dma) and d.eng == eng and not dma:
                if eng == "pe" or not (self.same_engine_sync or d.late or eng in self.sync_engs):
                    continue
            dd.append(d)
        op.deps = dd
        for d in dd:
            d.need_inc = True
        for b in rd:
            b.rd.append(op)
        for b in wr:
            b.lw = op
            b.rd = []
        self.ops[eng].append(op)
        return op

    def emit(self, nc, stack):
        esem = {e: stack.enter_context(nc.semaphore(f"s_{e}")) for e in self.ENG}
        dsem = {}
        ecount = {e: 0 for e in self.ENG}
        dcount = {}
        for e in self.ENG:
            for op in self.ops[e]:
                if op.dma:
                    k = op.key
                    if k not in dsem:
                        dsem[k] = stack.enter_context(nc.semaphore(f"d_{len(dsem)}"))
                        dcount[k] = 0
        for e in self.ENG:
            for op in reversed(self.ops[e]):
                if not op.dma:
                    op.need_inc = True
                    break
        for e in self.ENG:
            for op in self.ops[e]:
                if op.dma == "cc":
                    dcount[op.key] += 1
                    op.done = (dsem[op.key], dcount[op.key])
                elif op.dma:
                    dcount[op.key] += 16 * int(op.dma)
                    op.done = (dsem[op.key], dcount[op.key])
                elif op.need_inc:
                    ecount[e] += 1
                    op.done = (esem[e], ecount[e])
        self.final = [(dsem[k], dcount[k]) for k in dsem] + [
            (esem[e], ecount[e]) for e in self.ENG if ecount[e] > 0]
        prog = self

        def run_engine(ename, h, extra_final=False):
            seen = {}
            for op in prog.ops[ename]:
                waits = {}
                for d in op.deps:
                    s, v = d.done
                    key = id(s)
                    if key not in waits or waits[key][1] < v:
                        waits[key] = (s, v)
                for key, (s, v) in waits.items():
                    if seen.get(key, 0) >= v:
                        continue
                    h.wait_ge(s, v)
                    seen[key] = v
                res = op.fn(h)
                if op.dma == "cc":
                    res.then_inc(op.done[0], 1)
                elif op.dma:
                    if not isinstance(res, (list, tuple)):
                        res = [res]
                    assert len(res) == int(op.dma), (op.tag, len(res), op.dma)
                    for r in res:
                        r.then_inc(op.done[0], 16)
                elif op.need_inc:
                    res.then_inc(op.done[0], 1)
            if extra_final:
                for s, v in prog.final:
                    if seen.get(id(s), 0) >= v:
                        continue
                    h.wait_ge(s, v)

        print("TOTAL OPS", getattr(self, "nadd", 0), flush=True)
        with nc.Block() as block:
            @block.tensor
            def _(h):
                run_engine("pe", h)

            @block.scalar
            def _(h):
                run_engine("act", h)

            @block.vector
            def _(h):
                run_engine("dve", h)

            @block.gpsimd
            def _(h):
                run_engine("pool", h)

            @block.sync
            def _(h):
                run_engine("sp", h, extra_final=True)


def build_program():
    nc = bass.Bass("TRN2", target_bir_lowering=False)
    P = Prog(same_engine_sync=False)
    st = ExitStack()

    def din(name, shape):
        return nc.dram_tensor(name, list(shape), F32, kind="ExternalInput").ap()

    def dout(name, shape):
        return nc.dram_tensor(name, list(shape), F32, kind="ExternalOutput").ap()

    xp = din("xp", [(NT + 1) * 128, DM])
    xs = din("xs", [64, DM])
    stC = din("stC", [16 * 512, 512])
    stn = din("stn", [4, 128, 16])
    stm = din("stm", [4, 4])
    stconv = din("stconv", [12, E])
    flags = din("flags", [1, 8])
    ngc = din("ngc", [128, 16])
    fng = din("fng", [1, DM])
    a_w_in = din("a_w_in", [DM, 3 * E])
    a_lng = din("a_lng", [128, 16])
    a_lnb = din("a_lnb", [128, 16])
    a_lng_row = din("a_lng_row", [1, E])
    a_lnb_row = din("a_lnb_row", [1, E])
    a_ws = din("a_ws", [8 * 128, 128])
    a_bs = din("a_bs", [1, 8 * 128])
    a_w_out = din("a_w_out", [E, DM])
    b_w_in = din("b_w_in", [DM, 2 * E])
    b_cw = din("b_cw", [128, 64])
    b_cb = din("b_cb", [128, 16])
    wbd = din("wbd", [128, 48 * 128])
    wbdT = din("wbdT", [128, 48 * 128])
    b_wg = din("b_wg", [128, 48 * 8])
    b_bg = din("b_bg", [1, 8])
    b_hg = din("b_hg", [128, 16])
    b_skip = din("b_skip", [128, 16])
    b_w_out = din("b_w_out", [E, DM])

    yp = dout("yp", [NT * 128, DM])
    ys = dout("ys", [64, DM])
    sguv = dout("sguv", [64, E])
    Cp = dout("Cp", [16 * 128, 512])
    np_o = dout("np_o", [128, 16])
    mp_o = dout("mp_o", [1, 4])
    convp = dout("convp", [3, E])
    Cs = dout("Cs", [4 * 16 * 128, 512])
    ns_o = dout("ns_o", [4, 128, 16])
    ms_o = dout("ms_o", [4, 4])
    convs = dout("convs", [12, E])

    x1s = nc.dram_tensor("x1s", [(NT + 1) * 128 + 64, DM], F32).ap()
    summ_in = [nc.dram_tensor(f"summ_in{i}", [512, 512], F32) for i in range(4)]
    summ_out = [nc.dram_tensor(f"summ_out{i}", [4 * 512, 512], F32) for i in range(4)]
    summ_in_m = nc.dram_tensor("summ_in_m", [128, 512], F32)
    summ_out_m = nc.dram_tensor("summ_out_m", [4 * 128, 512], F32)

    with st:
        def sb(name, shape, dt=F32):
            return st.enter_context(nc.sbuf_tensor(name, list(shape), dt))

        BIGW = sb("BIGW", [128, 65536], BF16)
        F32A = sb("F32A", [128, 8192], F32)
        Win0 = BIGW[:, 0:49152].rearrange("p (a c) -> p a c", a=8)
        Wout0 = BIGW[:, 49152:65536].rearrange("p (a c) -> p a c", a=16)
        Win1 = BIGW[:, 0:32768].rearrange("p (a c) -> p a c", a=8)
        Wout1 = BIGW[:, 32768:49152].rearrange("p (a c) -> p a c", a=16)
        SP = 49152
        xcT = BIGW[:, SP:SP + 2048].rearrange("p (a c) -> p a c", a=16)
        xmT = BIGW[:, SP + 2048:SP + 4096].rearrange("p (a c) -> p a c", a=16)
        qT = BIGW[:, SP + 4096:SP + 6144].rearrange("p (a c) -> p a c", a=16)
        kT = BIGW[:, SP + 6144:SP + 8192].rearrange("p (a c) -> p a c", a=16)
        kw = BIGW[:, SP + 8192:SP + 10240]
        vtok = BIGW[:, SP + 10240:SP + 12288]
        Cb0 = BIGW[:, SP + 12288:SP + 14336].rearrange("p (a c) -> p a c", a=4)
        Cb = [Cb0, Cb0]
        JUNK = BIGW[:, SP + 14336:SP + 16384].bitcast(F32)
        tmpZ_2 = BIGW[:, SP + 14336:SP + 15360].bitcast(F32)
        tmpS_2 = BIGW[:, SP + 15360:SP + 16384].bitcast(F32)
        bWin0, bWout0, bWin1, bWout1 = P.buf("Win0"), P.buf("Wout0"), P.buf("Win1"), P.buf("Wout1")
        bWin0u, bWin0z, bWin1z = P.buf("Win0u"), P.buf("Win0z"), P.buf("Win1z")
        bxcT, bxmT, bqT, bkT, bkw, bvtok = [P.buf(n) for n in ("xcT", "xmT", "qT", "kT", "kw", "vtok")]
        bCb0 = P.buf("Cb0"); bCb = [bCb0, bCb0]
        v_f32 = F32A[:, 0:2048]
        biasT = F32A[:, 2048:4096].rearrange("p (a c) -> p a c", a=16)
        biasTs = F32A[:, 4096:5120].rearrange("p (a c) -> p a c", a=16)
        tmpU = F32A[:, 5120:5632]
        RSb = F32A[:, 6656:7680].rearrange("p (a c) -> p a c", a=8)
        Cst = F32A[:, 0:8192].rearrange("p (a c) -> p a c", a=16)
        bv_f32, bbiasT, bbiasTs, btmpU = P.buf("v_f32"), P.buf("biasT"), P.buf("biasTs"), P.buf("tmpU")
        bC = [P.buf(f"C{h}") for h in range(16)]

        Wbd_t = sb("Wbd", [128, 48 * 128], BF16); bWbd = P.buf("Wbd")
        Wbd = Wbd_t[:, :].rearrange("p (a c) -> p a c", a=48)
        vhat = Wbd_t[:, 0:E]; bvhat = P.buf("vhat")
        xt0 = sb("xt0", [128, DM]); bxt0 = P.buf("xt0")
        xsf = sb("xsf", [128, DM]); bxsf = P.buf("xsf")
        xt = [xt0, xsf]; bxt = [bxt0, bxsf]
        xnT = sb("xnT", [128, 8, 128], BF16); bxnT = P.buf("xnT"); bxnTs = [P.buf(f"xnT{i}") for i in range(8)]
        outT_1 = F32A[:, 5632:6656].bitcast(BF16).rearrange("p (a c) -> p a c", a=16)
        outT_2 = kw.rearrange("p (a c) -> p a c", a=16)
        boutT = bkw
        XH = sb("XH", [128, 16 * 132]); bxpad = P.buf("xpad"); bhn = bxpad
        xpad = XH[:, :].rearrange("p (a c) -> p a c", a=16)
        hn = XH[:, 0:E]
        halo = sb("halo", [128, 16, 12]); bhalo = P.buf("halo")
        halo0 = sb("halo0", [128, 16, 3]); bhalo0 = P.buf("halo0")
        xcf_t = sb("xcf", [128, 16 * 128]); bxcf = P.buf("xcf")
        bxcfc = [P.buf(f"xcf{i}") for i in range(16)]
        ALLX = [bxcf] + bxcfc
        dummy = sb("fence_t", [128, 2]); bdummy = P.buf("dummy")
        xcf = xcf_t[:, :].rearrange("p (a c) -> p a c", a=16)
        crow = xcf_t; bcrow = bxcf
        tmpZ_1 = Wbd_t[:, 2048:3072].bitcast(F32)
        tmpS_1 = Wbd_t[:, 3072:4096].bitcast(F32)
        btmpZ = P.buf("tmpZ"); btmpS = P.buf("tmpS")
        OV2 = sb("OV2", [128, DM]); bOV2 = P.buf("OV2")
        FGb = OV2; bFGb = bOV2
        identf = sb("identf", [128, 128]); bident = P.buf("ident")
        maskT = sb("maskT", [128, 128]); bmaskT = P.buf("maskT")
        onesf = sb("onesf", [128, 128]); bones = P.buf("ones")
        onesb = sb("onesb", [128, 4], BF16); bonesb = P.buf("onesb")
        WmT = OV2[:, 0:512].bitcast(BF16).rearrange("p (a c) -> p a c", a=8); bWmT = P.buf("WmT")
        WmTs = OV2[:, 512:768].bitcast(BF16).rearrange("p (a c) -> p a c", a=8); bWmTs = P.buf("WmTs")
        wtmp = OV2[:, 768:896]; bwtmp = P.buf("wtmp")
        wtmp2 = OV2[:, 896:1024]; bwtmp2 = P.buf("wtmp2")
        BSb = XH[:, 0:1024].rearrange("p (a c) -> p a c", a=8); bBSb = bxpad
        BSs = XH[:, 1024:1536].rearrange("p (a c) -> p a c", a=8); bBSs = bxpad
        cvec = sb("cvec", [128, 160]); bcvec = P.buf("cvec")
        flg = sb("flg", [128, 8]); bflg = P.buf("flg")
        Wg = sb("Wg", [128, 32, 8], BF16); bWg = P.buf("Wg")
        wgf = xcf_t[:, 0:384]; bwgf = bxcf
        bgb = sb("bgb", [128, 8]); bbgb = P.buf("bgb")
        sm = sb("sm", [128, 96]); bsm = P.buf("sm")
        nst = sb("nst", [128, 16]); bnst = P.buf("nst")
        mst = sb("mst", [128, 4]); bmst = P.buf("mst")
        Bacc = sb("Bacc", [128, 4]); bBacc = P.buf("Bacc")
        nbb = sb("nbb", [128, 4], BF16); bnbb = P.buf("nbb")
        Sw = sb("Sw", [128, 128], BF16); bSw = P.buf("Sw")
        diagA = tmpZ_2; bdiagA = btmpZ
        misc = tmpS_2; bmisc = btmpS
        pmisc = sb("pmisc", [128, 64]); bpmisc = P.buf("pmisc")
        stats = sb("stats", [128, 4, 6]); bstats = P.buf("stats")
        mv = sb("mv", [128, 4, 2]); bmv = P.buf("mv")

        for b_ in (bsm, bpmisc, bstats, bmv, bnbb, bmst, bnst, bBacc):
            P.late_bufs.add(id(b_))
        ps = [st.enter_context(nc.psum_tensor(f"ps{i}", [128, 512], F32)) for i in range(8)]
        bps = [P.buf(f"ps{i}") for i in range(8)]
        pctr = [0]

        reserved = set()

        def bank(reserve=False):
            while True:
                i = pctr[0] % 8
                pctr[0] += 1
                if i not in reserved:
                    break
            if reserve:
                reserved.add(i)
            return ps[i], bps[i]

        def release(pb):
            for i in range(8):
                if ps[i] is pb:
                    reserved.discard(i)

        A = P.add
        dq = [0]

        def dmaq():
            dq[0] += 1
            return "sp"

        A("pool", lambda h: h.memset(identf[:], 1.0), wr=[bident])
        A("pool", lambda h: h.affine_select(out=identf[:], in_=identf[:], pattern=[[-1, 128]],
                                            compare_op=ALU.is_equal, fill=0.0, base=0, channel_multiplier=1),
          rd=[bident], wr=[bident])
        A("pool", lambda h: h.memset(maskT[:], 1.0), wr=[bmaskT])
        A("pool", lambda h: h.affine_select(out=maskT[:], in_=maskT[:], pattern=[[1, 128]],
                                            compare_op=ALU.is_ge, fill=0.0, base=0, channel_multiplier=-1),
          rd=[bmaskT], wr=[bmaskT])
        A("pool", lambda h: h.memset(onesf[:], 1.0), wr=[bones])
        A("pool", lambda h: h.memset(onesb[:], 1.0), wr=[bonesb])
        def ld_cvec(h):
            r = []
            for i, (src, n) in enumerate([(ngc, 16), (a_lng, 16), (a_lnb, 16), (b_cb, 16), (b_hg, 16), (b_skip, 16)]):
                r.append(h.dma_start(out=cvec[:, i * 16:(i + 1) * 16], in_=src[:, :]))
            r.append(h.dma_start(out=cvec[:, 96:160], in_=b_cw[:, :]))
            r.append(h.dma_start(out=flg[:], in_=flags.partition_broadcast(128)))
            r.append(h.dma_start(out=BSb[:].rearrange("p a c -> p (a c)"), in_=a_bs.partition_broadcast(128)))
            r.append(h.dma_start(out=wgf[:], in_=b_wg[:, :]))
            return r
        A("sp", ld_cvec, wr=[bcvec, bflg, bBSb, bwgf], dma=10, key="cvec")
        A("sp", lambda h: h.dma_start(out=bgb[:], in_=b_bg.partition_broadcast(128)), wr=[bbgb], dma=1, key="bgb")
        NG0, NG1, LNG, LNB, CB, HG, SKIP, CW = 0, 8, 16, 32, 48, 64, 80, 96

        def ld_w(dst, src, nchunk, cols=None):
            a, ncol = dst.shape[1], dst.shape[2]
            lo, hi = cols if cols is not None else (0, ncol)
            pieces = [(i, c0) for i in range(a) for c0 in range(lo, hi, 2048)]

            def f(h):
                r = []
                sv = src.rearrange("(a p) c -> p a c", p=128)
                for i, c0 in pieces:
                    c1 = min(hi, c0 + 2048)
                    r.append(h.dma_start(out=dst[:, i, c0:c1], in_=sv[:, i, c0:c1]))
                return r
            f.n = len(pieces)
            return f
        f_ = ld_w(Win0, a_w_in, 8, (E, 2 * E)); A("pool", f_, wr=[bWin0], dma=f_.n, key="Win0v")
        f_ = ld_w(Win0, a_w_in, 8, (0, E)); A("pool", f_, wr=[bWin0u], dma=f_.n, key="Win0u")
        f_ = ld_w(Win0, a_w_in, 8, (2 * E, 3 * E)); A("pool", f_, wr=[bWin0z], dma=f_.n, key="Win0z")
        f_ = ld_w(Wout0, a_w_out, 4); A("pool", f_, wr=[bWout0], dma=f_.n, key="Wout0")

        def sgu_consts(Tt, Wdst, bWdst, BSsrc, bBSsrc, bdst, bbdst, sample):
            for g in range(8):
                if not sample:
                    A("sp", lambda h, g=g: h.dma_start(out=wtmp[:], in_=a_ws[g * 128:(g + 1) * 128, :]),
                      wr=[bwtmp], dma=1, key="wtmp")
                else:
                    A("pool", lambda h: h.memset(wtmp[:], 0.0), wr=[bwtmp])
                    A("sp", lambda h, g=g: [h.dma_start(out=wtmp[16 * j:16 * j + 16, 16 * j:16 * j + 16],
                                                        in_=a_ws[g * 128:g * 128 + 16, 0:16]) for j in range(4)],
                      wr=[bwtmp], dma=4, key="wtmp")
                pb, bpb = bank()
                A("pe", lambda h, pb=pb: h.matmul(pb[:Tt, :Tt], lhsT=wtmp[:Tt, :Tt], rhs=identf[:Tt, :Tt],
                                                  start=True, stop=True), rd=[bwtmp, bident], wr=[bpb])
                A("dve", lambda h, pb=pb: h.tensor_tensor(out=wtmp2[:Tt, :Tt], in0=pb[:Tt, :Tt], in1=maskT[:Tt, :Tt],
                                                          op=ALU.mult), rd=[bpb, bmaskT], wr=[bwtmp2])
                A("pool", lambda h, g=g: h.tensor_copy(out=Wdst[:Tt, g, :Tt], in_=wtmp2[:Tt, :Tt]),
                  rd=[bwtmp2], wr=[bWdst])
                pb2, bpb2 = bank()
                A("pe", lambda h, pb2=pb2: h.matmul(pb2[:, :Tt], lhsT=onesf[:Tt, :], rhs=wtmp2[:Tt, :Tt],
                                                    start=True, stop=True), rd=[bwtmp2, bones], wr=[bpb2])
                for i in range(2):
                    ct = 2 * g + i
                    A("dve", lambda h, pb2=pb2, ct=ct, g=g: h.scalar_tensor_tensor(
                        out=bdst[:, ct, :Tt], in0=pb2[:, :Tt], scalar=cvec[:, LNB + ct:LNB + ct + 1],
                        in1=BSsrc[:, g, :Tt], op0=ALU.mult, op1=ALU.add), rd=[bpb2, bcvec, bBSsrc], wr=[bbdst])

        A("pool", lambda h: h.tensor_copy(out=BSs[:].rearrange("p a (j t) -> p a j t", j=4),
                                          in_=BSb[:, :, 0:16].unsqueeze(2).to_broadcast([128, 8, 4, 16])),
          rd=[bBSb], wr=[bBSs])
        sgu_consts(128, WmT, bWmT, BSb, bBSb, biasT, bbiasT, False)
        sgu_consts(64, WmTs, bWmTs, BSs, bBSs, biasTs, bbiasTs, True)

        for ct in range(16):
            pb, bpb = bank()
            for qi in range(3):
                A("sp", lambda h, qi=qi, ct=ct: h.dma_start(
                    out=wtmp[:], in_=wbdT[:, (qi * 16 + ct) * 128:(qi * 16 + ct + 1) * 128]),
                  wr=[bwtmp], dma=1, key="wtmp")
                col = 0 if qi < 2 else 8
                A("pe", lambda h, pb=pb, qi=qi, ct=ct, col=col: h.matmul(
                    pb[:, col:col + 8], lhsT=wtmp[:], rhs=wgf[:, (qi * 16 + ct) * 8:(qi * 16 + ct + 1) * 8],
                    start=(qi != 1), stop=(qi != 0)), rd=[bwtmp, bwgf], wr=[bpb])
            A("dve", lambda h, pb=pb, ct=ct: h.tensor_copy(out=Wg[:, ct, :], in_=pb[:, 0:8]), rd=[bpb], wr=[bWg])
            A("dve", lambda h, pb=pb, ct=ct: h.tensor_copy(out=Wg[:, 16 + ct, :], in_=pb[:, 8:16]), rd=[bpb], wr=[bWg])

        def rms_pre_a(src_ap, Tt, par, rd, J, bJ):
            X, bX = xt[par], bxt[par]
            A("sp", lambda h: h.dma_start(out=X[:Tt, :], in_=src_ap), rd=list(rd), wr=[bX], dma=1, key=f"xt{par}")
            A("act", lambda h: h.activation(out=J[:Tt, :], in_=X[:Tt, :], func=AF.Square, accum_out=sm[:Tt, 0:1]),
              rd=[bX], wr=list(bJ) + [bsm])
            A("act", lambda h: h.activation(out=sm[:Tt, 1:2], in_=sm[:Tt, 0:1], func=AF.Sqrt, scale=1.0 / DM, bias=cvec_eps[:Tt, 0:1]),
              rd=[bsm, bceps], wr=[bsm])
            A("dve", lambda h: h.reciprocal(out=sm[:Tt, 2:3], in_=sm[:Tt, 1:2]), rd=[bsm], wr=[bsm])

        def rms_pre_b(Tt, par, S, bS):
            X, bX = xt[par], bxt[par]
            A("act", lambda h: h.activation(out=S[:Tt, :], in_=X[:Tt, :], func=AF.Copy, scale=sm[:Tt, 2:3]),
              rd=[bX, bsm], wr=[bS])

        def rms_pre(src_ap, Tt, par, rd, S, bS):
            rms_pre_a(src_ap, Tt, par, rd, S, [bS])
            rms_pre_b(Tt, par, S, bS)

        def rms_post(Tt, gcol, S, bS):
            for half in range(2):
                pb, bpb = bank()
                for i in range(4):
                    dt_ = half * 4 + i
                    A("pe", lambda h, pb=pb, i=i, dt_=dt_: h.matmul(
                        pb[:, i * 128:i * 128 + Tt], lhsT=S[:Tt, dt_ * 128:(dt_ + 1) * 128], rhs=identf[:Tt, :Tt],
                        start=True, stop=True), rd=[bS, bident], wr=[bpb])
                for i in range(4):
                    dt_ = half * 4 + i
                    A("act", lambda h, pb=pb, i=i, dt_=dt_: h.activation(
                        out=xnT[:, dt_, :Tt], in_=pb[:, i * 128:i * 128 + Tt], func=AF.Copy,
                        scale=cvec[:, gcol + dt_:gcol + dt_ + 1]), rd=[bpb, bcvec], wr=[bxnTs[dt_]])

        def rms_front(src_ap, Tt, par, gcol, rd=()):
            rms_pre(src_ap, Tt, par, rd, xt[1 - par], bxt[1 - par])
            rms_post(Tt, gcol, xt[1 - par], bxt[1 - par])

        cvec_eps = sb("cvec_eps", [128, 4]); bceps = P.buf("ceps")
        A("pool", lambda h: h.memset(cvec_eps[:, 0:1], RMS_EPS), wr=[bceps])
        A("pool", lambda h: h.memset(cvec_eps[:, 1:2], LN_EPS), wr=[bceps])
        A("pool", lambda h: h.memset(cvec_eps[:, 2:3], 1.0), wr=[bceps])

        def proj_cm(W, bW, col0, cg, Tt):
            pb, bpb = bank()
            for i in range(4):
                c0 = col0 + (cg * 4 + i) * 128
                for dt_ in range(8):
                    A("pe", lambda h, pb=pb, i=i, c0=c0, dt_=dt_: h.matmul(
                        pb[:, i * Tt:(i + 1) * Tt], lhsT=W[:, dt_, c0:c0 + 128], rhs=xnT[:, dt_, :Tt],
                        start=(dt_ == 0), stop=(dt_ == 7)), rd=[bW, bxnTs[dt_]], wr=[bpb])
            return pb, bpb

        def out_proj(W, bW, Tt, par, outT):
            X, bX = xt[par], bxt[par]
            for half in range(2):
                pb, bpb = bank()
                for ct in range(16):
                    A("pe", lambda h, pb=pb, ct=ct, half=half: h.matmul(
                        pb[:Tt, :], lhsT=outT[:, ct, :Tt], rhs=W[:, ct, half * 512:(half + 1) * 512],
                        start=(ct == 0), stop=(ct == 15)), rd=[boutT, bW], wr=[bpb])
                A("dve", lambda h, pb=pb, half=half: h.tensor_tensor(
                    out=X[:Tt, half * 512:(half + 1) * 512], in0=pb[:Tt, :], in1=X[:Tt, half * 512:(half + 1) * 512],
                    op=ALU.add), rd=[bpb, bX], wr=[bX])

        def layer0_tile(src_ap, dst_row, Tt, par, sample, pre=False, nxt=None):
            X, bX = xt[par], bxt[par]
            outT, tmpZ, tmpS = outT_1, tmpZ_1, tmpS_1
            S0 = v_f32[:, 0:1024]
            if not pre:
                rms_pre(src_ap, Tt, par, (), S0, bv_f32)
            rms_post(Tt, NG0, S0, bv_f32)
            Wm_, bWm_ = (WmTs, bWmTs) if sample else (WmT, bWmT)
            bt_, bbt_ = (biasTs, bbiasTs) if sample else (biasT, bbiasT)
            for cb in range(4):
                pb, bpb = bank()
                for dt_ in range(8):
                    A("pe", lambda h, pb=pb, cb=cb, dt_=dt_: h.matmul(
                        pb[:Tt, :], lhsT=xnT[:, dt_, :Tt], rhs=Win0[:, dt_, E + cb * 512:E + (cb + 1) * 512],
                        start=(dt_ == 0), stop=(dt_ == 7)), rd=[bxnTs[dt_], bWin0], wr=[bpb])
                A("act", lambda h, pb=pb, cb=cb: h.activation(out=v_f32[:Tt, cb * 512:(cb + 1) * 512], in_=pb[:Tt, :],
                                                              func=AF.Gelu_apprx_tanh), rd=[bpb], wr=[bv_f32])
            for cb in range(4):
                A("dve", lambda h, cb=cb: h.bn_stats(out=stats[:Tt, cb, :], in_=v_f32[:Tt, cb * 512:(cb + 1) * 512]),
                  rd=[bv_f32], wr=[bstats])
            A("dve", lambda h: h.bn_aggr(out=sm[:Tt, 82:84], in_=stats[:Tt, :, :].rearrange("p a c -> p (a c)")), rd=[bstats], wr=[bsm])
            A("dve", lambda h: h.tensor_copy(out=sm[:Tt, 84:85], in_=sm[:Tt, 83:84]), rd=[bsm], wr=[bsm])
            A("act", lambda h: h.activation(out=sm[:Tt, 4:5], in_=sm[:Tt, 84:85], func=AF.Sqrt, bias=cvec_eps[:Tt, 1:2]),
              rd=[bsm, bceps], wr=[bsm])
            A("dve", lambda h: h.reciprocal(out=sm[:Tt, 5:6], in_=sm[:Tt, 4:5]), rd=[bsm], wr=[bsm])
            A("dve", lambda h: h.tensor_scalar(out=vhat[:Tt, :], in0=v_f32[:Tt, :], scalar1=sm[:Tt, 82:83],
                                               scalar2=sm[:Tt, 5:6], op0=ALU.subtract, op1=ALU.mult),
              rd=[bv_f32, bsm], wr=[bvhat])
            if sample:
                A("dve", lambda h: h.tensor_scalar(out=v_f32[:Tt, :], in0=v_f32[:Tt, :], scalar1=sm[:Tt, 82:83],
                                                   scalar2=sm[:Tt, 5:6], op0=ALU.subtract, op1=ALU.mult),
                  rd=[bsm], wr=[bv_f32])
                A("sp", lambda h: h.dma_start(out=hn[:Tt, :], in_=a_lng_row.partition_broadcast(Tt)), wr=[bhn], dma=1, key="lnrow")
                A("dve", lambda h: h.tensor_tensor(out=v_f32[:Tt, :], in0=v_f32[:Tt, :], in1=hn[:Tt, :], op=ALU.mult),
                  rd=[bhn, bv_f32], wr=[bv_f32])
                A("sp", lambda h: h.dma_start(out=hn[:Tt, :], in_=a_lnb_row.partition_broadcast(Tt)), wr=[bhn], dma=1, key="lnrow")
                A("dve", lambda h: h.tensor_tensor(out=v_f32[:Tt, :], in0=v_f32[:Tt, :], in1=hn[:Tt, :], op=ALU.add),
                  rd=[bhn, bv_f32], wr=[bv_f32])
                A("sp", lambda h: h.dma_start(out=sguv[:, :], in_=v_f32[:Tt, :]), rd=[bv_f32], dma=1, key="sguv")
            if nxt is not None:
                rms_pre(nxt[0], nxt[1], 1 - par, (), S0, bv_f32)
            for cg in range(4):
                pu, bpu = proj_cm(Win0, bWin0u, 0, cg, Tt)
                pz, bpz = proj_cm(Win0, bWin0z, 2 * E, cg, Tt)
                pS, bpS = bank()
                for i in range(4):
                    ct = cg * 4 + i
                    A("pe", lambda h, pS=pS, i=i, ct=ct: h.matmul(
                        pS[:, i * Tt:(i + 1) * Tt], lhsT=vhat[:Tt, ct * 128:(ct + 1) * 128], rhs=Wm_[:Tt, ct // 2, :Tt],
                        start=True, stop=True), rd=[bvhat, bWm_], wr=[bpS])
                n4 = 4 * Tt
                A("act", lambda h, pu=pu: h.activation(out=tmpU[:, :n4], in_=pu[:, :n4], func=AF.Gelu_apprx_tanh),
                  rd=[bpu], wr=[btmpU])
                A("act", lambda h, pz=pz: h.activation(out=tmpZ[:, :n4], in_=pz[:, :n4], func=AF.Silu),
                  rd=[bpz], wr=[btmpZ])
                for i in range(4):
                    ct = cg * 4 + i
                    A("dve", lambda h, pS=pS, i=i, ct=ct: h.scalar_tensor_tensor(
                        out=tmpS[:, i * Tt:(i + 1) * Tt], in0=pS[:, i * Tt:(i + 1) * Tt],
                        scalar=cvec[:, LNG + ct:LNG + ct + 1], in1=bt_[:, ct, :Tt], op0=ALU.mult, op1=ALU.add),
                      rd=[bpS, bcvec, bbt_], wr=[btmpS])
                A("dve", lambda h: h.tensor_tensor(out=tmpS[:, :n4], in0=tmpS[:, :n4], in1=tmpU[:, :n4], op=ALU.mult),
                  rd=[btmpU, btmpS], wr=[btmpS])
                A("dve", lambda h, cg=cg: h.tensor_tensor(
                    out=outT[:, cg * 4:(cg + 1) * 4, :Tt], in0=tmpS[:, :n4].rearrange("p (a t) -> p a t", a=4),
                    in1=tmpZ[:, :n4].rearrange("p (a t) -> p a t", a=4), op=ALU.mult),
                  rd=[btmpS, btmpZ], wr=[boutT])
            out_proj(Wout0, bWout0, Tt, par, outT)
            A("sp", lambda h: h.dma_start(out=x1s[dst_row:dst_row + Tt, :], in_=X[:Tt, :]), rd=[bX], wr=[bx1s],
              dma=1, key=f"x1st{par}")

        bx1s = P.buf("x1s")

        def l1_front(src_row, Tt, par, nstr, L, need_q, halo_mode, conv_out=None, pre=None, nxt=None, halo_only=False):
            if pre is None:
                rms_front(x1s[src_row:src_row + Tt, :], Tt, par, NG1, rd=[bx1s])
            elif pre == "B0":
                rms_pre(x1s[src_row:src_row + Tt, :], Tt, par, [bx1s], hn[:, 0:1024], bxpad)
                rms_post(Tt, NG1, hn[:, 0:1024], bxpad)
            elif pre == "B1":
                rms_pre_b(Tt, par, hn[:, 0:1024], bxpad)
                rms_post(Tt, NG1, hn[:, 0:1024], bxpad)
            else:
                if not pre:
                    rms_pre(x1s[src_row:src_row + Tt, :], Tt, par, [bx1s], hn[:, 0:1024], bxpad)
                rms_post(Tt, NG1, hn[:, 0:1024], bxpad)
            A("dve", lambda h: h.memset(dummy[:, 0:1], 0.0), wr=ALLX + [bdummy])
            xp4 = xpad[:, :, 0:nstr * (L + 3)].rearrange("p a (j t) -> p a j t", j=nstr)
            if halo_mode == "zero":
                A("pool", lambda h: h.memset(xp4[:, :, :, 0:3], 0.0), wr=[bxpad])
            elif halo_mode == "copy":
                A("act", lambda h: h.activation(out=xp4[:, :, :, 0:3], in_=halo[:, :, 0:3].unsqueeze(2), func=AF.Copy), rd=[bhalo], wr=[bxpad])
            elif halo_mode == "flag":
                A("act", lambda h: h.activation(out=xp4[:, :, :, 0:3], in_=halo[:, :, 0:3].unsqueeze(2), func=AF.Copy,
                                                scale=flg[:, 0:1]), rd=[bhalo, bflg], wr=[bxpad])
            elif halo_mode == "sample":
                A("sp", lambda h: h.dma_start(out=crow[:12, :], in_=stconv[:, :]), wr=[bcrow], dma=1, key="crow")
                pb, bpb = bank()
                for ct in range(16):
                    A("pe", lambda h, pb=pb, ct=ct: h.matmul(pb[:, ct * 12:(ct + 1) * 12], lhsT=crow[:12, ct * 128:(ct + 1) * 128],
                                                             rhs=identf[:12, :12], start=True, stop=True),
                      rd=[bcrow, bident], wr=[bpb])
                A("act", lambda h, pb=pb: h.activation(out=xp4[:, :, :, 0:3],
                                                       in_=pb[:, 0:192].rearrange("p (a j t) -> p a j t", a=16, j=4),
                                                       func=AF.Copy), rd=[bpb], wr=[bxpad])
            for cg in range(4):
                pb, bpb = proj_cm(Win1, bWin1, 0, cg, Tt)
                A("act", lambda h, pb=pb, cg=cg: h.activation(
                    out=xp4[:, cg * 4:(cg + 1) * 4, :, 3:3 + L],
                    in_=pb[:, :4 * Tt].rearrange("p (a j t) -> p a j t", a=4, j=nstr), func=AF.Copy), rd=[bpb], wr=[bxpad])
                A("act", lambda h, cg=cg: h.activation(
                    out=xmT[:, cg * 4:(cg + 1) * 4, :Tt].rearrange("p a (j t) -> p a j t", j=nstr),
                    in_=xp4[:, cg * 4:(cg + 1) * 4, :, 3:3 + L], func=AF.Copy), rd=[bxpad], wr=[bxmT])
            A("act", lambda h: h.activation(out=halo[:, :, 0:3 * nstr].rearrange("p a (j t) -> p a j t", j=nstr),
                                            in_=xp4[:, :, :, L:L + 3], func=AF.Copy), rd=[bxpad], wr=[bhalo])
            if conv_out is not None:
                conv_tail_out(nstr, 3 * nstr, conv_out[0], conv_out[1])
            if halo_only:
                A("act", lambda h: h.activation(out=halo0[:, :, :], in_=halo[:, :, 0:3], func=AF.Copy), rd=[bhalo], wr=[bhalo0])
                if nxt is not None:
                    rms_pre(x1s[nxt:nxt + 128, :], 128, 1 - par, [bx1s], hn[:, 0:1024], bxpad)
                return
            xc4 = xcf[:, :, :Tt].rearrange("p a (j t) -> p a j t", j=nstr)
            A("dve", lambda h: h.memset(dummy[:, 1:2], 0.0), wr=ALLX + [bdummy])
            for ct in range(16):
                A("act", lambda h, ct=ct: h.activation(
                    out=xc4[:, ct], in_=xp4[:, ct, :, 0:L], func=AF.Identity, scale=cvec[:, CW + ct * 4:CW + ct * 4 + 1],
                    bias=cvec[:, CB + ct:CB + ct + 1]), rd=[bxpad, bcvec], wr=[bxcfc[ct]])
            for ct_, j in [(c_, j_) for hf in range(2) for j_ in range(1, 4) for c_ in range(hf * 8, hf * 8 + 8)]:
                for ct in (ct_,):
                    A("dve", lambda h, ct=ct, j=j: h.scalar_tensor_tensor(
                        out=xc4[:, ct], in0=xp4[:, ct, :, j:j + L], scalar=cvec[:, CW + ct * 4 + j:CW + ct * 4 + j + 1],
                        in1=xc4[:, ct], op0=ALU.mult, op1=ALU.add), rd=[bxpad, bcvec, bxcfc[ct]], wr=[bxcfc[ct]])
            for cg in range(4):
                A("act", lambda h, cg=cg: h.activation(out=xcf[:, cg * 4:(cg + 1) * 4, :Tt], in_=xcf[:, cg * 4:(cg + 1) * 4, :Tt],
                                                       func=AF.Silu), rd=bxcfc[cg * 4:(cg + 1) * 4], wr=bxcfc[cg * 4:(cg + 1) * 4])
            for cg in range(4):
                A("act", lambda h, cg=cg: h.activation(out=xcT[:, cg * 4:(cg + 1) * 4, :Tt], in_=xcf[:, cg * 4:(cg + 1) * 4, :Tt],
                                                       func=AF.Copy), rd=bxcfc[cg * 4:(cg + 1) * 4], wr=[bxcT])
            if nxt is not None and pre in ("B0", "B1"):
                rms_pre_a(x1s[nxt:nxt + 128, :], 128, 1 - par, [bx1s], JUNK, [btmpZ, btmpS])
            elif nxt is not None:
                rms_pre(x1s[nxt:nxt + 128, :], 128, 1 - par, [bx1s], hn[:, 0:1024], bxpad)
            if need_q:
                for qi, (dst, bdst, scl) in enumerate([(qT, bqT, 1.0), (kT, bkT, KSCALE)]):
                    for cg in range(4):
                        pb, bpb = bank()
                        for i in range(4):
                            ct = cg * 4 + i
                            A("pe", lambda h, pb=pb, i=i, ct=ct, qi=qi: h.matmul(
                                pb[:, i * Tt:(i + 1) * Tt], lhsT=Wbd[:, qi * 16 + ct, :], rhs=xcT[:, ct, :Tt],
                                start=True, stop=True), rd=[bWbd, bxcT], wr=[bpb])
                        A("act", lambda h, pb=pb, cg=cg, dst=dst, scl=scl: h.activation(
                            out=dst[:, cg * 4:(cg + 1) * 4, :Tt], in_=pb[:, :4 * Tt].rearrange("p (a t) -> p a t", a=4),
                            func=AF.Copy, scale=scl), rd=[bpb], wr=[bdst])
                A("dve", lambda h: h.tensor_tensor(out=xcf[:, :, :Tt], in0=xcf[:, :, :Tt],
                                                    in1=cvec[:, SKIP:SKIP + 16].unsqueeze(2).to_broadcast([128, 16, Tt]),
                                                    op=ALU.mult), rd=bxcfc + [bcvec], wr=bxcfc)

        cbpar = [0]

        def scan_chunk(off, L, full, Cview, nview, bCl, bnl, mview, bml, acc_B=False):
            mview = mview[:, :]
            pg, bpg = bank()
            for ct in range(16):
                A("pe", lambda h, pg=pg, ct=ct: h.matmul(pg[:L, 0:8], lhsT=xcT[:, ct, off:off + L], rhs=Wg[:, ct, :],
                                                         start=(ct == 0), stop=False), rd=[bxcT, bWg], wr=[bpg])
            for ct in range(16):
                A("pe", lambda h, pg=pg, ct=ct: h.matmul(pg[:L, 0:8], lhsT=xmT[:, ct, off:off + L], rhs=Wg[:, 16 + ct, :],
                                                         start=False, stop=(ct == 15)), rd=[bxmT, bWg], wr=[bpg])
            A("dve", lambda h, pg=pg: h.tensor_tensor(out=sm[:L, 8:16], in0=pg[:L, 0:8], in1=bgb[:L, :], op=ALU.add),
              rd=[bpg, bbgb], wr=[bsm])
            A("act", lambda h: h.activation(out=sm[:L, 16:20], in_=sm[:L, 12:16], func=AF.Abs), rd=[bsm], wr=[bsm])
            A("act", lambda h: h.activation(out=sm[:L, 20:24], in_=sm[:L, 16:20], func=AF.Exp, scale=-1.0), rd=[bsm], wr=[bsm])
            A("act", lambda h: h.activation(out=sm[:L, 24:28], in_=sm[:L, 20:24], func=AF.Ln, bias=cvec_eps[:L, 2:3]),
              rd=[bsm, bceps], wr=[bsm])
            A("dve", lambda h: h.tensor_single_scalar(out=sm[:L, 28:32], in_=sm[:L, 12:16], scalar=0.0, op=ALU.min),
              rd=[bsm], wr=[bsm])
            A("dve", lambda h: h.tensor_tensor(out=sm[:L, 28:32], in0=sm[:L, 28:32], in1=sm[:L, 24:28], op=ALU.subtract),
              rd=[bsm], wr=[bsm])
            pc, bpc = bank()
            A("pe", lambda h, pc=pc: h.matmul(pc[:L, 0:4], lhsT=maskT[:L, :L], rhs=sm[:L, 28:32], start=True, stop=True),
              rd=[bmaskT, bsm], wr=[bpc])
            A("pe", lambda h, pc=pc: h.matmul(pc[:, 8:12], lhsT=onesf[:L, :], rhs=sm[:L, 28:32], start=True, stop=True),
              rd=[bones, bsm], wr=[bpc])
            A("dve", lambda h, pc=pc: h.tensor_copy(out=sm[:L, 72:76], in_=pc[:L, 0:4]), rd=[bpc], wr=[bsm])
            A("dve", lambda h: h.tensor_tensor(out=sm[:L, 32:36], in0=sm[:L, 8:12], in1=sm[:L, 72:76], op=ALU.subtract),
              rd=[bsm], wr=[bsm])
            A("dve", lambda h: h.tensor_tensor(
                out=diagA[:L, :4 * L].rearrange("p (a t) -> p a t", a=4),
                in0=identf[:L, :L].unsqueeze(1).to_broadcast([L, 4, L]),
                in1=sm[:L, 32:36].unsqueeze(2).to_broadcast([L, 4, L]), op=ALU.mult), rd=[bident, bsm], wr=[bdiagA])
            pa, bpa = bank()
            A("pe", lambda h, pa=pa: h.matmul(pa[:, :4 * L], lhsT=onesf[:L, :], rhs=diagA[:L, :4 * L], start=True, stop=True),
              rd=[bones, bdiagA], wr=[bpa])
            A("dve", lambda h, pa=pa: h.tensor_reduce(out=pmisc[:, 0:4], in_=pa[:, :4 * L].rearrange("p (a t) -> p a t", a=4),
                                                      axis=AX.X, op=ALU.max), rd=[bpa], wr=[bpmisc])
            A("dve", lambda h: h.tensor_tensor(out=pmisc[:, 4:8], in0=pmisc[:, 0:4], in1=mview, op=ALU.max),
              rd=[bpmisc, bml], wr=[bpmisc])
            A("dve", lambda h: h.tensor_tensor(out=pmisc[:, 12:16], in0=mview, in1=pmisc[:, 4:8], op=ALU.subtract),
              rd=[bpmisc, bml], wr=[bpmisc])
            A("act", lambda h: h.activation(out=pmisc[:, 8:12], in_=pmisc[:, 12:16], func=AF.Exp), rd=[bpmisc], wr=[bpmisc])
            A("dve", lambda h, pc=pc: h.tensor_copy(out=pmisc[:, 16:20], in_=pc[:, 8:12]), rd=[bpc], wr=[bpmisc])
            A("dve", lambda h: h.tensor_tensor(out=mview, in0=pmisc[:, 16:20], in1=pmisc[:, 4:8], op=ALU.add),
              rd=[bpmisc], wr=[bml])
            if acc_B:
                A("dve", lambda h: h.tensor_tensor(out=Bacc[:, :], in0=Bacc[:, :], in1=pmisc[:, 16:20], op=ALU.add),
                  rd=[bpmisc, bBacc], wr=[bBacc])
            A("dve", lambda h: h.tensor_tensor(out=sm[:L, 64:68], in0=sm[:L, 32:36], in1=pmisc[:L, 4:8], op=ALU.subtract),
              rd=[bsm, bpmisc], wr=[bsm])
            A("act", lambda h: h.activation(out=sm[:L, 44:48], in_=sm[:L, 64:68], func=AF.Exp), rd=[bsm], wr=[bsm])
            A("dve", lambda h: h.tensor_single_scalar(out=sm[:L, 52:56], in_=sm[:L, 44:48], scalar=KSCALE, op=ALU.mult),
              rd=[bsm], wr=[bsm])
            if full:
                A("dve", lambda h: h.tensor_tensor(out=sm[:L, 64:68], in0=sm[:L, 72:76], in1=pmisc[:L, 4:8], op=ALU.add),
                  rd=[bsm, bpmisc], wr=[bsm])
                A("act", lambda h: h.activation(out=sm[:L, 48:52], in_=sm[:L, 64:68], func=AF.Exp, scale=-1.0), rd=[bsm], wr=[bsm])
            for hh in range(4):
                pk, bpk = bank()
                for i in range(4):
                    ct = hh * 4 + i
                    A("pe", lambda h, pk=pk, i=i, ct=ct: h.matmul(pk[:L, i * 128:(i + 1) * 128], lhsT=xcT[:, ct, off:off + L],
                                                                  rhs=Wbd[:, 16 + ct, :], start=True, stop=True),
                      rd=[bxcT, bWbd], wr=[bpk])
                A("act", lambda h, pk=pk, hh=hh: h.activation(out=kw[:L, hh * 512:(hh + 1) * 512], in_=pk[:L, :], func=AF.Copy,
                                                              scale=sm[:L, 52 + hh:53 + hh]), rd=[bpk, bsm], wr=[bkw])
                pv, bpv = bank()
                for i in range(4):
                    ct = hh * 4 + i
                    A("pe", lambda h, pv=pv, i=i, ct=ct: h.matmul(pv[:L, i * 128:(i + 1) * 128], lhsT=xmT[:, ct, off:off + L],
                                                                  rhs=Wbd[:, 32 + ct, :], start=True, stop=True),
                      rd=[bxmT, bWbd], wr=[bpv])
                A("dve", lambda h, pv=pv, hh=hh: h.tensor_copy(out=vtok[:L, hh * 512:(hh + 1) * 512], in_=pv[:L, :]),
                  rd=[bpv], wr=[bvtok])
            pden, bpden = (None, None)
            if full:
                pden, bpden = bank(reserve=True)
            pn, bpn = bank(reserve=True)
            for hh in range(4):
                c0col = pmisc[:, 8 + hh:9 + hh]
                if full:
                    par = cbpar[0] % 2
                    cbpar[0] += 1
                    CB_, bCB_ = Cb[par], bCb[par]
                    for dt_ in range(4):
                        A("act", lambda h, hh=hh, dt_=dt_, CB_=CB_, c0col=c0col: h.activation(
                            out=CB_[:, dt_, :], in_=Cview(hh, dt_), func=AF.Copy, scale=c0col),
                          rd=[bCl[hh * 4 + dt_], bpmisc], wr=[bCB_])
                    A("dve", lambda h, hh=hh, c0col=c0col: h.tensor_scalar(out=nbb[:, :], in0=nview[:, hh * 4:(hh + 1) * 4],
                                                                           scalar1=c0col, scalar2=None, op0=ALU.mult),
                      rd=[bnl, bpmisc], wr=[bnbb])
                    pst, bpst = bank()
                    for dt_ in range(4):
                        ct = hh * 4 + dt_
                        A("pe", lambda h, pst=pst, ct=ct, dt_=dt_: h.matmul(
                            pst[:L, :L], lhsT=kT[:, ct, off:off + L], rhs=qT[:, ct, off:off + L],
                            start=(dt_ == 0), stop=(dt_ == 3)), rd=[bkT, bqT], wr=[bpst])
                    A("dve", lambda h, pst=pst, hh=hh: h.scalar_tensor_tensor(
                        out=Sw[:L, :L], in0=pst[:L, :L], scalar=sm[:L, 44 + hh:45 + hh], in1=maskT[:L, :L],
                        op0=ALU.mult, op1=ALU.mult), rd=[bpst, bsm, bmaskT], wr=[bSw])
                    pnum, bpnum = bank()
                    A("pe", lambda h, pnum=pnum, hh=hh: h.matmul(pnum[:L, :], lhsT=Sw[:L, :L], rhs=vtok[:L, hh * 512:(hh + 1) * 512],
                                                                 start=True, stop=False), rd=[bSw, bvtok], wr=[bpnum])
                    for dt_ in range(4):
                        ct = hh * 4 + dt_
                        A("pe", lambda h, pnum=pnum, ct=ct, dt_=dt_, CB_=CB_: h.matmul(
                            pnum[:L, :], lhsT=qT[:, ct, off:off + L], rhs=CB_[:, dt_, :], start=False, stop=(dt_ == 3)),
                          rd=[bqT, bCB_], wr=[bpnum])
                    A("pe", lambda h, hh=hh: h.matmul(pden[:L, hh:hh + 1], lhsT=Sw[:L, :L], rhs=onesb[:L, 0:1],
                                                      start=True, stop=False), rd=[bSw, bonesb], wr=[bpden])
                    for dt_ in range(4):
                        ct = hh * 4 + dt_
                        A("pe", lambda h, hh=hh, ct=ct, dt_=dt_: h.matmul(
                            pden[:L, hh:hh + 1], lhsT=qT[:, ct, off:off + L], rhs=nbb[:, dt_:dt_ + 1],
                            start=False, stop=(dt_ == 3)), rd=[bqT, bnbb], wr=[bpden])
                    A("act", lambda h, pnum=pnum, hh=hh: h.activation(out=hn[:L, hh * 512:(hh + 1) * 512], in_=pnum[:L, :],
                                                                      func=AF.Copy), rd=[bpnum], wr=[bhn])
                for dt_ in range(4):
                    ct = hh * 4 + dt_
                    pu_, bpu_ = bank()
                    A("pe", lambda h, pu_=pu_, ct=ct, hh=hh: h.matmul(
                        pu_[:, :], lhsT=kw[:L, ct * 128:(ct + 1) * 128], rhs=vtok[:L, hh * 512:(hh + 1) * 512],
                        start=True, stop=True), rd=[bkw, bvtok], wr=[bpu_])
                    A("dve", lambda h, pu_=pu_, hh=hh, dt_=dt_, c0col=c0col: h.scalar_tensor_tensor(
                        out=Cview(hh, dt_), in0=Cview(hh, dt_), scalar=c0col, in1=pu_[:, :], op0=ALU.mult, op1=ALU.add),
                      rd=[bpu_, bpmisc, bCl[hh * 4 + dt_]], wr=[bCl[hh * 4 + dt_]])
                    A("pe", lambda h, ct=ct: h.matmul(pn[:, ct:ct + 1], lhsT=kw[:L, ct * 128:(ct + 1) * 128], rhs=onesb[:L, 0:1],
                                                      start=True, stop=True), rd=[bkw, bonesb], wr=[bpn])
                A("dve", lambda h, hh=hh, c0col=c0col: h.scalar_tensor_tensor(
                    out=nview[:, hh * 4:(hh + 1) * 4], in0=nview[:, hh * 4:(hh + 1) * 4], scalar=c0col,
                    in1=pn[:, hh * 4:(hh + 1) * 4], op0=ALU.mult, op1=ALU.add), rd=[bpn, bpmisc, bnl], wr=[bnl])
            release(pn)
            if full:
                release(pden)
                A("act", lambda h: h.activation(out=sm[:L, 56:60], in_=pden[:L, 0:4], func=AF.Abs), rd=[bpden], wr=[bsm])
                A("dve", lambda h: h.tensor_tensor(out=sm[:L, 56:60], in0=sm[:L, 56:60], in1=sm[:L, 48:52], op=ALU.max),
                  rd=[bsm], wr=[bsm])
                A("dve", lambda h: h.reciprocal(out=sm[:L, 60:64], in_=sm[:L, 56:60]), rd=[bsm], wr=[bsm])
                for hh in range(4):
                    A("dve", lambda h, hh=hh: h.bn_stats(out=stats[:L, hh, :], in_=hn[:L, hh * 512:(hh + 1) * 512]),
                      rd=[bhn], wr=[bstats])
                    A("dve", lambda h, hh=hh: h.bn_aggr(out=mv[:L, hh, :], in_=stats[:L, hh, :]), rd=[bstats], wr=[bmv])
                A("dve", lambda h: h.tensor_tensor(out=sm[:L, 64:68], in0=sm[:L, 60:64], in1=sm[:L, 60:64], op=ALU.mult),
                  rd=[bsm], wr=[bsm])
                A("dve", lambda h: h.tensor_tensor(out=sm[:L, 64:68], in0=sm[:L, 64:68], in1=mv[:L, :, 1], op=ALU.mult),
                  rd=[bsm, bmv], wr=[bsm])
                A("act", lambda h: h.activation(out=sm[:L, 64:68], in_=sm[:L, 64:68], func=AF.Sqrt, bias=cvec_eps[:L, 1:2]),
                  rd=[bsm, bceps], wr=[bsm])
                A("dve", lambda h: h.reciprocal(out=sm[:L, 68:72], in_=sm[:L, 64:68]), rd=[bsm], wr=[bsm])
                A("dve", lambda h: h.tensor_tensor(out=sm[:L, 68:72], in0=sm[:L, 68:72], in1=sm[:L, 60:64], op=ALU.mult),
                  rd=[bsm], wr=[bsm])
                for hh in range(4):
                    A("dve", lambda h, hh=hh: h.tensor_scalar(
                        out=hn[:L, hh * 512:(hh + 1) * 512], in0=hn[:L, hh * 512:(hh + 1) * 512], scalar1=mv[:L, hh, 0:1],
                        scalar2=sm[:L, 68 + hh:69 + hh], op0=ALU.subtract, op1=ALU.mult), rd=[bhn, bmv, bsm], wr=[bhn])

        def hn_transpose(banks, off, L, Tt):
            for cg in range(4):
                pb, bpb = banks[cg]
                for i in range(4):
                    ct = cg * 4 + i
                    A("pe", lambda h, pb=pb, i=i, ct=ct: h.matmul(pb[:, i * Tt + off:i * Tt + off + L],
                                                                  lhsT=hn[:L, ct * 128:(ct + 1) * 128], rhs=identf[:L, :L],
                                                                  start=True, stop=True), rd=[bhn, bident], wr=[bpb])

        def l1_tail(banks, Tt, par, out_ap, key):
            X, bX = xt[par], bxt[par]
            xsf, bxsf = xt[1 - par], bxt[1 - par]
            outT, tmpZ, tmpS = outT_2, tmpZ_2, tmpS_2
            n4 = 4 * Tt
            for cg in range(4):
                pz, bpz = proj_cm(Win1, bWin1z, E, cg, Tt)
                A("act", lambda h, pz=pz: h.activation(out=tmpZ[:, :n4], in_=pz[:, :n4], func=AF.Silu), rd=[bpz], wr=[btmpZ])
                pb, bpb = banks[cg]
                for i in range(4):
                    ct = cg * 4 + i
                    A("dve", lambda h, pb=pb, i=i, ct=ct: h.scalar_tensor_tensor(
                        out=tmpS[:, i * Tt:(i + 1) * Tt], in0=pb[:, i * Tt:(i + 1) * Tt], scalar=cvec[:, HG + ct:HG + ct + 1],
                        in1=xcf[:, ct, :Tt], op0=ALU.mult, op1=ALU.add), rd=[bpb, bcvec, bxcfc[ct]], wr=[btmpS])
                A("dve", lambda h, cg=cg: h.tensor_tensor(
                    out=outT[:, cg * 4:(cg + 1) * 4, :Tt], in0=tmpS[:, :n4].rearrange("p (a t) -> p a t", a=4),
                    in1=tmpZ[:, :n4].rearrange("p (a t) -> p a t", a=4), op=ALU.mult), rd=[btmpS, btmpZ], wr=[boutT])
            out_proj(Wout1, bWout1, Tt, par, outT)
            A("act", lambda h: h.activation(out=JUNK[:Tt, :], in_=X[:Tt, :], func=AF.Square, accum_out=sm[:Tt, 86:87]),
              rd=[bX], wr=[btmpZ, btmpS, bsm])
            A("act", lambda h: h.activation(out=sm[:Tt, 87:88], in_=sm[:Tt, 86:87], func=AF.Sqrt, scale=1.0 / DM, bias=cvec_eps[:Tt, 0:1]),
              rd=[bsm, bceps], wr=[bsm])
            A("dve", lambda h: h.reciprocal(out=sm[:Tt, 88:89], in_=sm[:Tt, 87:88]), rd=[bsm], wr=[bsm])
            A("dve", lambda h: h.scalar_tensor_tensor(out=X[:Tt, :], in0=X[:Tt, :], scalar=sm[:Tt, 88:89], in1=FGb[:Tt, :],
                                                      op0=ALU.mult, op1=ALU.mult), rd=[bX, bsm, bFGb], wr=[bX])
            A("sp", lambda h: h.dma_start(out=out_ap, in_=X[:Tt, :]), rd=[bX], dma=1, key=key)

        def conv_tail_out(nstr, nrows, out_ap, key):
            for q in range(4):
                pb, bpb = bank()
                for i in range(4):
                    ct = q * 4 + i
                    A("pe", lambda h, pb=pb, i=i, ct=ct: h.matmul(pb[:nrows, i * 128:(i + 1) * 128], lhsT=halo[:, ct, 0:nrows],
                                                                  rhs=identf[:, :], start=True, stop=True),
                      rd=[bhalo, bident], wr=[bpb])
                A("dve", lambda h, pb=pb, q=q: h.tensor_copy(out=crow[:nrows, q * 512:(q + 1) * 512], in_=pb[:nrows, :]),
                  rd=[bpb], wr=[bcrow])
            A("sp", lambda h: h.dma_start(out=out_ap, in_=crow[:nrows, :]), rd=[bcrow], dma=1, key=key)

        layer0_tile(xs[:, :], (NT + 1) * 128, 64, 0, True, pre=False, nxt=(xp[0:128, :], 128))
        for i in range(NT + 1):
            nx = (xp[(i + 1) * 128:(i + 2) * 128, :], 128) if i < NT else None
            layer0_tile(xp[i * 128:(i + 1) * 128, :], i * 128, 128, (i + 1) % 2, False, pre=True, nxt=nx)

        def zero_state(extra=()):
            for hh in range(4):
                A("pool", lambda h, hh=hh: h.memset(Cst[:, hh * 4:(hh + 1) * 4, :], 0.0), wr=bC[hh * 4:(hh + 1) * 4] + list(extra))
            A("pool", lambda h: h.memset(nst[:], 0.0), wr=[bnst])
            A("pool", lambda h: h.memset(mst[:], 0.0), wr=[bmst])

        def Cv(hh, dt_):
            return Cst[:, hh * 4 + dt_, :]

        bsin = P.buf("summ_in")
        bsout = P.buf("summ_out")
        SROW = (NT + 1) * 128

        def phase_w1():
            W0ALL = [bWin0, bWin0u, bWin0z, bWout0]
            f_ = ld_w(Win1, b_w_in, 8, (0, E)); A("pool", f_, wr=[bWin1] + W0ALL, dma=f_.n, key="Win1")
            A("pool", lambda h: [h.dma_start(out=Wbd_t[:, i * 2048:(i + 1) * 2048], in_=wbd[:, i * 2048:(i + 1) * 2048]) for i in range(3)], wr=[bWbd, bvhat, btmpZ, btmpS], dma=3, key="Wbd")

            zero_state(extra=[bv_f32, bbiasT, bbiasTs, btmpU, btmpZ, btmpS, bWout0, bxcT, bxmT, bqT, bkT, bkw, bvtok, bCb[0], bCb[1]])
            A("pool", lambda h: h.memset(Bacc[:], 0.0), wr=[bBacc])
            A("sp", lambda h: h.dma_start(out=FGb[:, :], in_=fng.partition_broadcast(128)),
              wr=[bFGb, bWmT, bWmTs, bwtmp, bwtmp2], dma=1, key="FGb")


        def phase_A():
            import os
            KA = int(os.environ.get("KA", str(NT)))
            KSCAN = int(os.environ.get("KSCAN", "1"))
            l1_front(0, 128, 0, 1, 128, False, "zero", pre=False, nxt=128, halo_only=True)
            for i in range(KA):
                l1_front((i + 1) * 128, 128, (i + 1) % 2, 1, 128, False, "flag" if i == 0 else "copy",
                         pre=True, nxt=((i + 2) * 128 if i < KA - 1 else None))
                if KSCAN:
                    scan_chunk(0, 128, False, Cv, nst, bC, bnst, mst, bmst, acc_B=True)
                if i == 0:
                    W0ALL = [bWin0, bWin0u, bWin0z, bWout0]
                    f_ = ld_w(Win1, b_w_in, 8, (E, 2 * E)); A("pool", f_, wr=[bWin1z] + W0ALL, dma=f_.n, key="Win1z")
                    f_ = ld_w(Wout1, b_w_out, 4); A("pool", f_, wr=[bWout1] + W0ALL, dma=f_.n, key="Wout1")

        def phase_X():
            A("dve", lambda h: h.memset(misc[:], 0.0), wr=[bmisc])
            A("dve", lambda h: h.tensor_copy(out=misc[:, 0:16], in_=nst[:, :]), rd=[bnst], wr=[bmisc])
            A("dve", lambda h: h.tensor_copy(out=misc[:, 16:20], in_=mst[:, :]), rd=[bmst], wr=[bmisc])
            A("dve", lambda h: h.tensor_copy(out=misc[:, 20:24], in_=Bacc[:, :]), rd=[bBacc], wr=[bmisc])
            A("sp", lambda h: [h.dma_start(out=summ_in[hh][:, :].rearrange("(a p) e -> p a e", p=128), in_=Cst[:, hh * 4:(hh + 1) * 4, :])
                               for hh in range(4)], rd=bC, wr=[bsin], dma=4, key="summC")
            A("sp", lambda h: h.dma_start(out=summ_in_m[:, :], in_=misc[:, :]), rd=[bmisc], wr=[bsin], dma=1, key="summM")
            RG = [[0, 1, 2, 3], [4, 5, 6, 7]]
            for hh in range(4):
                A("pool", lambda h, hh=hh: h.collective_compute("AllGather", ALU.bypass, replica_groups=RG,
                                                                ins=[summ_in[hh].ap().opt()], outs=[summ_out[hh].ap().opt()]),
                  rd=[bsin], wr=[bsout], dma="cc", key="cc")
            A("pool", lambda h: h.collective_compute("AllGather", ALU.bypass, replica_groups=RG,
                                                     ins=[summ_in_m.ap().opt()], outs=[summ_out_m.ap().opt()]),
              rd=[bsin], wr=[bsout], dma="cc", key="cc")

        def phase_S():
            l1_front(SROW, 64, 0, 4, 16, True, "sample", conv_out=(convs[:, :], "convs"))
            sbanks = [bank(reserve=True) for _ in range(4)]
            for j in range(4):
                for hh in range(4):
                    A("sp", lambda h, j=j, hh=hh: h.dma_start(
                        out=Cst[:, hh * 4:(hh + 1) * 4, :],
                        in_=stC[j * 2048 + hh * 512:j * 2048 + (hh + 1) * 512, :].rearrange("(a p) e -> p a e", p=128)),
                      wr=bC[hh * 4:(hh + 1) * 4], dma=1, key=f"Cld{hh}")
                A("sp", lambda h, j=j: h.dma_start(out=nst[:, :], in_=stn[j, :, :]), wr=[bnst], dma=1, key="nld")
                A("sp", lambda h, j=j: h.dma_start(out=mst[:, :], in_=stm[j:j + 1, :].partition_broadcast(128)),
                  wr=[bmst], dma=1, key="mld")
                scan_chunk(16 * j, 16, True, Cv, nst, bC, bnst, mst, bmst)
                hn_transpose(sbanks, 16 * j, 16, 64)
                for hh in range(4):
                    A("sp", lambda h, j=j, hh=hh: h.dma_start(
                        out=Cs[j * 2048 + hh * 512:j * 2048 + (hh + 1) * 512, :].rearrange("(a p) e -> p a e", p=128),
                        in_=Cst[:, hh * 4:(hh + 1) * 4, :]), rd=bC[hh * 4:(hh + 1) * 4], dma=1, key=f"Cst_out{hh}")
                A("sp", lambda h, j=j: h.dma_start(out=ns_o[j, :, :], in_=nst[:, :]), rd=[bnst], dma=1, key="nst_out")
                A("sp", lambda h, j=j: h.dma_start(out=ms_o[j:j + 1, :], in_=mst[0:1, :]), rd=[bmst], dma=1, key="mst_out")
            l1_tail(sbanks, 64, 0, ys[:, :], "ys")
            for pb, _ in sbanks:
                release(pb)

        def phase_C():
            zero_state()
            for r in range(3):
                A("sp", lambda h, r=r: h.dma_start(out=misc[:, :], in_=summ_out_m[r * 128:(r + 1) * 128, :]),
                  rd=[bsout], wr=[bmisc], dma=1, key="miscld")
                pm = flg[:, 1 + r:2 + r]
                A("dve", lambda h, pm=pm: h.tensor_scalar(out=pmisc[:, 20:24], in0=misc[:, 20:24], scalar1=pm, scalar2=None, op0=ALU.mult),
                  rd=[bmisc, bflg], wr=[bpmisc])
                A("dve", lambda h, pm=pm: h.tensor_scalar(out=pmisc[:, 28:29], in0=pm, scalar1=-1.0, scalar2=1e30, op0=ALU.add, op1=ALU.mult),
                  rd=[bflg], wr=[bpmisc])
                A("dve", lambda h, pm=pm: h.tensor_scalar(out=pmisc[:, 24:28], in0=misc[:, 16:20], scalar1=pm, scalar2=pmisc[:, 28:29],
                                                          op0=ALU.mult, op1=ALU.add), rd=[bmisc, bflg, bpmisc], wr=[bpmisc])
                A("dve", lambda h: h.tensor_tensor(out=pmisc[:, 44:48], in0=pmisc[:, 20:24], in1=mst[:, :], op=ALU.add),
                  rd=[bpmisc, bmst], wr=[bpmisc])
                A("dve", lambda h: h.tensor_tensor(out=pmisc[:, 32:36], in0=pmisc[:, 44:48], in1=pmisc[:, 24:28], op=ALU.max),
                  rd=[bpmisc], wr=[bpmisc])
                A("dve", lambda h: h.tensor_tensor(out=pmisc[:, 44:48], in0=pmisc[:, 44:48], in1=pmisc[:, 32:36], op=ALU.subtract),
                  rd=[bpmisc], wr=[bpmisc])
                A("act", lambda h: h.activation(out=pmisc[:, 36:40], in_=pmisc[:, 44:48], func=AF.Exp), rd=[bpmisc], wr=[bpmisc])
                A("dve", lambda h: h.tensor_tensor(out=pmisc[:, 44:48], in0=pmisc[:, 24:28], in1=pmisc[:, 32:36], op=ALU.subtract),
                  rd=[bpmisc], wr=[bpmisc])
                A("act", lambda h: h.activation(out=pmisc[:, 40:44], in_=pmisc[:, 44:48], func=AF.Exp), rd=[bpmisc], wr=[bpmisc])
                A("dve", lambda h: h.tensor_copy(out=mst[:, :], in_=pmisc[:, 32:36]), rd=[bpmisc], wr=[bmst])
                for hh in range(4):
                    A("sp", lambda h, r=r, hh=hh: h.dma_start(
                        out=hn[:, :].rearrange("p (a e) -> p a e", a=4),
                        in_=summ_out[hh][r * 512:(r + 1) * 512, :].rearrange("(a p) e -> p a e", p=128)),
                      rd=[bsout], wr=[bhn], dma=1, key="Crld")
                    for dt_ in range(4):
                        A("act", lambda h, hh=hh, dt_=dt_: h.activation(out=Cv(hh, dt_), in_=Cv(hh, dt_), func=AF.Copy,
                                                                        scale=pmisc[:, 36 + hh:37 + hh]), rd=[bpmisc, bC[hh * 4 + dt_]], wr=[bC[hh * 4 + dt_]])
                        A("dve", lambda h, hh=hh, dt_=dt_: h.scalar_tensor_tensor(
                            out=Cv(hh, dt_), in0=hn[:, dt_ * 512:(dt_ + 1) * 512], scalar=pmisc[:, 40 + hh:41 + hh], in1=Cv(hh, dt_),
                            op0=ALU.mult, op1=ALU.add), rd=[bhn, bpmisc, bC[hh * 4 + dt_]], wr=[bC[hh * 4 + dt_]])
                    A("dve", lambda h, hh=hh: h.tensor_scalar(out=nst[:, hh * 4:(hh + 1) * 4], in0=nst[:, hh * 4:(hh + 1) * 4],
                                                              scalar1=pmisc[:, 36 + hh:37 + hh], scalar2=None, op0=ALU.mult),
                      rd=[bpmisc, bnst], wr=[bnst])
                    A("dve", lambda h, hh=hh: h.scalar_tensor_tensor(
                        out=nst[:, hh * 4:(hh + 1) * 4], in0=misc[:, hh * 4:(hh + 1) * 4], scalar=pmisc[:, 40 + hh:41 + hh],
                        in1=nst[:, hh * 4:(hh + 1) * 4], op0=ALU.mult, op1=ALU.add), rd=[bmisc, bpmisc, bnst], wr=[bnst])

        def phase_B():
            A("act", lambda h: h.activation(out=halo[:, :, 0:3], in_=halo0[:, :, :], func=AF.Copy), rd=[bhalo0], wr=[bhalo])
            for i in range(NT):
                pr = i % 2
                l1_front((i + 1) * 128, 128, pr, 1, 128, True, "flag" if i == 0 else "copy",
                         conv_out=(convp[:, :], "convp") if i == NT - 1 else None,
                         pre=("B0" if i == 0 else "B1"), nxt=((i + 2) * 128 if i < NT - 1 else None))
                scan_chunk(0, 128, True, Cv, nst, bC, bnst, mst, bmst)
                banks = [bank(reserve=True) for _ in range(4)]
                hn_transpose(banks, 0, 128, 128)
                l1_tail(banks, 128, pr, yp[i * 128:(i + 1) * 128, :], "yp")
                for pb, _ in banks:
                    release(pb)
            A("sp", lambda h: h.dma_start(out=Cp[:, :].rearrange("(a p) e -> p a e", p=128), in_=Cst[:, :, :]), rd=bC, dma=1, key="Cp")
            A("sp", lambda h: h.dma_start(out=np_o[:, :], in_=nst[:, :]), rd=[bnst], dma=1, key="np")
            A("sp", lambda h: h.dma_start(out=mp_o[:, :], in_=mst[0:1, :]), rd=[bmst], dma=1, key="mp")

        import os
        KSTOP = int(os.environ.get('KSTOP', '9'))
        for _k, _f in enumerate([phase_w1, phase_A, phase_X, phase_S, phase_C, phase_B]):
            if KSTOP > _k:
                _f()
        P.emit(nc, st)
    return nc


_NC_CACHE = {}


def kernel(**inputs):
    f = lambda k: np.ascontiguousarray(np.asarray(inputs[k], dtype=np.float32))
    x_prompt, x_sample = f("x_prompt"), f("x_sample")
    stC_, stn_, stm_, stcv_ = f("state_mlstm_C"), f("state_mlstm_n"), f("state_mlstm_m"), f("state_mlstm_conv")

    def cols(v, n):
        return np.ascontiguousarray(v.reshape(n, 128).T)

    ngc = np.concatenate([cols(f("norm_g")[0], 8), cols(f("norm_g")[1], 8)], axis=1)
    cw = f("b_conv_w")[0]
    b_cw = np.ascontiguousarray(cw.reshape(4, 16, 128).transpose(2, 1, 0).reshape(128, 64))

    def bdiag(w):
        out = np.zeros((16, 128, 128), np.float32)
        wr = w.reshape(16, 32, 4, 4)
        for n in range(32):
            out[:, 4 * n:4 * n + 4, 4 * n:4 * n + 4] = wr[:, n]
        return out
    bds = [bdiag(f(k)[0]) for k in ("b_wq", "b_wk", "b_wv")]
    wbd = np.ascontiguousarray(np.concatenate(bds, 0).transpose(1, 0, 2).reshape(128, 48 * 128))
    wbdT = np.ascontiguousarray(np.concatenate(bds, 0).transpose(2, 0, 1).reshape(128, 48 * 128))
    wg = f("b_w_gates")[0]
    b_wg = np.ascontiguousarray(wg.reshape(48, 128, 8).transpose(1, 0, 2).reshape(128, 48 * 8))
    shared = {
        "ngc": ngc, "fng": f("final_norm_g").reshape(1, DM),
        "a_w_in": f("a_w_in")[0], "a_lng": cols(f("a_ln_g")[0], 16), "a_lnb": cols(f("a_ln_b")[0], 16),
        "a_lng_row": f("a_ln_g")[0].reshape(1, E), "a_lnb_row": f("a_ln_b")[0].reshape(1, E),
        "a_ws": f("a_w_s")[0].reshape(8 * 128, 128), "a_bs": f("a_b_s")[0].reshape(1, 8 * 128),
        "a_w_out": f("a_w_out")[0], "b_w_in": f("b_w_in")[0], "b_cw": b_cw, "b_cb": cols(f("b_conv_b")[0], 16),
        "wbd": wbd, "wbdT": wbdT, "b_wg": b_wg, "b_bg": f("b_b_gates")[0].reshape(1, 8),
        "b_hg": cols(f("b_hnorm_g")[0], 16), "b_skip": cols(f("b_skip")[0], 16), "b_w_out": f("b_w_out")[0],
    }
    in_maps = []
    for c in range(8):
        b, g = c // 4, c % 4
        xpc = np.zeros(((NT + 1) * 128, DM), np.float32)
        lo = g * 2048 - 128
        if g == 0:
            xpc[128:] = x_prompt[b, 0:2048]
        else:
            xpc[:] = x_prompt[b, lo:lo + (NT + 1) * 128]
        fl = np.zeros((1, 8), np.float32)
        fl[0, 0] = 0.0 if g == 0 else 1.0
        for r in range(3):
            fl[0, 1 + r] = 1.0 if r < g else 0.0
        sl = slice(4 * c, 4 * c + 4)
        m = dict(shared)
        m.update({
            "xp": xpc, "xs": np.ascontiguousarray(x_sample[sl].reshape(64, DM)),
            "stC": np.ascontiguousarray(stC_[0, sl].reshape(16 * 512, 512)),
            "stn": np.ascontiguousarray(stn_[0, sl].reshape(4, 4, 4, 128).transpose(0, 3, 1, 2).reshape(4, 128, 16)),
            "stm": np.ascontiguousarray(stm_[0, sl].reshape(4, 4)),
            "stconv": np.ascontiguousarray(stcv_[0, sl].reshape(12, E)),
            "flags": fl,
        })
        in_maps.append(m)
    if "nc" not in _NC_CACHE:
        _NC_CACHE["nc"] = build_program()
    res = run_bass_kernel_spmd(_NC_CACHE["nc"], in_maps, core_ids=list(range(8)))
    R = res.results
    y_prompt = np.stack([np.concatenate([R[b * 4 + g]["yp"] for g in range(4)], 0) for b in range(2)]).astype(np.float32)
    y_sample = np.concatenate([R[c]["ys"].reshape(4, 16, DM) for c in range(8)], 0).astype(np.float32)
    sgu_v = np.concatenate([R[c]["sguv"].reshape(4, 16, E) for c in range(8)], 0)[None].astype(np.float32)

    def n_from(a):
        return a.reshape(128, 4, 4).transpose(1, 2, 0).reshape(4, 512)
    C_prompt = np.stack([R[b * 4 + 3]["Cp"].reshape(4, 512, 512) for b in range(2)])[None].astype(np.float32)
    n_prompt = np.stack([n_from(R[b * 4 + 3]["np_o"]) for b in range(2)])[None].astype(np.float32)
    m_prompt = np.stack([R[b * 4 + 3]["mp_o"].reshape(4) for b in range(2)])[None].astype(np.float32)
    conv_prompt = np.stack([R[b * 4 + 3]["convp"].reshape(3, E) for b in range(2)])[None].astype(np.float32)
    C_sample = np.concatenate([R[c]["Cs"].reshape(4, 4, 512, 512) for c in range(8)], 0)[None].astype(np.float32)
    n_sample = np.concatenate([np.stack([n_from(R[c]["ns_o"][j]) for j in range(4)]) for c in range(8)], 0)[None].astype(np.float32)
    m_sample = np.concatenate([R[c]["ms_o"].reshape(4, 4) for c in range(8)], 0)[None].astype(np.float32)
    conv_sample = np.concatenate([R[c]["convs"].reshape(4, 3, E) for c in range(8)], 0)[None].astype(np.float32)
    return (y_prompt, y_sample, sgu_v, C_prompt, n_prompt, m_prompt, conv_prompt,
            C_sample, n_sample, m_sample, conv_sample)
```

```python
import numpy as np
import concourse.bass as bass
import concourse.mybir as mybir
from concourse.bass_utils import run_bass_kernel_spmd
from contextlib import ExitStack

F32 = mybir.dt.float32
BF16 = mybir.dt.bfloat16
AF = mybir.ActivationFunctionType
ALU = mybir.AluOpType
AX = mybir.AxisListType

NT = 16
DM = 1024
E = 2048
H = 4
DH = 512
RMS_EPS = 1e-6
LN_EPS = 1e-5
KSCALE = float(DH ** -0.5)


class Buf:
    __slots__ = ("name", "lw", "rd")

    def __init__(self, name):
        self.name = name
        self.lw = None
        self.rd = []


class Op:
    __slots__ = ("eng", "fn", "deps", "dma", "key", "done", "need_inc", "tag", "late")


class Prog:
    ENG = ["pe", "act", "dve", "pool", "sp"]

    def __init__(self, same_engine_sync=False):
        self.ops = {e: [] for e in self.ENG}
        self.same_engine_sync = same_engine_sync
        self.late_bufs = set()
        import os
        self.sync_engs = set(os.environ.get("KSYNC", "dve,act,pool").split(","))
        self.nbuf = 0
        import os
        self.limit = int(os.environ.get("KLIMIT", "100000000"))

    def buf(self, name=None):
        self.nbuf += 1
        return Buf(name or f"b{self.nbuf}")

    def add(self, eng, fn, rd=(), wr=(), dma=0, key=None, tag=None, cc=False):
        self.nadd = getattr(self, "nadd", 0) + 1
        if self.nadd > self.limit:
            return None
        if self.nadd == self.limit:
            import inspect
            fr = inspect.stack()[1]
            print("LAST OP", eng, fr.lineno, flush=True)
        op = Op()
        op.eng = eng
        op.fn = fn
        op.dma = dma
        op.key = key
        op.done = None
        op.need_inc = bool(dma)
        op.tag = tag
        op.late = any(id(b) in self.late_bufs for b in wr)
        deps = []
        for b in rd:
            if b.lw is not None:
                deps.append(b.lw)
        for b in wr:
            if b.lw is not None:
                deps.append(b.lw)
            deps.extend(b.rd)
        seen = set()
        dd = []
        for d in deps:
            if id(d) in seen or d is op:
                continue
            seen.add(id(d))
            if (not d.dma) and d.eng == eng and not dma:
                if eng == "pe" or not (self.same_engine_sync or d.late or eng in self.sync_engs):
                    continue
            dd.append(d)
        op.deps = dd
        for d in dd:
            d.need_inc = True
        for b in rd:
            b.rd.append(op)
        for b in wr:
            b.lw = op
            b.rd = []
        self.ops[eng].append(op)
        return op

    def emit(self, nc, stack):
        esem = {e: stack.enter_context(nc.semaphore(f"s_{e}")) for e in self.ENG}
        dsem = {}
        ecount = {e: 0 for e in self.ENG}
        dcount = {}
        for e in self.ENG:
            for op in self.ops[e]:
                if op.dma:
                    k = op.key
                    if k not in dsem:
                        dsem[k] = stack.enter_context(nc.semaphore(f"d_{len(dsem)}"))
                        dcount[k] = 0
        for e in self.ENG:
            for op in reversed(self.ops[e]):
                if not op.dma:
                    op.need_inc = True
                    break
        for e in self.ENG:
            for op in self.ops[e]:
                if op.dma == "cc":
                    dcount[op.key] += 1
                    op.done = (dsem[op.key], dcount[op.key])
                elif op.dma:
                    dcount[op.key] += 16 * int(op.dma)
                    op.done = (dsem[op.key], dcount[op.key])
                elif op.need_inc:
                    ecount[e] += 1
                    op.done = (esem[e], ecount[e])
        self.final = [(dsem[k], dcount[k]) for k in dsem] + [
            (esem[e], ecount[e]) for e in self.ENG if ecount[e] > 0]
        prog = self

        def run_engine(ename, h, extra_final=False):
            seen = {}
            for op in prog.ops[ename]:
                waits = {}
                for d in op.deps:
                    s, v = d.done
                    key = id(s)
                    if key not in waits or waits[key][1] < v:
                        waits[key] = (s, v)
                for key, (s, v) in waits.items():
                    if seen.get(key, 0) >= v:
                        continue
                    h.wait_ge(s, v)
                    seen[key] = v
                res = op.fn(h)
                if op.dma == "cc":
                    res.then_inc(op.done[0], 1)
                elif op.dma:
                    if not isinstance(res, (list, tuple)):
                        res = [res]
                    assert len(res) == int(op.dma), (op.tag, len(res), op.dma)
                    for r in res:
                        r.then_inc(op.done[0], 16)
                elif op.need_inc:
                    res.then_inc(op.done[0], 1)
            if extra_final:
                for s, v in prog.final:
                    if seen.get(id(s), 0) >= v:
                        continue
                    h.wait_ge(s, v)

        print("TOTAL OPS", getattr(self, "nadd", 0), flush=True)
        with nc.Block() as block:
            @block.tensor
            def _(h):
                run_engine("pe", h)

            @block.scalar
            def _(h):
                run_engine("act", h)

            @block.vector
            def _(h):
                run_engine("dve", h)

            @block.gpsimd
            def _(h):
                run_engine("pool", h)

            @block.sync
            def _(h):
                run_engine("sp", h, extra_final=True)


def build_program():
    nc = bass.Bass("TRN2", target_bir_lowering=False)
    P = Prog(same_engine_sync=False)
    st = ExitStack()

    def din(name, shape):
        return nc.dram_tensor(name, list(shape), F32, kind="ExternalInput").ap()

    def dout(name, shape):
        return nc.dram_tensor(name, list(shape), F32, kind="ExternalOutput").ap()

    xp = din("xp", [(NT + 1) * 128, DM])
    xs = din("xs", [64, DM])
    stC = din("stC", [16 * 512, 512])
    stn = din("stn", [4, 128, 16])
    stm = din("stm", [4, 4])
    stconv = din("stconv", [12, E])
    flags = din("flags", [1, 8])
    ngc = din("ngc", [128, 16])
    fng = din("fng", [1, DM])
    a_w_in = din("a_w_in", [DM, 3 * E])
    a_lng = din("a_lng", [128, 16])
    a_lnb = din("a_lnb", [128, 16])
    a_lng_row = din("a_lng_row", [1, E])
    a_lnb_row = din("a_lnb_row", [1, E])
    a_ws = din("a_ws", [8 * 128, 128])
    a_bs = din("a_bs", [1, 8 * 128])
    a_w_out = din("a_w_out", [E, DM])
    b_w_in = din("b_w_in", [DM, 2 * E])
    b_cw = din("b_cw", [128, 64])
    b_cb = din("b_cb", [128, 16])
    wbd = din("wbd", [128, 48 * 128])
    wbdT = din("wbdT", [128, 48 * 128])
    b_wg = din("b_wg", [128, 48 * 8])
    b_bg = din("b_bg", [1, 8])
    b_hg = din("b_hg", [128, 16])
    b_skip = din("b_skip", [128, 16])
    b_w_out = din("b_w_out", [E, DM])

    yp = dout("yp", [NT * 128, DM])
    ys = dout("ys", [64, DM])
    sguv = dout("sguv", [64, E])
    Cp = dout("Cp", [16 * 128, 512])
    np_o = dout("np_o", [128, 16])
    mp_o = dout("mp_o", [1, 4])
    convp = dout("convp", [3, E])
    Cs = dout("Cs", [4 * 16 * 128, 512])
    ns_o = dout("ns_o", [4, 128, 16])
    ms_o = dout("ms_o", [4, 4])
    convs = dout("convs", [12, E])

    x1s = nc.dram_tensor("x1s", [(NT + 1) * 128 + 64, DM], F32).ap()
    summ_in = [nc.dram_tensor(f"summ_in{i}", [512, 512], F32) for i in range(4)]
    summ_out = [nc.dram_tensor(f"summ_out{i}", [4 * 512, 512], F32) for i in range(4)]
    summ_in_m = nc.dram_tensor("summ_in_m", [128, 512], F32)
    summ_out_m = nc.dram_tensor("summ_out_m", [4 * 128, 512], F32)

    with st:
        def sb(name, shape, dt=F32):
            return st.enter_context(nc.sbuf_tensor(name, list(shape), dt))

        BIGW = sb("BIGW", [128, 65536], BF16)
        F32A = sb("F32A", [128, 8192], F32)
        Win0 = BIGW[:, 0:49152].rearrange("p (a c) -> p a c", a=8)
        Wout0 = BIGW[:, 49152:65536].rearrange("p (a c) -> p a c", a=16)
        Win1 = BIGW[:, 0:32768].rearrange("p (a c) -> p a c", a=8)
        Wout1 = BIGW[:, 32768:49152].rearrange("p (a c) -> p a c", a=16)
        SP = 49152
        xcT = BIGW[:, SP:SP + 2048].rearrange("p (a c) -> p a c", a=16)
        xmT = BIGW[:, SP + 2048:SP + 4096].rearrange("p (a c) -> p a c", a=16)
        qT = BIGW[:, SP + 4096:SP + 6144].rearrange("p (a c) -> p a c", a=16)
        kT = BIGW[:, SP + 6144:SP + 8192].rearrange("p (a c) -> p a c", a=16)
        kw = BIGW[:, SP + 8192:SP + 10240]
        vtok = BIGW[:, SP + 10240:SP + 12288]
        Cb0 = BIGW[:, SP + 12288:SP + 14336].rearrange("p (a c) -> p a c", a=4)
        Cb = [Cb0, Cb0]
        JUNK = BIGW[:, SP + 14336:SP + 16384].bitcast(F32)
        tmpZ_2 = BIGW[:, SP + 14336:SP + 15360].bitcast(F32)
        tmpS_2 = BIGW[:, SP + 15360:SP + 16384].bitcast(F32)
        bWin0, bWout0, bWin1, bWout1 = P.buf("Win0"), P.buf("Wout0"), P.buf("Win1"), P.buf("Wout1")
        bWin0u, bWin0z, bWin1z = P.buf("Win0u"), P.buf("Win0z"), P.buf("Win1z")
        bxcT, bxmT, bqT, bkT, bkw, bvtok = [P.buf(n) for n in ("xcT", "xmT", "qT", "kT", "kw", "vtok")]
        bxcTs = [P.buf(f"xcT{i}") for i in range(4)]
        bxmTs = [P.buf(f"xmT{i}") for i in range(4)]
        bCb0 = P.buf("Cb0"); bCb = [bCb0, bCb0]
        v_f32 = F32A[:, 0:2048]
        biasT = F32A[:, 2048:4096].rearrange("p (a c) -> p a c", a=16)
        biasTs = F32A[:, 4096:5120].rearrange("p (a c) -> p a c", a=16)
        tmpU = F32A[:, 5120:5632]
        RSb = F32A[:, 6656:7680].rearrange("p (a c) -> p a c", a=8)
        Cst = F32A[:, 0:8192].rearrange("p (a c) -> p a c", a=16)
        bv_f32, bbiasT, bbiasTs, btmpU = P.buf("v_f32"), P.buf("biasT"), P.buf("biasTs"), P.buf("tmpU")
        bC = [P.buf(f"C{h}") for h in range(16)]

        Wbd_t = sb("Wbd", [128, 48 * 128], BF16); bWbd = P.buf("Wbd")
        Wbd = Wbd_t[:, :].rearrange("p (a c) -> p a c", a=48)
        vhat = Wbd_t[:, 0:E]; bvhat = P.buf("vhat")
        xt0 = sb("xt0", [128, DM]); bxt0 = P.buf("xt0")
        xsf = sb("xsf", [128, DM]); bxsf = P.buf("xsf")
        xt = [xt0, xsf]; bxt = [bxt0, bxsf]
        xnT = sb("xnT", [128, 8, 128], BF16); bxnT = P.buf("xnT"); bxnTs = [P.buf(f"xnT{i}") for i in range(8)]
        outT_1 = F32A[:, 5632:6656].bitcast(BF16).rearrange("p (a c) -> p a c", a=16)
        outT_2 = kw.rearrange("p (a c) -> p a c", a=16)
        boutT = bkw
        XH = sb("XH", [128, 16 * 132]); bxpad = P.buf("xpad"); bhn = bxpad
        xpad = XH[:, :].rearrange("p (a c) -> p a c", a=16)
        hn = XH[:, 0:E]
        halo = sb("halo", [128, 16, 12]); bhalo = P.buf("halo")
        halo0 = sb("halo0", [128, 16, 3]); bhalo0 = P.buf("halo0")
        xcf_t = sb("xcf", [128, 16 * 128]); bxcf = P.buf("xcf")
        bxcfc = [P.buf(f"xcf{i}") for i in range(16)]
        ALLX = [bxcf] + bxcfc
        dummy = sb("fence_t", [128, 2]); bdummy = P.buf("dummy")
        xcf = xcf_t[:, :].rearrange("p (a c) -> p a c", a=16)
        crow = xcf_t; bcrow = bxcf
        tmpZ_1 = Wbd_t[:, 2048:3072].bitcast(F32)
        tmpS_1 = Wbd_t[:, 3072:4096].bitcast(F32)
        btmpZ = P.buf("tmpZ"); btmpS = P.buf("tmpS")
        OV2 = sb("OV2", [128, DM]); bOV2 = P.buf("OV2")
        FGb = OV2; bFGb = bOV2
        identf = sb("identf", [128, 128]); bident = P.buf("ident")
        maskT = sb("maskT", [128, 128]); bmaskT = P.buf("maskT")
        onesf = sb("onesf", [128, 128]); bones = P.buf("ones")
        onesb = sb("onesb", [128, 4], BF16); bonesb = P.buf("onesb")
        WmT = OV2[:, 0:512].bitcast(BF16).rearrange("p (a c) -> p a c", a=8); bWmT = P.buf("WmT")
        WmTs = OV2[:, 512:768].bitcast(BF16).rearrange("p (a c) -> p a c", a=8); bWmTs = P.buf("WmTs")
        wtmp = OV2[:, 768:896]; bwtmp = P.buf("wtmp")
        wtmp2 = OV2[:, 896:1024]; bwtmp2 = P.buf("wtmp2")
        BSb = XH[:, 0:1024].rearrange("p (a c) -> p a c", a=8); bBSb = bxpad
        BSs = XH[:, 1024:1536].rearrange("p (a c) -> p a c", a=8); bBSs = bxpad
        cvec = sb("cvec", [128, 160]); bcvec = P.buf("cvec")
        flg = sb("flg", [128, 8]); bflg = P.buf("flg")
        Wg = sb("Wg", [128, 32, 8], BF16); bWg = P.buf("Wg")
        wgf = xcf_t[:, 0:384]; bwgf = bxcf
        bgb = sb("bgb", [128, 8]); bbgb = P.buf("bgb")
        sm = sb("sm", [128, 96]); bsm = P.buf("sm")
        nst = sb("nst", [128, 16]); bnst = P.buf("nst")
        mst = sb("mst", [128, 4]); bmst = P.buf("mst")
        Bacc = sb("Bacc", [128, 4]); bBacc = P.buf("Bacc")
        nbb = sb("nbb", [128, 4], BF16); bnbb = P.buf("nbb")
        Sw = sb("Sw", [128, 128], BF16); bSw = P.buf("Sw")
        diagA = tmpZ_2; bdiagA = btmpZ
        misc = tmpS_2; bmisc = btmpS
        pmisc = sb("pmisc", [128, 64]); bpmisc = P.buf("pmisc")
        stats = sb("stats", [128, 4, 6]); bstats = P.buf("stats")
        mv = sb("mv", [128, 4, 2]); bmv = P.buf("mv")

        for b_ in (bsm, bpmisc, bstats, bmv, bnbb, bmst, bnst, bBacc):
            P.late_bufs.add(id(b_))
        ps = [st.enter_context(nc.psum_tensor(f"ps{i}", [128, 512], F32)) for i in range(8)]
        bps = [P.buf(f"ps{i}") for i in range(8)]
        pctr = [0]

        reserved = set()

        def bank(reserve=False):
            while True:
                i = pctr[0] % 8
                pctr[0] += 1
                if i not in reserved:
                    break
            if reserve:
                reserved.add(i)
            return ps[i], bps[i]

        def release(pb):
            for i in range(8):
                if ps[i] is pb:
                    reserved.discard(i)

        A = P.add
        dq = [0]

        def dmaq():
            dq[0] += 1
            return "sp"

        A("pool", lambda h: h.memset(identf[:], 1.0), wr=[bident])
        A("pool", lambda h: h.affine_select(out=identf[:], in_=identf[:], pattern=[[-1, 128]],
                                            compare_op=ALU.is_equal, fill=0.0, base=0, channel_multiplier=1),
          rd=[bident], wr=[bident])
        A("pool", lambda h: h.memset(maskT[:], 1.0), wr=[bmaskT])
        A("pool", lambda h: h.affine_select(out=maskT[:], in_=maskT[:], pattern=[[1, 128]],
                                            compare_op=ALU.is_ge, fill=0.0, base=0, channel_multiplier=-1),
          rd=[bmaskT], wr=[bmaskT])
        A("pool", lambda h: h.memset(onesf[:], 1.0), wr=[bones])
        A("pool", lambda h: h.memset(onesb[:], 1.0), wr=[bonesb])
        def ld_cvec(h):
            r = []
            for i, (src, n) in enumerate([(ngc, 16), (a_lng, 16), (a_lnb, 16), (b_cb, 16), (b_hg, 16), (b_skip, 16)]):
                r.append(h.dma_start(out=cvec[:, i * 16:(i + 1) * 16], in_=src[:, :]))
            r.append(h.dma_start(out=cvec[:, 96:160], in_=b_cw[:, :]))
            r.append(h.dma_start(out=flg[:], in_=flags.partition_broadcast(128)))
            r.append(h.dma_start(out=BSb[:].rearrange("p a c -> p (a c)"), in_=a_bs.partition_broadcast(128)))
            r.append(h.dma_start(out=wgf[:], in_=b_wg[:, :]))
            return r
        A("sp", ld_cvec, wr=[bcvec, bflg, bBSb, bwgf], dma=10, key="cvec")
        A("sp", lambda h: h.dma_start(out=bgb[:], in_=b_bg.partition_broadcast(128)), wr=[bbgb], dma=1, key="bgb")
        NG0, NG1, LNG, LNB, CB, HG, SKIP, CW = 0, 8, 16, 32, 48, 64, 80, 96

        def ld_w(dst, src, nchunk, cols=None):
            a, ncol = dst.shape[1], dst.shape[2]
            lo, hi = cols if cols is not None else (0, ncol)
            pieces = [(i, c0) for i in range(a) for c0 in range(lo, hi, 2048)]

            def f(h):
                r = []
                sv = src.rearrange("(a p) c -> p a c", p=128)
                for i, c0 in pieces:
                    c1 = min(hi, c0 + 2048)
                    r.append(h.dma_start(out=dst[:, i, c0:c1], in_=sv[:, i, c0:c1]))
                return r
            f.n = len(pieces)
            return f
        f_ = ld_w(Win0, a_w_in, 8, (E, 2 * E)); A("pool", f_, wr=[bWin0], dma=f_.n, key="Win0v")
        f_ = ld_w(Win0, a_w_in, 8, (0, E)); A("pool", f_, wr=[bWin0u], dma=f_.n, key="Win0u")
        f_ = ld_w(Win0, a_w_in, 8, (2 * E, 3 * E)); A("pool", f_, wr=[bWin0z], dma=f_.n, key="Win0z")
        f_ = ld_w(Wout0, a_w_out, 4); A("pool", f_, wr=[bWout0], dma=f_.n, key="Wout0")

        def sgu_consts(Tt, Wdst, bWdst, BSsrc, bBSsrc, bdst, bbdst, sample):
            for g in range(8):
                if not sample:
                    A("sp", lambda h, g=g: h.dma_start(out=wtmp[:], in_=a_ws[g * 128:(g + 1) * 128, :]),
                      wr=[bwtmp], dma=1, key="wtmp")
                else:
                    A("pool", lambda h: h.memset(wtmp[:], 0.0), wr=[bwtmp])
                    A("sp", lambda h, g=g: [h.dma_start(out=wtmp[16 * j:16 * j + 16, 16 * j:16 * j + 16],
                                                        in_=a_ws[g * 128:g * 128 + 16, 0:16]) for j in range(4)],
                      wr=[bwtmp], dma=4, key="wtmp")
                pb, bpb = bank()
                A("pe", lambda h, pb=pb: h.matmul(pb[:Tt, :Tt], lhsT=wtmp[:Tt, :Tt], rhs=identf[:Tt, :Tt],
                                                  start=True, stop=True), rd=[bwtmp, bident], wr=[bpb])
                A("dve", lambda h, pb=pb: h.tensor_tensor(out=wtmp2[:Tt, :Tt], in0=pb[:Tt, :Tt], in1=maskT[:Tt, :Tt],
                                                          op=ALU.mult), rd=[bpb, bmaskT], wr=[bwtmp2])
                A("pool", lambda h, g=g: h.tensor_copy(out=Wdst[:Tt, g, :Tt], in_=wtmp2[:Tt, :Tt]),
                  rd=[bwtmp2], wr=[bWdst])
                pb2, bpb2 = bank()
                A("pe", lambda h, pb2=pb2: h.matmul(pb2[:, :Tt], lhsT=onesf[:Tt, :], rhs=wtmp2[:Tt, :Tt],
                                                    start=True, stop=True), rd=[bwtmp2, bones], wr=[bpb2])
                for i in range(2):
                    ct = 2 * g + i
                    A("dve", lambda h, pb2=pb2, ct=ct, g=g: h.scalar_tensor_tensor(
                        out=bdst[:, ct, :Tt], in0=pb2[:, :Tt], scalar=cvec[:, LNB + ct:LNB + ct + 1],
                        in1=BSsrc[:, g, :Tt], op0=ALU.mult, op1=ALU.add), rd=[bpb2, bcvec, bBSsrc], wr=[bbdst])

        A("pool", lambda h: h.tensor_copy(out=BSs[:].rearrange("p a (j t) -> p a j t", j=4),
                                          in_=BSb[:, :, 0:16].unsqueeze(2).to_broadcast([128, 8, 4, 16])),
          rd=[bBSb], wr=[bBSs])
        sgu_consts(128, WmT, bWmT, BSb, bBSb, biasT, bbiasT, False)
        sgu_consts(64, WmTs, bWmTs, BSs, bBSs, biasTs, bbiasTs, True)

        for ct in range(16):
            pb, bpb = bank()
            for qi in range(3):
                A("sp", lambda h, qi=qi, ct=ct: h.dma_start(
                    out=wtmp[:], in_=wbdT[:, (qi * 16 + ct) * 128:(qi * 16 + ct + 1) * 128]),
                  wr=[bwtmp], dma=1, key="wtmp")
                col = 0 if qi < 2 else 8
                A("pe", lambda h, pb=pb, qi=qi, ct=ct, col=col: h.matmul(
                    pb[:, col:col + 8], lhsT=wtmp[:], rhs=wgf[:, (qi * 16 + ct) * 8:(qi * 16 + ct + 1) * 8],
                    start=(qi != 1), stop=(qi != 0)), rd=[bwtmp, bwgf], wr=[bpb])
            A("dve", lambda h, pb=pb, ct=ct: h.tensor_copy(out=Wg[:, ct, :], in_=pb[:, 0:8]), rd=[bpb], wr=[bWg])
            A("dve", lambda h, pb=pb, ct=ct: h.tensor_copy(out=Wg[:, 16 + ct, :], in_=pb[:, 8:16]), rd=[bpb], wr=[bWg])

        def rms_pre_a(src_ap, Tt, par, rd, J, bJ):
            X, bX = xt[par], bxt[par]
            A("sp", lambda h: h.dma_start(out=X[:Tt, :], in_=src_ap), rd=list(rd), wr=[bX], dma=1, key=f"xt{par}")
            A("act", lambda h: h.activation(out=J[:Tt, :], in_=X[:Tt, :], func=AF.Square, accum_out=sm[:Tt, 0:1]),
              rd=[bX], wr=list(bJ) + [bsm])
            A("act", lambda h: h.activation(out=sm[:Tt, 1:2], in_=sm[:Tt, 0:1], func=AF.Sqrt, scale=1.0 / DM, bias=cvec_eps[:Tt, 0:1]),
              rd=[bsm, bceps], wr=[bsm])
            A("dve", lambda h: h.reciprocal(out=sm[:Tt, 2:3], in_=sm[:Tt, 1:2]), rd=[bsm], wr=[bsm])

        def rms_pre_b(Tt, par, S, bS):
            X, bX = xt[par], bxt[par]
            A("act", lambda h: h.activation(out=S[:Tt, :], in_=X[:Tt, :], func=AF.Copy, scale=sm[:Tt, 2:3]),
              rd=[bX, bsm], wr=[bS])

        def rms_pre(src_ap, Tt, par, rd, S, bS):
            rms_pre_a(src_ap, Tt, par, rd, S, [bS])
            rms_pre_b(Tt, par, S, bS)

        def rms_post(Tt, gcol, S, bS):
            for half in range(2):
                pb, bpb = bank()
                for i in range(4):
                    dt_ = half * 4 + i
                    A("pe", lambda h, pb=pb, i=i, dt_=dt_: h.matmul(
                        pb[:, i * 128:i * 128 + Tt], lhsT=S[:Tt, dt_ * 128:(dt_ + 1) * 128], rhs=identf[:Tt, :Tt],
                        start=True, stop=True), rd=[bS, bident], wr=[bpb])
                for i in range(4):
                    dt_ = half * 4 + i
                    A("act", lambda h, pb=pb, i=i, dt_=dt_: h.activation(
                        out=xnT[:, dt_, :Tt], in_=pb[:, i * 128:i * 128 + Tt], func=AF.Copy,
                        scale=cvec[:, gcol + dt_:gcol + dt_ + 1]), rd=[bpb, bcvec], wr=[bxnTs[dt_]])

        def rms_front(src_ap, Tt, par, gcol, rd=()):
            rms_pre(src_ap, Tt, par, rd, xt[1 - par], bxt[1 - par])
            rms_post(Tt, gcol, xt[1 - par], bxt[1 - par])

        cvec_eps = sb("cvec_eps", [128, 4]); bceps = P.buf("ceps")
        A("pool", lambda h: h.memset(cvec_eps[:, 0:1], RMS_EPS), wr=[bceps])
        A("pool", lambda h: h.memset(cvec_eps[:, 1:2], LN_EPS), wr=[bceps])
        A("pool", lambda h: h.memset(cvec_eps[:, 2:3], 1.0), wr=[bceps])

        def proj_cm(W, bW, col0, cg, Tt):
            pb, bpb = bank()
            for i in range(4):
                c0 = col0 + (cg * 4 + i) * 128
                for dt_ in range(8):
                    A("pe", lambda h, pb=pb, i=i, c0=c0, dt_=dt_: h.matmul(
                        pb[:, i * Tt:(i + 1) * Tt], lhsT=W[:, dt_, c0:c0 + 128], rhs=xnT[:, dt_, :Tt],
                        start=(dt_ == 0), stop=(dt_ == 7)), rd=[bW, bxnTs[dt_]], wr=[bpb])
            return pb, bpb

        def out_proj(W, bW, Tt, par, outT):
            X, bX = xt[par], bxt[par]
            for half in range(2):
                pb, bpb = bank()
                for ct in range(16):
                    A("pe", lambda h, pb=pb, ct=ct, half=half: h.matmul(
                        pb[:Tt, :], lhsT=outT[:, ct, :Tt], rhs=W[:, ct, half * 512:(half + 1) * 512],
                        start=(ct == 0), stop=(ct == 15)), rd=[boutT, bW], wr=[bpb])
                A("dve", lambda h, pb=pb, half=half: h.tensor_tensor(
                    out=X[:Tt, half * 512:(half + 1) * 512], in0=pb[:Tt, :], in1=X[:Tt, half * 512:(half + 1) * 512],
                    op=ALU.add), rd=[bpb, bX], wr=[bX])

        def layer0_tile(src_ap, dst_row, Tt, par, sample, pre=False, nxt=None):
            X, bX = xt[par], bxt[par]
            outT, tmpZ, tmpS = outT_1, tmpZ_1, tmpS_1
            S0 = v_f32[:, 0:1024]
            if not pre:
                rms_pre(src_ap, Tt, par, (), S0, bv_f32)
            rms_post(Tt, NG0, S0, bv_f32)
            Wm_, bWm_ = (WmTs, bWmTs) if sample else (WmT, bWmT)
            bt_, bbt_ = (biasTs, bbiasTs) if sample else (biasT, bbiasT)
            for cb in range(4):
                pb, bpb = bank()
                for dt_ in range(8):
                    A("pe", lambda h, pb=pb, cb=cb, dt_=dt_: h.matmul(
                        pb[:Tt, :], lhsT=xnT[:, dt_, :Tt], rhs=Win0[:, dt_, E + cb * 512:E + (cb + 1) * 512],
                        start=(dt_ == 0), stop=(dt_ == 7)), rd=[bxnTs[dt_], bWin0], wr=[bpb])
                A("act", lambda h, pb=pb, cb=cb: h.activation(out=v_f32[:Tt, cb * 512:(cb + 1) * 512], in_=pb[:Tt, :],
                                                              func=AF.Gelu_apprx_tanh), rd=[bpb], wr=[bv_f32])
            for cb in range(4):
                A("dve", lambda h, cb=cb: h.bn_stats(out=stats[:Tt, cb, :], in_=v_f32[:Tt, cb * 512:(cb + 1) * 512]),
                  rd=[bv_f32], wr=[bstats])
            A("dve", lambda h: h.bn_aggr(out=sm[:Tt, 82:84], in_=stats[:Tt, :, :].rearrange("p a c -> p (a c)")), rd=[bstats], wr=[bsm])
            A("dve", lambda h: h.tensor_copy(out=sm[:Tt, 84:85], in_=sm[:Tt, 83:84]), rd=[bsm], wr=[bsm])
            A("act", lambda h: h.activation(out=sm[:Tt, 4:5], in_=sm[:Tt, 84:85], func=AF.Sqrt, bias=cvec_eps[:Tt, 1:2]),
              rd=[bsm, bceps], wr=[bsm])
            A("dve", lambda h: h.reciprocal(out=sm[:Tt, 5:6], in_=sm[:Tt, 4:5]), rd=[bsm], wr=[bsm])
            A("dve", lambda h: h.tensor_scalar(out=vhat[:Tt, :], in0=v_f32[:Tt, :], scalar1=sm[:Tt, 82:83],
                                               scalar2=sm[:Tt, 5:6], op0=ALU.subtract, op1=ALU.mult),
              rd=[bv_f32, bsm], wr=[bvhat])
            if sample:
                A("dve", lambda h: h.tensor_scalar(out=v_f32[:Tt, :], in0=v_f32[:Tt, :], scalar1=sm[:Tt, 82:83],
                                                   scalar2=sm[:Tt, 5:6], op0=ALU.subtract, op1=ALU.mult),
                  rd=[bsm], wr=[bv_f32])
                A("sp", lambda h: h.dma_start(out=hn[:Tt, :], in_=a_lng_row.partition_broadcast(Tt)), wr=[bhn], dma=1, key="lnrow")
                A("dve", lambda h: h.tensor_tensor(out=v_f32[:Tt, :], in0=v_f32[:Tt, :], in1=hn[:Tt, :], op=ALU.mult),
                  rd=[bhn, bv_f32], wr=[bv_f32])
                A("sp", lambda h: h.dma_start(out=hn[:Tt, :], in_=a_lnb_row.partition_broadcast(Tt)), wr=[bhn], dma=1, key="lnrow")
                A("dve", lambda h: h.tensor_tensor(out=v_f32[:Tt, :], in0=v_f32[:Tt, :], in1=hn[:Tt, :], op=ALU.add),
                  rd=[bhn, bv_f32], wr=[bv_f32])
                A("sp", lambda h: h.dma_start(out=sguv[:, :], in_=v_f32[:Tt, :]), rd=[bv_f32], dma=1, key="sguv")
            if nxt is not None:
                rms_pre(nxt[0], nxt[1], 1 - par, (), S0, bv_f32)
            for cg in range(4):
                pu, bpu = proj_cm(Win0, bWin0u, 0, cg, Tt)
                pz, bpz = proj_cm(Win0, bWin0z, 2 * E, cg, Tt)
                pS, bpS = bank()
                for i in range(4):
                    ct = cg * 4 + i
                    A("pe", lambda h, pS=pS, i=i, ct=ct: h.matmul(
                        pS[:, i * Tt:(i + 1) * Tt], lhsT=vhat[:Tt, ct * 128:(ct + 1) * 128], rhs=Wm_[:Tt, ct // 2, :Tt],
                        start=True, stop=True), rd=[bvhat, bWm_], wr=[bpS])
                n4 = 4 * Tt
                A("act", lambda h, pu=pu: h.activation(out=tmpU[:, :n4], in_=pu[:, :n4], func=AF.Gelu_apprx_tanh),
                  rd=[bpu], wr=[btmpU])
                A("act", lambda h, pz=pz: h.activation(out=tmpZ[:, :n4], in_=pz[:, :n4], func=AF.Silu),
                  rd=[bpz], wr=[btmpZ])
                for i in range(4):
                    ct = cg * 4 + i
                    A("dve", lambda h, pS=pS, i=i, ct=ct: h.scalar_tensor_tensor(
                        out=tmpS[:, i * Tt:(i + 1) * Tt], in0=pS[:, i * Tt:(i + 1) * Tt],
                        scalar=cvec[:, LNG + ct:LNG + ct + 1], in1=bt_[:, ct, :Tt], op0=ALU.mult, op1=ALU.add),
                      rd=[bpS, bcvec, bbt_], wr=[btmpS])
                A("dve", lambda h: h.tensor_tensor(out=tmpS[:, :n4], in0=tmpS[:, :n4], in1=tmpU[:, :n4], op=ALU.mult),
                  rd=[btmpU, btmpS], wr=[btmpS])
                A("dve", lambda h, cg=cg: h.tensor_tensor(
                    out=outT[:, cg * 4:(cg + 1) * 4, :Tt], in0=tmpS[:, :n4].rearrange("p (a t) -> p a t", a=4),
                    in1=tmpZ[:, :n4].rearrange("p (a t) -> p a t", a=4), op=ALU.mult),
                  rd=[btmpS, btmpZ], wr=[boutT])
            out_proj(Wout0, bWout0, Tt, par, outT)
            A("sp", lambda h: h.dma_start(out=x1s[dst_row:dst_row + Tt, :], in_=X[:Tt, :]), rd=[bX], wr=[bx1s],
              dma=1, key=f"x1st{par}")

        bx1s = P.buf("x1s")

        def l1_front(src_row, Tt, par, nstr, L, need_q, halo_mode, conv_out=None, pre=None, nxt=None, halo_only=False):
            if pre is None:
                rms_front(x1s[src_row:src_row + Tt, :], Tt, par, NG1, rd=[bx1s])
            elif pre == "B0":
                rms_pre(x1s[src_row:src_row + Tt, :], Tt, par, [bx1s], hn[:, 0:1024], bxpad)
                rms_post(Tt, NG1, hn[:, 0:1024], bxpad)
            elif pre == "B1":
                rms_pre_b(Tt, par, hn[:, 0:1024], bxpad)
                rms_post(Tt, NG1, hn[:, 0:1024], bxpad)
            else:
                if not pre:
                    rms_pre(x1s[src_row:src_row + Tt, :], Tt, par, [bx1s], hn[:, 0:1024], bxpad)
                rms_post(Tt, NG1, hn[:, 0:1024], bxpad)
            A("dve", lambda h: h.memset(dummy[:, 0:1], 0.0), wr=ALLX + [bdummy])
            xp4 = xpad[:, :, 0:nstr * (L + 3)].rearrange("p a (j t) -> p a j t", j=nstr)
            if halo_mode == "zero":
                A("pool", lambda h: h.memset(xp4[:, :, :, 0:3], 0.0), wr=[bxpad])
            elif halo_mode == "copy":
                A("act", lambda h: h.activation(out=xp4[:, :, :, 0:3], in_=halo[:, :, 0:3].unsqueeze(2), func=AF.Copy), rd=[bhalo], wr=[bxpad])
            elif halo_mode == "flag":
                A("act", lambda h: h.activation(out=xp4[:, :, :, 0:3], in_=halo[:, :, 0:3].unsqueeze(2), func=AF.Copy,
                                                scale=flg[:, 0:1]), rd=[bhalo, bflg], wr=[bxpad])
            elif halo_mode == "sample":
                A("sp", lambda h: h.dma_start(out=crow[:12, :], in_=stconv[:, :]), wr=[bcrow], dma=1, key="crow")
                pb, bpb = bank()
                for ct in range(16):
                    A("pe", lambda h, pb=pb, ct=ct: h.matmul(pb[:, ct * 12:(ct + 1) * 12], lhsT=crow[:12, ct * 128:(ct + 1) * 128],
                                                             rhs=identf[:12, :12], start=True, stop=True),
                      rd=[bcrow, bident], wr=[bpb])
                A("act", lambda h, pb=pb: h.activation(out=xp4[:, :, :, 0:3],
                                                       in_=pb[:, 0:192].rearrange("p (a j t) -> p a j t", a=16, j=4),
                                                       func=AF.Copy), rd=[bpb], wr=[bxpad])
            for cg in range(4):
                pb, bpb = proj_cm(Win1, bWin1, 0, cg, Tt)
                A("act", lambda h, pb=pb, cg=cg: h.activation(
                    out=xp4[:, cg * 4:(cg + 1) * 4, :, 3:3 + L],
                    in_=pb[:, :4 * Tt].rearrange("p (a j t) -> p a j t", a=4, j=nstr), func=AF.Copy), rd=[bpb], wr=[bxpad])
                A("act", lambda h, cg=cg: h.activation(
                    out=xmT[:, cg * 4:(cg + 1) * 4, :Tt].rearrange("p a (j t) -> p a j t", j=nstr),
                    in_=xp4[:, cg * 4:(cg + 1) * 4, :, 3:3 + L], func=AF.Copy), rd=[bxpad], wr=[bxmTs[cg]])
            A("act", lambda h: h.activation(out=halo[:, :, 0:3 * nstr].rearrange("p a (j t) -> p a j t", j=nstr),
                                            in_=xp4[:, :, :, L:L + 3], func=AF.Copy), rd=[bxpad], wr=[bhalo])
            if conv_out is not None:
                conv_tail_out(nstr, 3 * nstr, conv_out[0], conv_out[1])
            if halo_only:
                A("act", lambda h: h.activation(out=halo0[:, :, :], in_=halo[:, :, 0:3], func=AF.Copy), rd=[bhalo], wr=[bhalo0])
                if nxt is not None:
                    rms_pre(x1s[nxt:nxt + 128, :], 128, 1 - par, [bx1s], hn[:, 0:1024], bxpad)
                return
            xc4 = xcf[:, :, :Tt].rearrange("p a (j t) -> p a j t", j=nstr)
            A("dve", lambda h: h.memset(dummy[:, 1:2], 0.0), wr=ALLX + [bdummy])
            for ct in range(16):
                A("act", lambda h, ct=ct: h.activation(
                    out=xc4[:, ct], in_=xp4[:, ct, :, 0:L], func=AF.Identity, scale=cvec[:, CW + ct * 4:CW + ct * 4 + 1],
                    bias=cvec[:, CB + ct:CB + ct + 1]), rd=[bxpad, bcvec], wr=[bxcfc[ct]])
            for ct_, j in [(c_, j_) for hf in range(2) for j_ in range(1, 4) for c_ in range(hf * 8, hf * 8 + 8)]:
                for ct in (ct_,):
                    A("dve", lambda h, ct=ct, j=j: h.scalar_tensor_tensor(
                        out=xc4[:, ct], in0=xp4[:, ct, :, j:j + L], scalar=cvec[:, CW + ct * 4 + j:CW + ct * 4 + j + 1],
                        in1=xc4[:, ct], op0=ALU.mult, op1=ALU.add), rd=[bxpad, bcvec, bxcfc[ct]], wr=[bxcfc[ct]])
            for cg in range(4):
                A("act", lambda h, cg=cg: h.activation(out=xcf[:, cg * 4:(cg + 1) * 4, :Tt], in_=xcf[:, cg * 4:(cg + 1) * 4, :Tt],
                                                       func=AF.Silu), rd=bxcfc[cg * 4:(cg + 1) * 4], wr=bxcfc[cg * 4:(cg + 1) * 4])
            for cg in range(4):
                A("act", lambda h, cg=cg: h.activation(out=xcT[:, cg * 4:(cg + 1) * 4, :Tt], in_=xcf[:, cg * 4:(cg + 1) * 4, :Tt],
                                                       func=AF.Copy), rd=bxcfc[cg * 4:(cg + 1) * 4], wr=[bxcTs[cg]])
            if nxt is not None and pre in ("B0", "B1"):
                rms_pre_a(x1s[nxt:nxt + 128, :], 128, 1 - par, [bx1s], JUNK, [btmpZ, btmpS])
            elif nxt is not None:
                rms_pre(x1s[nxt:nxt + 128, :], 128, 1 - par, [bx1s], hn[:, 0:1024], bxpad)
            if need_q:
                for qi, (dst, bdst, scl) in enumerate([(qT, bqT, 1.0), (kT, bkT, KSCALE)]):
                    for cg in range(4):
                        pb, bpb = bank()
                        for i in range(4):
                            ct = cg * 4 + i
                            A("pe", lambda h, pb=pb, i=i, ct=ct, qi=qi: h.matmul(
                                pb[:, i * Tt:(i + 1) * Tt], lhsT=Wbd[:, qi * 16 + ct, :], rhs=xcT[:, ct, :Tt],
                                start=True, stop=True), rd=[bWbd, bxcTs[ct // 4]], wr=[bpb])
                        A("act", lambda h, pb=pb, cg=cg, dst=dst, scl=scl: h.activation(
                            out=dst[:, cg * 4:(cg + 1) * 4, :Tt], in_=pb[:, :4 * Tt].rearrange("p (a t) -> p a t", a=4),
                            func=AF.Copy, scale=scl), rd=[bpb], wr=[bdst])
                A("dve", lambda h: h.tensor_tensor(out=xcf[:, :, :Tt], in0=xcf[:, :, :Tt],
                                                    in1=cvec[:, SKIP:SKIP + 16].unsqueeze(2).to_broadcast([128, 16, Tt]),
                                                    op=ALU.mult), rd=bxcfc + [bcvec], wr=bxcfc)

        cbpar = [0]

        def scan_chunk(off, L, full, Cview, nview, bCl, bnl, mview, bml, acc_B=False):
            mview = mview[:, :]
            pg, bpg = bank()
            for ct in range(16):
                A("pe", lambda h, pg=pg, ct=ct: h.matmul(pg[:L, 0:8], lhsT=xcT[:, ct, off:off + L], rhs=Wg[:, ct, :],
                                                         start=(ct == 0), stop=False), rd=[bxcTs[ct // 4], bWg], wr=[bpg])
            for ct in range(16):
                A("pe", lambda h, pg=pg, ct=ct: h.matmul(pg[:L, 0:8], lhsT=xmT[:, ct, off:off + L], rhs=Wg[:, 16 + ct, :],
                                                         start=False, stop=(ct == 15)), rd=[bxmTs[ct // 4], bWg], wr=[bpg])
            A("dve", lambda h, pg=pg: h.tensor_tensor(out=sm[:L, 8:16], in0=pg[:L, 0:8], in1=bgb[:L, :], op=ALU.add),
              rd=[bpg, bbgb], wr=[bsm])
            A("act", lambda h: h.activation(out=sm[:L, 16:20], in_=sm[:L, 12:16], func=AF.Abs), rd=[bsm], wr=[bsm])
            A("act", lambda h: h.activation(out=sm[:L, 20:24], in_=sm[:L, 16:20], func=AF.Exp, scale=-1.0), rd=[bsm], wr=[bsm])
            A("act", lambda h: h.activation(out=sm[:L, 24:28], in_=sm[:L, 20:24], func=AF.Ln, bias=cvec_eps[:L, 2:3]),
              rd=[bsm, bceps], wr=[bsm])
            A("dve", lambda h: h.tensor_single_scalar(out=sm[:L, 28:32], in_=sm[:L, 12:16], scalar=0.0, op=ALU.min),
              rd=[bsm], wr=[bsm])
            A("dve", lambda h: h.tensor_tensor(out=sm[:L, 28:32], in0=sm[:L, 28:32], in1=sm[:L, 24:28], op=ALU.subtract),
              rd=[bsm], wr=[bsm])
            pc, bpc = bank()
            A("pe", lambda h, pc=pc: h.matmul(pc[:L, 0:4], lhsT=maskT[:L, :L], rhs=sm[:L, 28:32], start=True, stop=True),
              rd=[bmaskT, bsm], wr=[bpc])
            A("pe", lambda h, pc=pc: h.matmul(pc[:, 8:12], lhsT=onesf[:L, :], rhs=sm[:L, 28:32], start=True, stop=True),
              rd=[bones, bsm], wr=[bpc])
            A("dve", lambda h, pc=pc: h.tensor_copy(out=sm[:L, 72:76], in_=pc[:L, 0:4]), rd=[bpc], wr=[bsm])
            A("dve", lambda h: h.tensor_tensor(out=sm[:L, 32:36], in0=sm[:L, 8:12], in1=sm[:L, 72:76], op=ALU.subtract),
              rd=[bsm], wr=[bsm])
            A("dve", lambda h: h.tensor_tensor(
                out=diagA[:L, :4 * L].rearrange("p (a t) -> p a t", a=4),
                in0=identf[:L, :L].unsqueeze(1).to_broadcast([L, 4, L]),
                in1=sm[:L, 32:36].unsqueeze(2).to_broadcast([L, 4, L]), op=ALU.mult), rd=[bident, bsm], wr=[bdiagA])
            pa, bpa = bank()
            A("pe", lambda h, pa=pa: h.matmul(pa[:, :4 * L], lhsT=onesf[:L, :], rhs=diagA[:L, :4 * L], start=True, stop=True),
              rd=[bones, bdiagA], wr=[bpa])
            A("dve", lambda h, pa=pa: h.tensor_reduce(out=pmisc[:, 0:4], in_=pa[:, :4 * L].rearrange("p (a t) -> p a t", a=4),
                                                      axis=AX.X, op=ALU.max), rd=[bpa], wr=[bpmisc])
            A("dve", lambda h: h.tensor_tensor(out=pmisc[:, 4:8], in0=pmisc[:, 0:4], in1=mview, op=ALU.max),
              rd=[bpmisc, bml], wr=[bpmisc])
            A("dve", lambda h: h.tensor_tensor(out=pmisc[:, 12:16], in0=mview, in1=pmisc[:, 4:8], op=ALU.subtract),
              rd=[bpmisc, bml], wr=[bpmisc])
            A("act", lambda h: h.activation(out=pmisc[:, 8:12], in_=pmisc[:, 12:16], func=AF.Exp), rd=[bpmisc], wr=[bpmisc])
            A("dve", lambda h, pc=pc: h.tensor_copy(out=pmisc[:, 16:20], in_=pc[:, 8:12]), rd=[bpc], wr=[bpmisc])
            A("dve", lambda h: h.tensor_tensor(out=mview, in0=pmisc[:, 16:20], in1=pmisc[:, 4:8], op=ALU.add),
              rd=[bpmisc], wr=[bml])
            if acc_B:
                A("dve", lambda h: h.tensor_tensor(out=Bacc[:, :], in0=Bacc[:, :], in1=pmisc[:, 16:20], op=ALU.add),
                  rd=[bpmisc, bBacc], wr=[bBacc])
            A("dve", lambda h: h.tensor_tensor(out=sm[:L, 64:68], in0=sm[:L, 32:36], in1=pmisc[:L, 4:8], op=ALU.subtract),
              rd=[bsm, bpmisc], wr=[bsm])
            A("act", lambda h: h.activation(out=sm[:L, 44:48], in_=sm[:L, 64:68], func=AF.Exp), rd=[bsm], wr=[bsm])
            A("dve", lambda h: h.tensor_single_scalar(out=sm[:L, 52:56], in_=sm[:L, 44:48], scalar=KSCALE, op=ALU.mult),
              rd=[bsm], wr=[bsm])
            if full:
                A("dve", lambda h: h.tensor_tensor(out=sm[:L, 64:68], in0=sm[:L, 72:76], in1=pmisc[:L, 4:8], op=ALU.add),
                  rd=[bsm, bpmisc], wr=[bsm])
                A("act", lambda h: h.activation(out=sm[:L, 48:52], in_=sm[:L, 64:68], func=AF.Exp, scale=-1.0), rd=[bsm], wr=[bsm])
            for hh in range(4):
                pk, bpk = bank()
                for i in range(4):
                    ct = hh * 4 + i
                    A("pe", lambda h, pk=pk, i=i, ct=ct: h.matmul(pk[:L, i * 128:(i + 1) * 128], lhsT=xcT[:, ct, off:off + L],
                                                                  rhs=Wbd[:, 16 + ct, :], start=True, stop=True),
                      rd=[bxcTs[ct // 4], bWbd], wr=[bpk])
                A("act", lambda h, pk=pk, hh=hh: h.activation(out=kw[:L, hh * 512:(hh + 1) * 512], in_=pk[:L, :], func=AF.Copy,
                                                              scale=sm[:L, 52 + hh:53 + hh]), rd=[bpk, bsm], wr=[bkw])
                pv, bpv = bank()
                for i in range(4):
                    ct = hh * 4 + i
                    A("pe", lambda h, pv=pv, i=i, ct=ct: h.matmul(pv[:L, i * 128:(i + 1) * 128], lhsT=xmT[:, ct, off:off + L],
                                                                  rhs=Wbd[:, 32 + ct, :], start=True, stop=True),
                      rd=[bxmTs[ct // 4], bWbd], wr=[bpv])
                A("dve", lambda h, pv=pv, hh=hh: h.tensor_copy(out=vtok[:L, hh * 512:(hh + 1) * 512], in_=pv[:L, :]),
                  rd=[bpv], wr=[bvtok])
            pden, bpden = (None, None)
            if full:
                pden, bpden = bank(reserve=True)
            pn, bpn = bank(reserve=True)
            for hh in range(4):
                c0col = pmisc[:, 8 + hh:9 + hh]
                if full:
                    par = cbpar[0] % 2
                    cbpar[0] += 1
                    CB_, bCB_ = Cb[par], bCb[par]
                    for dt_ in range(4):
                        A("act", lambda h, hh=hh, dt_=dt_, CB_=CB_, c0col=c0col: h.activation(
                            out=CB_[:, dt_, :], in_=Cview(hh, dt_), func=AF.Copy, scale=c0col),
                          rd=[bCl[hh * 4 + dt_], bpmisc], wr=[bCB_])
                    A("dve", lambda h, hh=hh, c0col=c0col: h.tensor_scalar(out=nbb[:, :], in0=nview[:, hh * 4:(hh + 1) * 4],
                                                                           scalar1=c0col, scalar2=None, op0=ALU.mult),
                      rd=[bnl, bpmisc], wr=[bnbb])
                    pst, bpst = bank()
                    for dt_ in range(4):
                        ct = hh * 4 + dt_
                        A("pe", lambda h, pst=pst, ct=ct, dt_=dt_: h.matmul(
                            pst[:L, :L], lhsT=kT[:, ct, off:off + L], rhs=qT[:, ct, off:off + L],
                            start=(dt_ == 0), stop=(dt_ == 3)), rd=[bkT, bqT], wr=[bpst])
                    A("dve", lambda h, pst=pst, hh=hh: h.scalar_tensor_tensor(
                        out=Sw[:L, :L], in0=pst[:L, :L], scalar=sm[:L, 44 + hh:45 + hh], in1=maskT[:L, :L],
                        op0=ALU.mult, op1=ALU.mult), rd=[bpst, bsm, bmaskT], wr=[bSw])
                    pnum, bpnum = bank()
                    A("pe", lambda h, pnum=pnum, hh=hh: h.matmul(pnum[:L, :], lhsT=Sw[:L, :L], rhs=vtok[:L, hh * 512:(hh + 1) * 512],
                                                                 start=True, stop=False), rd=[bSw, bvtok], wr=[bpnum])
                    for dt_ in range(4):
                        ct = hh * 4 + dt_
                        A("pe", lambda h, pnum=pnum, ct=ct, dt_=dt_, CB_=CB_: h.matmul(
                            pnum[:L, :], lhsT=qT[:, ct, off:off + L], rhs=CB_[:, dt_, :], start=False, stop=(dt_ == 3)),
                          rd=[bqT, bCB_], wr=[bpnum])
                    A("pe", lambda h, hh=hh: h.matmul(pden[:L, hh:hh + 1], lhsT=Sw[:L, :L], rhs=onesb[:L, 0:1],
                                                      start=True, stop=False), rd=[bSw, bonesb], wr=[bpden])
                    for dt_ in range(4):
                        ct = hh * 4 + dt_
                        A("pe", lambda h, hh=hh, ct=ct, dt_=dt_: h.matmul(
                            pden[:L, hh:hh + 1], lhsT=qT[:, ct, off:off + L], rhs=nbb[:, dt_:dt_ + 1],
                            start=False, stop=(dt_ == 3)), rd=[bqT, bnbb], wr=[bpden])
                    A("act", lambda h, pnum=pnum, hh=hh: h.activation(out=hn[:L, hh * 512:(hh + 1) * 512], in_=pnum[:L, :],
                                                                      func=AF.Copy), rd=[bpnum], wr=[bhn])
                for dt_ in range(4):
                    ct = hh * 4 + dt_
                    pu_, bpu_ = bank()
                    A("pe", lambda h, pu_=pu_, ct=ct, hh=hh: h.matmul(
                        pu_[:, :], lhsT=kw[:L, ct * 128:(ct + 1) * 128], rhs=vtok[:L, hh * 512:(hh + 1) * 512],
                        start=True, stop=True), rd=[bkw, bvtok], wr=[bpu_])
                    A("dve", lambda h, pu_=pu_, hh=hh, dt_=dt_, c0col=c0col: h.scalar_tensor_tensor(
                        out=Cview(hh, dt_), in0=Cview(hh, dt_), scalar=c0col, in1=pu_[:, :], op0=ALU.mult, op1=ALU.add),
                      rd=[bpu_, bpmisc, bCl[hh * 4 + dt_]], wr=[bCl[hh * 4 + dt_]])
                    A("pe", lambda h, ct=ct: h.matmul(pn[:, ct:ct + 1], lhsT=kw[:L, ct * 128:(ct + 1) * 128], rhs=onesb[:L, 0:1],
                                                      start=True, stop=True), rd=[bkw, bonesb], wr=[bpn])
                A("dve", lambda h, hh=hh, c0col=c0col: h.scalar_tensor_tensor(
                    out=nview[:, hh * 4:(hh + 1) * 4], in0=nview[:, hh * 4:(hh + 1) * 4], scalar=c0col,
                    in1=pn[:, hh * 4:(hh + 1) * 4], op0=ALU.mult, op1=ALU.add), rd=[bpn, bpmisc, bnl], wr=[bnl])
            release(pn)
            if full:
                release(pden)
                A("act", lambda h: h.activation(out=sm[:L, 56:60], in_=pden[:L, 0:4], func=AF.Abs), rd=[bpden], wr=[bsm])
                A("dve", lambda h: h.tensor_tensor(out=sm[:L, 56:60], in0=sm[:L, 56:60], in1=sm[:L, 48:52], op=ALU.max),
                  rd=[bsm], wr=[bsm])
                A("dve", lambda h: h.reciprocal(out=sm[:L, 60:64], in_=sm[:L, 56:60]), rd=[bsm], wr=[bsm])
                for hh in range(4):
                    A("dve", lambda h, hh=hh: h.bn_stats(out=stats[:L, hh, :], in_=hn[:L, hh * 512:(hh + 1) * 512]),
                      rd=[bhn], wr=[bstats])
                    A("dve", lambda h, hh=hh: h.bn_aggr(out=mv[:L, hh, :], in_=stats[:L, hh, :]), rd=[bstats], wr=[bmv])
                A("dve", lambda h: h.tensor_tensor(out=sm[:L, 64:68], in0=sm[:L, 60:64], in1=sm[:L, 60:64], op=ALU.mult),
                  rd=[bsm], wr=[bsm])
                A("dve", lambda h: h.tensor_tensor(out=sm[:L, 64:68], in0=sm[:L, 64:68], in1=mv[:L, :, 1], op=ALU.mult),
                  rd=[bsm, bmv], wr=[bsm])
                A("act", lambda h: h.activation(out=sm[:L, 64:68], in_=sm[:L, 64:68], func=AF.Sqrt, bias=cvec_eps[:L, 1:2]),
                  rd=[bsm, bceps], wr=[bsm])
                A("dve", lambda h: h.reciprocal(out=sm[:L, 68:72], in_=sm[:L, 64:68]), rd=[bsm], wr=[bsm])
                A("dve", lambda h: h.tensor_tensor(out=sm[:L, 68:72], in0=sm[:L, 68:72], in1=sm[:L, 60:64], op=ALU.mult),
                  rd=[bsm], wr=[bsm])
                for hh in range(4):
                    A("dve", lambda h, hh=hh: h.tensor_scalar(
                        out=hn[:L, hh * 512:(hh + 1) * 512], in0=hn[:L, hh * 512:(hh + 1) * 512], scalar1=mv[:L, hh, 0:1],
                        scalar2=sm[:L, 68 + hh:69 + hh], op0=ALU.subtract, op1=ALU.mult), rd=[bhn, bmv, bsm], wr=[bhn])

        def hn_transpose(banks, off, L, Tt):
            for cg in range(4):
                pb, bpb = banks[cg]
                for i in range(4):
                    ct = cg * 4 + i
                    A("pe", lambda h, pb=pb, i=i, ct=ct: h.matmul(pb[:, i * Tt + off:i * Tt + off + L],
                                                                  lhsT=hn[:L, ct * 128:(ct + 1) * 128], rhs=identf[:L, :L],
                                                                  start=True, stop=True), rd=[bhn, bident], wr=[bpb])

        def l1_tail(banks, Tt, par, out_ap, key):
            X, bX = xt[par], bxt[par]
            xsf, bxsf = xt[1 - par], bxt[1 - par]
            outT, tmpZ, tmpS = outT_2, tmpZ_2, tmpS_2
            n4 = 4 * Tt
            for cg in range(4):
                pz, bpz = proj_cm(Win1, bWin1z, E, cg, Tt)
                A("act", lambda h, pz=pz: h.activation(out=tmpZ[:, :n4], in_=pz[:, :n4], func=AF.Silu), rd=[bpz], wr=[btmpZ])
                pb, bpb = banks[cg]
                for i in range(4):
                    ct = cg * 4 + i
                    A("dve", lambda h, pb=pb, i=i, ct=ct: h.scalar_tensor_tensor(
                        out=tmpS[:, i * Tt:(i + 1) * Tt], in0=pb[:, i * Tt:(i + 1) * Tt], scalar=cvec[:, HG + ct:HG + ct + 1],
                        in1=xcf[:, ct, :Tt], op0=ALU.mult, op1=ALU.add), rd=[bpb, bcvec, bxcfc[ct]], wr=[btmpS])
                A("dve", lambda h, cg=cg: h.tensor_tensor(
                    out=outT[:, cg * 4:(cg + 1) * 4, :Tt], in0=tmpS[:, :n4].rearrange("p (a t) -> p a t", a=4),
                    in1=tmpZ[:, :n4].rearrange("p (a t) -> p a t", a=4), op=ALU.mult), rd=[btmpS, btmpZ], wr=[boutT])
            out_proj(Wout1, bWout1, Tt, par, outT)
            A("act", lambda h: h.activation(out=JUNK[:Tt, :], in_=X[:Tt, :], func=AF.Square, accum_out=sm[:Tt, 86:87]),
              rd=[bX], wr=[btmpZ, btmpS, bsm])
            A("act", lambda h: h.activation(out=sm[:Tt, 87:88], in_=sm[:Tt, 86:87], func=AF.Sqrt, scale=1.0 / DM, bias=cvec_eps[:Tt, 0:1]),
              rd=[bsm, bceps], wr=[bsm])
            A("dve", lambda h: h.reciprocal(out=sm[:Tt, 88:89], in_=sm[:Tt, 87:88]), rd=[bsm], wr=[bsm])
            A("dve", lambda h: h.scalar_tensor_tensor(out=X[:Tt, :], in0=X[:Tt, :], scalar=sm[:Tt, 88:89], in1=FGb[:Tt, :],
                                                      op0=ALU.mult, op1=ALU.mult), rd=[bX, bsm, bFGb], wr=[bX])
            A("sp", lambda h: h.dma_start(out=out_ap, in_=X[:Tt, :]), rd=[bX], dma=1, key=key)

        def conv_tail_out(nstr, nrows, out_ap, key):
            for q in range(4):
                pb, bpb = bank()
                for i in range(4):
                    ct = q * 4 + i
                    A("pe", lambda h, pb=pb, i=i, ct=ct: h.matmul(pb[:nrows, i * 128:(i + 1) * 128], lhsT=halo[:, ct, 0:nrows],
                                                                  rhs=identf[:, :], start=True, stop=True),
                      rd=[bhalo, bident], wr=[bpb])
                A("dve", lambda h, pb=pb, q=q: h.tensor_copy(out=crow[:nrows, q * 512:(q + 1) * 512], in_=pb[:nrows, :]),
                  rd=[bpb], wr=[bcrow])
            A("sp", lambda h: h.dma_start(out=out_ap, in_=crow[:nrows, :]), rd=[bcrow], dma=1, key=key)

        layer0_tile(xs[:, :], (NT + 1) * 128, 64, 0, True, pre=False, nxt=(xp[0:128, :], 128))
        for i in range(NT + 1):
            nx = (xp[(i + 1) * 128:(i + 2) * 128, :], 128) if i < NT else None
            layer0_tile(xp[i * 128:(i + 1) * 128, :], i * 128, 128, (i + 1) % 2, False, pre=True, nxt=nx)

        def zero_state(extra=()):
            for hh in range(4):
                A("pool", lambda h, hh=hh: h.memset(Cst[:, hh * 4:(hh + 1) * 4, :], 0.0), wr=bC[hh * 4:(hh + 1) * 4] + list(extra))
            A("pool", lambda h: h.memset(nst[:], 0.0), wr=[bnst])
            A("pool", lambda h: h.memset(mst[:], 0.0), wr=[bmst])

        def Cv(hh, dt_):
            return Cst[:, hh * 4 + dt_, :]

        bsin = P.buf("summ_in")
        bsout = P.buf("summ_out")
        SROW = (NT + 1) * 128

        def phase_w1():
            W0ALL = [bWin0, bWin0u, bWin0z, bWout0]
            f_ = ld_w(Win1, b_w_in, 8, (0, E)); A("pool", f_, wr=[bWin1] + W0ALL, dma=f_.n, key="Win1")
            A("pool", lambda h: [h.dma_start(out=Wbd_t[:, i * 2048:(i + 1) * 2048], in_=wbd[:, i * 2048:(i + 1) * 2048]) for i in range(3)], wr=[bWbd, bvhat, btmpZ, btmpS], dma=3, key="Wbd")

            zero_state(extra=[bv_f32, bbiasT, bbiasTs, btmpU, btmpZ, btmpS, bWout0, *bxcTs, *bxmTs, bqT, bkT, bkw, bvtok, bCb[0], bCb[1]])
            A("pool", lambda h: h.memset(Bacc[:], 0.0), wr=[bBacc])
            A("sp", lambda h: h.dma_start(out=FGb[:, :], in_=fng.partition_broadcast(128)),
              wr=[bFGb, bWmT, bWmTs, bwtmp, bwtmp2], dma=1, key="FGb")


        def phase_A():
            import os
            KA = int(os.environ.get("KA", str(NT)))
            KSCAN = int(os.environ.get("KSCAN", "1"))
            l1_front(0, 128, 0, 1, 128, False, "zero", pre=False, nxt=128, halo_only=True)
            for i in range(KA):
                l1_front((i + 1) * 128, 128, (i + 1) % 2, 1, 128, False, "flag" if i == 0 else "copy",
                         pre=True, nxt=((i + 2) * 128 if i < KA - 1 else None))
                if KSCAN:
                    scan_chunk(0, 128, False, Cv, nst, bC, bnst, mst, bmst, acc_B=True)
                if i == 0:
                    W0ALL = [bWin0, bWin0u, bWin0z, bWout0]
                    f_ = ld_w(Win1, b_w_in, 8, (E, 2 * E)); A("pool", f_, wr=[bWin1z] + W0ALL, dma=f_.n, key="Win1z")
                    f_ = ld_w(Wout1, b_w_out, 4); A("pool", f_, wr=[bWout1] + W0ALL, dma=f_.n, key="Wout1")

        def phase_X():
            A("dve", lambda h: h.memset(misc[:], 0.0), wr=[bmisc])
            A("dve", lambda h: h.tensor_copy(out=misc[:, 0:16], in_=nst[:, :]), rd=[bnst], wr=[bmisc])
            A("dve", lambda h: h.tensor_copy(out=misc[:, 16:20], in_=mst[:, :]), rd=[bmst], wr=[bmisc])
            A("dve", lambda h: h.tensor_copy(out=misc[:, 20:24], in_=Bacc[:, :]), rd=[bBacc], wr=[bmisc])
            A("sp", lambda h: [h.dma_start(out=summ_in[hh][:, :].rearrange("(a p) e -> p a e", p=128), in_=Cst[:, hh * 4:(hh + 1) * 4, :])
                               for hh in range(4)], rd=bC, wr=[bsin], dma=4, key="summC")
            A("sp", lambda h: h.dma_start(out=summ_in_m[:, :], in_=misc[:, :]), rd=[bmisc], wr=[bsin], dma=1, key="summM")
            RG = [[0, 1, 2, 3], [4, 5, 6, 7]]
            for hh in range(4):
                A("pool", lambda h, hh=hh: h.collective_compute("AllGather", ALU.bypass, replica_groups=RG,
                                                                ins=[summ_in[hh].ap().opt()], outs=[summ_out[hh].ap().opt()]),
                  rd=[bsin], wr=[bsout], dma="cc", key="cc")
            A("pool", lambda h: h.collective_compute("AllGather", ALU.bypass, replica_groups=RG,
                                                     ins=[summ_in_m.ap().opt()], outs=[summ_out_m.ap().opt()]),
              rd=[bsin], wr=[bsout], dma="cc", key="cc")

        def phase_S():
            l1_front(SROW, 64, 0, 4, 16, True, "sample", conv_out=(convs[:, :], "convs"))
            sbanks = [bank(reserve=True) for _ in range(4)]
            for j in range(4):
                for hh in range(4):
                    A("sp", lambda h, j=j, hh=hh: h.dma_start(
                        out=Cst[:, hh * 4:(hh + 1) * 4, :],
                        in_=stC[j * 2048 + hh * 512:j * 2048 + (hh + 1) * 512, :].rearrange("(a p) e -> p a e", p=128)),
                      wr=bC[hh * 4:(hh + 1) * 4], dma=1, key=f"Cld{hh}")
                A("sp", lambda h, j=j: h.dma_start(out=nst[:, :], in_=stn[j, :, :]), wr=[bnst], dma=1, key="nld")
                A("sp", lambda h, j=j: h.dma_start(out=mst[:, :], in_=stm[j:j + 1, :].partition_broadcast(128)),
                  wr=[bmst], dma=1, key="mld")
                scan_chunk(16 * j, 16, True, Cv, nst, bC, bnst, mst, bmst)
                hn_transpose(sbanks, 16 * j, 16, 64)
                for hh in range(4):
                    A("sp", lambda h, j=j, hh=hh: h.dma_start(
                        out=Cs[j * 2048 + hh * 512:j * 2048 + (hh + 1) * 512, :].rearrange("(a p) e -> p a e", p=128),
                        in_=Cst[:, hh * 4:(hh + 1) * 4, :]), rd=bC[hh * 4:(hh + 1) * 4], dma=1, key=f"Cst_out{hh}")
                A("sp", lambda h, j=j: h.dma_start(out=ns_o[j, :, :], in_=nst[:, :]), rd=[bnst], dma=1, key="nst_out")
                A("sp", lambda h, j=j: h.dma_start(out=ms_o[j:j + 1, :], in_=mst[0:1, :]), rd=[bmst], dma=1, key="mst_out")
            l1_tail(sbanks, 64, 0, ys[:, :], "ys")
            for pb, _ in sbanks:
                release(pb)

        def phase_C():
            zero_state()
            for r in range(3):
                A("sp", lambda h, r=r: h.dma_start(out=misc[:, :], in_=summ_out_m[r * 128:(r + 1) * 128, :]),
                  rd=[bsout], wr=[bmisc], dma=1, key="miscld")
                pm = flg[:, 1 + r:2 + r]
                A("dve", lambda h, pm=pm: h.tensor_scalar(out=pmisc[:, 20:24], in0=misc[:, 20:24], scalar1=pm, scalar2=None, op0=ALU.mult),
                  rd=[bmisc, bflg], wr=[bpmisc])
                A("dve", lambda h, pm=pm: h.tensor_scalar(out=pmisc[:, 28:29], in0=pm, scalar1=-1.0, scalar2=1e30, op0=ALU.add, op1=ALU.mult),
                  rd=[bflg], wr=[bpmisc])
                A("dve", lambda h, pm=pm: h.tensor_scalar(out=pmisc[:, 24:28], in0=misc[:, 16:20], scalar1=pm, scalar2=pmisc[:, 28:29],
                                                          op0=ALU.mult, op1=ALU.add), rd=[bmisc, bflg, bpmisc], wr=[bpmisc])
                A("dve", lambda h: h.tensor_tensor(out=pmisc[:, 44:48], in0=pmisc[:, 20:24], in1=mst[:, :], op=ALU.add),
                  rd=[bpmisc, bmst], wr=[bpmisc])
                A("dve", lambda h: h.tensor_tensor(out=pmisc[:, 32:36], in0=pmisc[:, 44:48], in1=pmisc[:, 24:28], op=ALU.max),
                  rd=[bpmisc], wr=[bpmisc])
                A("dve", lambda h: h.tensor_tensor(out=pmisc[:, 44:48], in0=pmisc[:, 44:48], in1=pmisc[:, 32:36], op=ALU.subtract),
                  rd=[bpmisc], wr=[bpmisc])
                A("act", lambda h: h.activation(out=pmisc[:, 36:40], in_=pmisc[:, 44:48], func=AF.Exp), rd=[bpmisc], wr=[bpmisc])
                A("dve", lambda h: h.tensor_tensor(out=pmisc[:, 44:48], in0=pmisc[:, 24:28], in1=pmisc[:, 32:36], op=ALU.subtract),
                  rd=[bpmisc], wr=[bpmisc])
                A("act", lambda h: h.activation(out=pmisc[:, 40:44], in_=pmisc[:, 44:48], func=AF.Exp), rd=[bpmisc], wr=[bpmisc])
                A("dve", lambda h: h.tensor_copy(out=mst[:, :], in_=pmisc[:, 32:36]), rd=[bpmisc], wr=[bmst])
                for hh in range(4):
                    A("sp", lambda h, r=r, hh=hh: h.dma_start(
                        out=hn[:, :].rearrange("p (a e) -> p a e", a=4),
                        in_=summ_out[hh][r * 512:(r + 1) * 512, :].rearrange("(a p) e -> p a e", p=128)),
                      rd=[bsout], wr=[bhn], dma=1, key="Crld")
                    for dt_ in range(4):
                        A("act", lambda h, hh=hh, dt_=dt_: h.activation(out=Cv(hh, dt_), in_=Cv(hh, dt_), func=AF.Copy,
                                                                        scale=pmisc[:, 36 + hh:37 + hh]), rd=[bpmisc, bC[hh * 4 + dt_]], wr=[bC[hh * 4 + dt_]])
                        A("dve", lambda h, hh=hh, dt_=dt_: h.scalar_tensor_tensor(
                            out=Cv(hh, dt_), in0=hn[:, dt_ * 512:(dt_ + 1) * 512], scalar=pmisc[:, 40 + hh:41 + hh], in1=Cv(hh, dt_),
                            op0=ALU.mult, op1=ALU.add), rd=[bhn, bpmisc, bC[hh * 4 + dt_]], wr=[bC[hh * 4 + dt_]])
                    A("dve", lambda h, hh=hh: h.tensor_scalar(out=nst[:, hh * 4:(hh + 1) * 4], in0=nst[:, hh * 4:(hh + 1) * 4],
                                                              scalar1=pmisc[:, 36 + hh:37 + hh], scalar2=None, op0=ALU.mult),
                      rd=[bpmisc, bnst], wr=[bnst])
                    A("dve", lambda h, hh=hh: h.scalar_tensor_tensor(
                        out=nst[:, hh * 4:(hh + 1) * 4], in0=misc[:, hh * 4:(hh + 1) * 4], scalar=pmisc[:, 40 + hh:41 + hh],
                        in1=nst[:, hh * 4:(hh + 1) * 4], op0=ALU.mult, op1=ALU.add), rd=[bmisc, bpmisc, bnst], wr=[bnst])

        def phase_B():
            A("act", lambda h: h.activation(out=halo[:, :, 0:3], in_=halo0[:, :, :], func=AF.Copy), rd=[bhalo0], wr=[bhalo])
            for i in range(NT):
                pr = i % 2
                l1_front((i + 1) * 128, 128, pr, 1, 128, True, "flag" if i == 0 else "copy",
                         conv_out=(convp[:, :], "convp") if i == NT - 1 else None,
                         pre=("B0" if i == 0 else "B1"), nxt=((i + 2) * 128 if i < NT - 1 else None))
                scan_chunk(0, 128, True, Cv, nst, bC, bnst, mst, bmst)
                banks = [bank(reserve=True) for _ in range(4)]
                hn_transpose(banks, 0, 128, 128)
                l1_tail(banks, 128, pr, yp[i * 128:(i + 1) * 128, :], "yp")
                for pb, _ in banks:
                    release(pb)
            A("sp", lambda h: h.dma_start(out=Cp[:, :].rearrange("(a p) e -> p a e", p=128), in_=Cst[:, :, :]), rd=bC, dma=1, key="Cp")
            A("sp", lambda h: h.dma_start(out=np_o[:, :], in_=nst[:, :]), rd=[bnst], dma=1, key="np")
            A("sp", lambda h: h.dma_start(out=mp_o[:, :], in_=mst[0:1, :]), rd=[bmst], dma=1, key="mp")

        import os
        KSTOP = int(os.environ.get('KSTOP', '9'))
        for _k, _f in enumerate([phase_w1, phase_A, phase_X, phase_S, phase_C, phase_B]):
            if KSTOP > _k:
                _f()
        P.emit(nc, st)
    return nc


_NC_CACHE = {}


def kernel(**inputs):
    f = lambda k: np.ascontiguousarray(np.asarray(inputs[k], dtype=np.float32))
    x_prompt, x_sample = f("x_prompt"), f("x_sample")
    stC_, stn_, stm_, stcv_ = f("state_mlstm_C"), f("state_mlstm_n"), f("state_mlstm_m"), f("state_mlstm_conv")

    def cols(v, n):
        return np.ascontiguousarray(v.reshape(n, 128).T)

    ngc = np.concatenate([cols(f("norm_g")[0], 8), cols(f("norm_g")[1], 8)], axis=1)
    cw = f("b_conv_w")[0]
    b_cw = np.ascontiguousarray(cw.reshape(4, 16, 128).transpose(2, 1, 0).reshape(128, 64))

    def bdiag(w):
        out = np.zeros((16, 128, 128), np.float32)
        wr = w.reshape(16, 32, 4, 4)
        for n in range(32):
            out[:, 4 * n:4 * n + 4, 4 * n:4 * n + 4] = wr[:, n]
        return out
    bds = [bdiag(f(k)[0]) for k in ("b_wq", "b_wk", "b_wv")]
    wbd = np.ascontiguousarray(np.concatenate(bds, 0).transpose(1, 0, 2).reshape(128, 48 * 128))
    wbdT = np.ascontiguousarray(np.concatenate(bds, 0).transpose(2, 0, 1).reshape(128, 48 * 128))
    wg = f("b_w_gates")[0]
    b_wg = np.ascontiguousarray(wg.reshape(48, 128, 8).transpose(1, 0, 2).reshape(128, 48 * 8))
    shared = {
        "ngc": ngc, "fng": f("final_norm_g").reshape(1, DM),
        "a_w_in": f("a_w_in")[0], "a_lng": cols(f("a_ln_g")[0], 16), "a_lnb": cols(f("a_ln_b")[0], 16),
        "a_lng_row": f("a_ln_g")[0].reshape(1, E), "a_lnb_row": f("a_ln_b")[0].reshape(1, E),
        "a_ws": f("a_w_s")[0].reshape(8 * 128, 128), "a_bs": f("a_b_s")[0].reshape(1, 8 * 128),
        "a_w_out": f("a_w_out")[0], "b_w_in": f("b_w_in")[0], "b_cw": b_cw, "b_cb": cols(f("b_conv_b")[0], 16),
        "wbd": wbd, "wbdT": wbdT, "b_wg": b_wg, "b_bg": f("b_b_gates")[0].reshape(1, 8),
        "b_hg": cols(f("b_hnorm_g")[0], 16), "b_skip": cols(f("b_skip")[0], 16), "b_w_out": f("b_w_out")[0],
    }
    in_maps = []
    for c in range(8):
        b, g = c // 4, c % 4
        xpc = np.zeros(((NT + 1) * 128, DM), np.float32)
        lo = g * 2048 - 128
        if g == 0:
            xpc[128:] = x_prompt[b, 0:2048]
        else:
            xpc[:] = x_prompt[b, lo:lo + (NT + 1) * 128]
        fl = np.zeros((1, 8), np.float32)
        fl[0, 0] = 0.0 if g == 0 else 1.0
        for r in range(3):
            fl[0, 1 + r] = 1.0 if r < g else 0.0
        sl = slice(4 * c, 4 * c + 4)
        m = dict(shared)
        m.update({
            "xp": xpc, "xs": np.ascontiguousarray(x_sample[sl].reshape(64, DM)),
            "stC": np.ascontiguousarray(stC_[0, sl].reshape(16 * 512, 512)),
            "stn": np.ascontiguousarray(stn_[0, sl].reshape(4, 4, 4, 128).transpose(0, 3, 1, 2).reshape(4, 128, 16)),
            "stm": np.ascontiguousarray(stm_[0, sl].reshape(4, 4)),
            "stconv": np.ascontiguousarray(stcv_[0, sl].reshape(12, E)),
            "flags": fl,
        })
        in_maps.append(m)
    if "nc" not in _NC_CACHE:
        _NC_CACHE["nc"] = build_program()
    res = run_bass_kernel_spmd(_NC_CACHE["nc"], in_maps, core_ids=list(range(8)))
    R = res.results
    y_prompt = np.stack([np.concatenate([R[b * 4 + g]["yp"] for g in range(4)], 0) for b in range(2)]).astype(np.float32)
    y_sample = np.concatenate([R[c]["ys"].reshape(4, 16, DM) for c in range(8)], 0).astype(np.float32)
    sgu_v = np.concatenate([R[c]["sguv"].reshape(4, 16, E) for c in range(8)], 0)[None].astype(np.float32)

    def n_from(a):
        return a.reshape(128, 4, 4).transpose(1, 2, 0).reshape(4, 512)
    C_prompt = np.stack([R[b * 4 + 3]["Cp"].reshape(4, 512, 512) for b in range(2)])[None].astype(np.float32)
    n_prompt = np.stack([n_from(R[b * 4 + 3]["np_o"]) for b in range(2)])[None].astype(np.float32)
    m_prompt = np.stack([R[b * 4 + 3]["mp_o"].reshape(4) for b in range(2)])[None].astype(np.float32)
    conv_prompt = np.stack([R[b * 4 + 3]["convp"].reshape(3, E) for b in range(2)])[None].astype(np.float32)
    C_sample = np.concatenate([R[c]["Cs"].reshape(4, 4, 512, 512) for c in range(8)], 0)[None].astype(np.float32)
    n_sample = np.concatenate([np.stack([n_from(R[c]["ns_o"][j]) for j in range(4)]) for c in range(8)], 0)[None].astype(np.float32)
    m_sample = np.concatenate([R[c]["ms_o"].reshape(4, 4) for c in range(8)], 0)[None].astype(np.float32)
    conv_sample = np.concatenate([R[c]["convs"].reshape(4, 3, E) for c in range(8)], 0)[None].astype(np.float32)
    return (y_prompt, y_sample, sgu_v, C_prompt, n_prompt, m_prompt, conv_prompt,
            C_sample, n_sample, m_sample, conv_sample)
```

```python
import numpy as np
import concourse.bass as bass
import concourse.mybir as mybir
from concourse.bass_utils import run_bass_kernel_spmd
from contextlib import ExitStack

F32 = mybir.dt.float32
BF16 = mybir.dt.bfloat16
AF = mybir.ActivationFunctionType
ALU = mybir.AluOpType
AX = mybir.AxisListType

NT = 16
DM = 1024
E = 2048
H = 4
DH = 512
RMS_EPS = 1e-6
LN_EPS = 1e-5
KSCALE = float(DH ** -0.5)


class Buf:
    __slots__ = ("name", "lw", "rd")

    def __init__(self, name):
        self.name = name
        self.lw = None
        self.rd = []


class Op:
    __slots__ = ("eng", "fn", "deps", "dma", "key", "done", "need_inc", "tag", "late")


class Prog:
    ENG = ["pe", "act", "dve", "pool", "sp"]

    def __init__(self, same_engine_sync=False):
        self.ops = {e: [] for e in self.ENG}
        self.same_engine_sync = same_engine_sync
        self.late_bufs = set()
        import os
        self.sync_engs = set(os.environ.get("KSYNC", "dve,act,pool").split(","))
        self.nbuf = 0
        import os
        self.limit = int(os.environ.get("KLIMIT", "100000000"))

    def buf(self, name=None):
        self.nbuf += 1
        return Buf(name or f"b{self.nbuf}")

    def add(self, eng, fn, rd=(), wr=(), dma=0, key=None, tag=None, cc=False):
        self.nadd = getattr(self, "nadd", 0) + 1
        if self.nadd > self.limit:
            return None
        if self.nadd == self.limit:
            import inspect
            fr = inspect.stack()[1]
            print("LAST OP", eng, fr.lineno, flush=True)
        op = Op()
        op.eng = eng
        op.fn = fn
        op.dma = dma
        op.key = key
        op.done = None
        op.need_inc = bool(dma)
        op.tag = tag
        op.late = any(id(b) in self.late_bufs for b in wr)
        deps = []
        for b in rd:
            if b.lw is not None:
                deps.append(b.lw)
        for b in wr:
            if b.lw is not None:
                deps.append(b.lw)
            deps.extend(b.rd)
        seen = set()
        dd = []
        for d in deps:
            if id(d) in seen or d is op:
                continue
            seen.add(id(d))
            if (not d.dma) and d.eng == eng and not dma:
                if eng == "pe" or not (self.same_engine_sync or d.late or eng in self.sync_engs):
                    continue
            dd.append(d)
        op.deps = dd
        for d in dd:
            d.need_inc = True
        for b in rd:
            b.rd.append(op)
        for b in wr:
            b.lw = op
            b.rd = []
        self.ops[eng].append(op)
        return op

    def emit(self, nc, stack):
        esem = {e: stack.enter_context(nc.semaphore(f"s_{e}")) for e in self.ENG}
        dsem = {}
        ecount = {e: 0 for e in self.ENG}
        dcount = {}
        for e in self.ENG:
            for op in self.ops[e]:
                if op.dma:
                    k = op.key
                    if k not in dsem:
                        dsem[k] = stack.enter_context(nc.semaphore(f"d_{len(dsem)}"))
                        dcount[k] = 0
        for e in self.ENG:
            for op in reversed(self.ops[e]):
                if not op.dma:
                    op.need_inc = True
                    break
        for e in self.ENG:
            for op in self.ops[e]:
                if op.dma == "cc":
                    dcount[op.key] += 1
                    op.done = (dsem[op.key], dcount[op.key])
                elif op.dma:
                    dcount[op.key] += 16 * int(op.dma)
                    op.done = (dsem[op.key], dcount[op.key])
                elif op.need_inc:
                    ecount[e] += 1
                    op.done = (esem[e], ecount[e])
        self.final = [(dsem[k], dcount[k]) for k in dsem] + [
            (esem[e], ecount[e]) for e in self.ENG if ecount[e] > 0]
        prog = self

        def run_engine(ename, h, extra_final=False):
            seen = {}
            for op in prog.ops[ename]:
                waits = {}
                for d in op.deps:
                    s, v = d.done
                    key = id(s)
                    if key not in waits or waits[key][1] < v:
                        waits[key] = (s, v)
                for key, (s, v) in waits.items():
                    if seen.get(key, 0) >= v:
                        continue
                    h.wait_ge(s, v)
                    seen[key] = v
                res = op.fn(h)
                if op.dma == "cc":
                    res.then_inc(op.done[0], 1)
                elif op.dma:
                    if not isinstance(res, (list, tuple)):
                        res = [res]
                    assert len(res) == int(op.dma), (op.tag, len(res), op.dma)
                    for r in res:
                        r.then_inc(op.done[0], 16)
                elif op.need_inc:
                    res.then_inc(op.done[0], 1)
            if extra_final:
                for s, v in prog.final:
                    if seen.get(id(s), 0) >= v:
                        continue
                    h.wait_ge(s, v)

        print("TOTAL OPS", getattr(self, "nadd", 0), flush=True)
        with nc.Block() as block:
            @block.tensor
            def _(h):
                run_engine("pe", h)

            @block.scalar
            def _(h):
                run_engine("act", h)

            @block.vector
            def _(h):
                run_engine("dve", h)

            @block.gpsimd
            def _(h):
                run_engine("pool", h)

            @block.sync
            def _(h):
                run_engine("sp", h, extra_final=True)


def build_program():
    nc = bass.Bass("TRN2", target_bir_lowering=False)
    P = Prog(same_engine_sync=False)
    st = ExitStack()

    def din(name, shape):
        return nc.dram_tensor(name, list(shape), F32, kind="ExternalInput").ap()

    def dout(name, shape):
        return nc.dram_tensor(name, list(shape), F32, kind="ExternalOutput").ap()

    xp = din("xp", [(NT + 1) * 128, DM])
    xs = din("xs", [64, DM])
    stC = din("stC", [16 * 512, 512])
    stn = din("stn", [4, 128, 16])
    stm = din("stm", [4, 4])
    stconv = din("stconv", [12, E])
    flags = din("flags", [1, 8])
    ngc = din("ngc", [128, 16])
    fng = din("fng", [1, DM])
    a_w_in = din("a_w_in", [DM, 3 * E])
    a_lng = din("a_lng", [128, 16])
    a_lnb = din("a_lnb", [128, 16])
    a_lng_row = din("a_lng_row", [1, E])
    a_lnb_row = din("a_lnb_row", [1, E])
    a_ws = din("a_ws", [8 * 128, 128])
    a_bs = din("a_bs", [1, 8 * 128])
    a_w_out = din("a_w_out", [E, DM])
    b_w_in = din("b_w_in", [DM, 2 * E])
    b_cw = din("b_cw", [128, 64])
    b_cb = din("b_cb", [128, 16])
    wbd = din("wbd", [128, 48 * 128])
    wbdT = din("wbdT", [128, 48 * 128])
    b_wg = din("b_wg", [128, 48 * 8])
    b_bg = din("b_bg", [1, 8])
    b_hg = din("b_hg", [128, 16])
    b_skip = din("b_skip", [128, 16])
    b_w_out = din("b_w_out", [E, DM])

    yp = dout("yp", [NT * 128, DM])
    ys = dout("ys", [64, DM])
    sguv = dout("sguv", [64, E])
    Cp = dout("Cp", [16 * 128, 512])
    np_o = dout("np_o", [128, 16])
    mp_o = dout("mp_o", [1, 4])
    convp = dout("convp", [3, E])
    Cs = dout("Cs", [4 * 16 * 128, 512])
    ns_o = dout("ns_o", [4, 128, 16])
    ms_o = dout("ms_o", [4, 4])
    convs = dout("convs", [12, E])

    x1s = nc.dram_tensor("x1s", [(NT + 1) * 128 + 64, DM], F32).ap()
    summ_in = [nc.dram_tensor(f"summ_in{i}", [512, 512], F32) for i in range(4)]
    summ_out = [nc.dram_tensor(f"summ_out{i}", [4 * 512, 512], F32) for i in range(4)]
    summ_in_m = nc.dram_tensor("summ_in_m", [128, 512], F32)
    summ_out_m = nc.dram_tensor("summ_out_m", [4 * 128, 512], F32)

    with st:
        def sb(name, shape, dt=F32):
            return st.enter_context(nc.sbuf_tensor(name, list(shape), dt))

        BIGW = sb("BIGW", [128, 65536], BF16)
        F32A = sb("F32A", [128, 8192], F32)
        Win0 = BIGW[:, 0:49152].rearrange("p (a c) -> p a c", a=8)
        Wout0 = BIGW[:, 49152:65536].rearrange("p (a c) -> p a c", a=16)
        Win1 = BIGW[:, 0:32768].rearrange("p (a c) -> p a c", a=8)
        Wout1 = BIGW[:, 32768:49152].rearrange("p (a c) -> p a c", a=16)
        SP = 49152
        xcT = BIGW[:, SP:SP + 2048].rearrange("p (a c) -> p a c", a=16)
        xmT = BIGW[:, SP + 2048:SP + 4096].rearrange("p (a c) -> p a c", a=16)
        qT = BIGW[:, SP + 4096:SP + 6144].rearrange("p (a c) -> p a c", a=16)
        kT = BIGW[:, SP + 6144:SP + 8192].rearrange("p (a c) -> p a c", a=16)
        kw = BIGW[:, SP + 8192:SP + 10240]
        vtok = BIGW[:, SP + 10240:SP + 12288]
        Cb0 = BIGW[:, SP + 12288:SP + 14336].rearrange("p (a c) -> p a c", a=4)
        Cb = [Cb0, Cb0]
        JUNK = BIGW[:, SP + 14336:SP + 16384].bitcast(F32)
        tmpZ_2 = BIGW[:, SP + 14336:SP + 15360].bitcast(F32)
        tmpS_2 = BIGW[:, SP + 15360:SP + 16384].bitcast(F32)
        bWin0, bWout0, bWin1, bWout1 = P.buf("Win0"), P.buf("Wout0"), P.buf("Win1"), P.buf("Wout1")
        bWin0u, bWin0z, bWin1z = P.buf("Win0u"), P.buf("Win0z"), P.buf("Win1z")
        bxcT, bxmT, bqT, bkT, bkw, bvtok = [P.buf(n) for n in ("xcT", "xmT", "qT", "kT", "kw", "vtok")]
        bxcTs = [P.buf(f"xcT{i}") for i in range(4)]
        bvtoks = [P.buf(f"vtok{i}") for i in range(4)]
        bxmTs = [P.buf(f"xmT{i}") for i in range(4)]
        bCb0 = P.buf("Cb0"); bCb = [bCb0, bCb0]
        v_f32 = F32A[:, 0:2048]
        biasT = F32A[:, 2048:4096].rearrange("p (a c) -> p a c", a=16)
        biasTs = F32A[:, 4096:5120].rearrange("p (a c) -> p a c", a=16)
        tmpU = F32A[:, 5120:5632]
        RSb = F32A[:, 6656:7680].rearrange("p (a c) -> p a c", a=8)
        Cst = F32A[:, 0:8192].rearrange("p (a c) -> p a c", a=16)
        bv_f32, bbiasT, bbiasTs, btmpU = P.buf("v_f32"), P.buf("biasT"), P.buf("biasTs"), P.buf("tmpU")
        bC = [P.buf(f"C{h}") for h in range(16)]

        Wbd_t = sb("Wbd", [128, 48 * 128], BF16); bWbd = P.buf("Wbd")
        Wbd = Wbd_t[:, :].rearrange("p (a c) -> p a c", a=48)
        vhat = Wbd_t[:, 0:E]; bvhat = P.buf("vhat")
        xt0 = sb("xt0", [128, DM]); bxt0 = P.buf("xt0")
        xsf = sb("xsf", [128, DM]); bxsf = P.buf("xsf")
        xt = [xt0, xsf]; bxt = [bxt0, bxsf]
        xnT = sb("xnT", [128, 8, 128], BF16); bxnT = P.buf("xnT"); bxnTs = [P.buf(f"xnT{i}") for i in range(8)]
        outT_1 = F32A[:, 5632:6656].bitcast(BF16).rearrange("p (a c) -> p a c", a=16)
        outT_2 = kw.rearrange("p (a c) -> p a c", a=16)
        boutT = bkw
        XH = sb("XH", [128, 16 * 132]); bxpad = P.buf("xpad"); bhn = bxpad
        xpad = XH[:, :].rearrange("p (a c) -> p a c", a=16)
        hn = XH[:, 0:E]
        halo = sb("halo", [128, 16, 12]); bhalo = P.buf("halo")
        halo0 = sb("halo0", [128, 16, 3]); bhalo0 = P.buf("halo0")
        xcf_t = sb("xcf", [128, 16 * 128]); bxcf = P.buf("xcf")
        bxcfc = [P.buf(f"xcf{i}") for i in range(16)]
        ALLX = [bxcf] + bxcfc
        dummy = sb("fence_t", [128, 2]); bdummy = P.buf("dummy")
        xcf = xcf_t[:, :].rearrange("p (a c) -> p a c", a=16)
        crow = xcf_t; bcrow = bxcf
        tmpZ_1 = Wbd_t[:, 2048:3072].bitcast(F32)
        tmpS_1 = Wbd_t[:, 3072:4096].bitcast(F32)
        btmpZ = P.buf("tmpZ"); btmpS = P.buf("tmpS")
        OV2 = sb("OV2", [128, DM]); bOV2 = P.buf("OV2")
        FGb = OV2; bFGb = bOV2
        identf = sb("identf", [128, 128]); bident = P.buf("ident")
        maskT = sb("maskT", [128, 128]); bmaskT = P.buf("maskT")
        onesf = sb("onesf", [128, 128]); bones = P.buf("ones")
        onesb = sb("onesb", [128, 4], BF16); bonesb = P.buf("onesb")
        WmT = OV2[:, 0:512].bitcast(BF16).rearrange("p (a c) -> p a c", a=8); bWmT = P.buf("WmT")
        WmTs = OV2[:, 512:768].bitcast(BF16).rearrange("p (a c) -> p a c", a=8); bWmTs = P.buf("WmTs")
        wtmp = OV2[:, 768:896]; bwtmp = P.buf("wtmp")
        wtmp2 = OV2[:, 896:1024]; bwtmp2 = P.buf("wtmp2")
        BSb = XH[:, 0:1024].rearrange("p (a c) -> p a c", a=8); bBSb = bxpad
        BSs = XH[:, 1024:1536].rearrange("p (a c) -> p a c", a=8); bBSs = bxpad
        cvec = sb("cvec", [128, 160]); bcvec = P.buf("cvec")
        flg = sb("flg", [128, 8]); bflg = P.buf("flg")
        Wg = sb("Wg", [128, 32, 8], BF16); bWg = P.buf("Wg")
        wgf = xcf_t[:, 0:384]; bwgf = bxcf
        bgb = sb("bgb", [128, 8]); bbgb = P.buf("bgb")
        sm = sb("sm", [128, 96]); bsm = P.buf("sm")
        nst = sb("nst", [128, 16]); bnst = P.buf("nst")
        mst = sb("mst", [128, 4]); bmst = P.buf("mst")
        Bacc = sb("Bacc", [128, 4]); bBacc = P.buf("Bacc")
        nbb = sb("nbb", [128, 4], BF16); bnbb = P.buf("nbb")
        Sw = sb("Sw", [128, 128], BF16); bSw = P.buf("Sw")
        diagA = tmpZ_2; bdiagA = btmpZ
        misc = tmpS_2; bmisc = btmpS
        pmisc = sb("pmisc", [128, 64]); bpmisc = P.buf("pmisc")
        stats = sb("stats", [128, 4, 6]); bstats = P.buf("stats")
        mv = sb("mv", [128, 4, 2]); bmv = P.buf("mv")

        for b_ in (bsm, bpmisc, bstats, bmv, bnbb, bmst, bnst, bBacc):
            P.late_bufs.add(id(b_))
        ps = [st.enter_context(nc.psum_tensor(f"ps{i}", [128, 512], F32)) for i in range(8)]
        bps = [P.buf(f"ps{i}") for i in range(8)]
        pctr = [0]

        reserved = set()

        def bank(reserve=False):
            while True:
                i = pctr[0] % 8
                pctr[0] += 1
                if i not in reserved:
                    break
            if reserve:
                reserved.add(i)
            return ps[i], bps[i]

        def release(pb):
            for i in range(8):
                if ps[i] is pb:
                    reserved.discard(i)

        A = P.add
        dq = [0]

        def dmaq():
            dq[0] += 1
            return "sp"

        A("pool", lambda h: h.memset(identf[:], 1.0), wr=[bident])
        A("pool", lambda h: h.affine_select(out=identf[:], in_=identf[:], pattern=[[-1, 128]],
                                            compare_op=ALU.is_equal, fill=0.0, base=0, channel_multiplier=1),
          rd=[bident], wr=[bident])
        A("pool", lambda h: h.memset(maskT[:], 1.0), wr=[bmaskT])
        A("pool", lambda h: h.affine_select(out=maskT[:], in_=maskT[:], pattern=[[1, 128]],
                                            compare_op=ALU.is_ge, fill=0.0, base=0, channel_multiplier=-1),
          rd=[bmaskT], wr=[bmaskT])
        A("pool", lambda h: h.memset(onesf[:], 1.0), wr=[bones])
        A("pool", lambda h: h.memset(onesb[:], 1.0), wr=[bonesb])
        def ld_cvec(h):
            r = []
            for i, (src, n) in enumerate([(ngc, 16), (a_lng, 16), (a_lnb, 16), (b_cb, 16), (b_hg, 16), (b_skip, 16)]):
                r.append(h.dma_start(out=cvec[:, i * 16:(i + 1) * 16], in_=src[:, :]))
            r.append(h.dma_start(out=cvec[:, 96:160], in_=b_cw[:, :]))
            r.append(h.dma_start(out=flg[:], in_=flags.partition_broadcast(128)))
            r.append(h.dma_start(out=BSb[:].rearrange("p a c -> p (a c)"), in_=a_bs.partition_broadcast(128)))
            r.append(h.dma_start(out=wgf[:], in_=b_wg[:, :]))
            return r
        A("sp", ld_cvec, wr=[bcvec, bflg, bBSb, bwgf], dma=10, key="cvec")
        A("sp", lambda h: h.dma_start(out=bgb[:], in_=b_bg.partition_broadcast(128)), wr=[bbgb], dma=1, key="bgb")
        NG0, NG1, LNG, LNB, CB, HG, SKIP, CW = 0, 8, 16, 32, 48, 64, 80, 96

        def ld_w(dst, src, nchunk, cols=None):
            a, ncol = dst.shape[1], dst.shape[2]
            lo, hi = cols if cols is not None else (0, ncol)
            pieces = [(i, c0) for i in range(a) for c0 in range(lo, hi, 2048)]

            def f(h):
                r = []
                sv = src.rearrange("(a p) c -> p a c", p=128)
                for i, c0 in pieces:
                    c1 = min(hi, c0 + 2048)
                    r.append(h.dma_start(out=dst[:, i, c0:c1], in_=sv[:, i, c0:c1]))
                return r
            f.n = len(pieces)
            return f
        f_ = ld_w(Win0, a_w_in, 8, (E, 2 * E)); A("pool", f_, wr=[bWin0], dma=f_.n, key="Win0v")
        f_ = ld_w(Win0, a_w_in, 8, (0, E)); A("pool", f_, wr=[bWin0u], dma=f_.n, key="Win0u")
        f_ = ld_w(Win0, a_w_in, 8, (2 * E, 3 * E)); A("pool", f_, wr=[bWin0z], dma=f_.n, key="Win0z")
        f_ = ld_w(Wout0, a_w_out, 4); A("pool", f_, wr=[bWout0], dma=f_.n, key="Wout0")

        def sgu_consts(Tt, Wdst, bWdst, BSsrc, bBSsrc, bdst, bbdst, sample):
            for g in range(8):
                if not sample:
                    A("sp", lambda h, g=g: h.dma_start(out=wtmp[:], in_=a_ws[g * 128:(g + 1) * 128, :]),
                      wr=[bwtmp], dma=1, key="wtmp")
                else:
                    A("pool", lambda h: h.memset(wtmp[:], 0.0), wr=[bwtmp])
                    A("sp", lambda h, g=g: [h.dma_start(out=wtmp[16 * j:16 * j + 16, 16 * j:16 * j + 16],
                                                        in_=a_ws[g * 128:g * 128 + 16, 0:16]) for j in range(4)],
                      wr=[bwtmp], dma=4, key="wtmp")
                pb, bpb = bank()
                A("pe", lambda h, pb=pb: h.matmul(pb[:Tt, :Tt], lhsT=wtmp[:Tt, :Tt], rhs=identf[:Tt, :Tt],
                                                  start=True, stop=True), rd=[bwtmp, bident], wr=[bpb])
                A("dve", lambda h, pb=pb: h.tensor_tensor(out=wtmp2[:Tt, :Tt], in0=pb[:Tt, :Tt], in1=maskT[:Tt, :Tt],
                                                          op=ALU.mult), rd=[bpb, bmaskT], wr=[bwtmp2])
                A("pool", lambda h, g=g: h.tensor_copy(out=Wdst[:Tt, g, :Tt], in_=wtmp2[:Tt, :Tt]),
                  rd=[bwtmp2], wr=[bWdst])
                pb2, bpb2 = bank()
                A("pe", lambda h, pb2=pb2: h.matmul(pb2[:, :Tt], lhsT=onesf[:Tt, :], rhs=wtmp2[:Tt, :Tt],
                                                    start=True, stop=True), rd=[bwtmp2, bones], wr=[bpb2])
                for i in range(2):
                    ct = 2 * g + i
                    A("dve", lambda h, pb2=pb2, ct=ct, g=g: h.scalar_tensor_tensor(
                        out=bdst[:, ct, :Tt], in0=pb2[:, :Tt], scalar=cvec[:, LNB + ct:LNB + ct + 1],
                        in1=BSsrc[:, g, :Tt], op0=ALU.mult, op1=ALU.add), rd=[bpb2, bcvec, bBSsrc], wr=[bbdst])

        A("pool", lambda h: h.tensor_copy(out=BSs[:].rearrange("p a (j t) -> p a j t", j=4),
                                          in_=BSb[:, :, 0:16].unsqueeze(2).to_broadcast([128, 8, 4, 16])),
          rd=[bBSb], wr=[bBSs])
        sgu_consts(128, WmT, bWmT, BSb, bBSb, biasT, bbiasT, False)
        sgu_consts(64, WmTs, bWmTs, BSs, bBSs, biasTs, bbiasTs, True)

        for ct in range(16):
            pb, bpb = bank()
            for qi in range(3):
                A("sp", lambda h, qi=qi, ct=ct: h.dma_start(
                    out=wtmp[:], in_=wbdT[:, (qi * 16 + ct) * 128:(qi * 16 + ct + 1) * 128]),
                  wr=[bwtmp], dma=1, key="wtmp")
                col = 0 if qi < 2 else 8
                A("pe", lambda h, pb=pb, qi=qi, ct=ct, col=col: h.matmul(
                    pb[:, col:col + 8], lhsT=wtmp[:], rhs=wgf[:, (qi * 16 + ct) * 8:(qi * 16 + ct + 1) * 8],
                    start=(qi != 1), stop=(qi != 0)), rd=[bwtmp, bwgf], wr=[bpb])
            A("dve", lambda h, pb=pb, ct=ct: h.tensor_copy(out=Wg[:, ct, :], in_=pb[:, 0:8]), rd=[bpb], wr=[bWg])
            A("dve", lambda h, pb=pb, ct=ct: h.tensor_copy(out=Wg[:, 16 + ct, :], in_=pb[:, 8:16]), rd=[bpb], wr=[bWg])

        def rms_pre_a(src_ap, Tt, par, rd, J, bJ):
            X, bX = xt[par], bxt[par]
            A("sp", lambda h: h.dma_start(out=X[:Tt, :], in_=src_ap), rd=list(rd), wr=[bX], dma=1, key=f"xt{par}")
            A("act", lambda h: h.activation(out=J[:Tt, :], in_=X[:Tt, :], func=AF.Square, accum_out=sm[:Tt, 0:1]),
              rd=[bX], wr=list(bJ) + [bsm])
            A("act", lambda h: h.activation(out=sm[:Tt, 1:2], in_=sm[:Tt, 0:1], func=AF.Sqrt, scale=1.0 / DM, bias=cvec_eps[:Tt, 0:1]),
              rd=[bsm, bceps], wr=[bsm])
            A("dve", lambda h: h.reciprocal(out=sm[:Tt, 2:3], in_=sm[:Tt, 1:2]), rd=[bsm], wr=[bsm])

        def rms_pre_b(Tt, par, S, bS):
            X, bX = xt[par], bxt[par]
            A("act", lambda h: h.activation(out=S[:Tt, :], in_=X[:Tt, :], func=AF.Copy, scale=sm[:Tt, 2:3]),
              rd=[bX, bsm], wr=[bS])

        def rms_pre(src_ap, Tt, par, rd, S, bS):
            rms_pre_a(src_ap, Tt, par, rd, S, [bS])
            rms_pre_b(Tt, par, S, bS)

        def rms_post(Tt, gcol, S, bS):
            for half in range(2):
                pb, bpb = bank()
                for i in range(4):
                    dt_ = half * 4 + i
                    A("pe", lambda h, pb=pb, i=i, dt_=dt_: h.matmul(
                        pb[:, i * 128:i * 128 + Tt], lhsT=S[:Tt, dt_ * 128:(dt_ + 1) * 128], rhs=identf[:Tt, :Tt],
                        start=True, stop=True), rd=[bS, bident], wr=[bpb])
                for i in range(4):
                    dt_ = half * 4 + i
                    A("act", lambda h, pb=pb, i=i, dt_=dt_: h.activation(
                        out=xnT[:, dt_, :Tt], in_=pb[:, i * 128:i * 128 + Tt], func=AF.Copy,
                        scale=cvec[:, gcol + dt_:gcol + dt_ + 1]), rd=[bpb, bcvec], wr=[bxnTs[dt_]])

        def rms_front(src_ap, Tt, par, gcol, rd=()):
            rms_pre(src_ap, Tt, par, rd, xt[1 - par], bxt[1 - par])
            rms_post(Tt, gcol, xt[1 - par], bxt[1 - par])

        cvec_eps = sb("cvec_eps", [128, 4]); bceps = P.buf("ceps")
        A("pool", lambda h: h.memset(cvec_eps[:, 0:1], RMS_EPS), wr=[bceps])
        A("pool", lambda h: h.memset(cvec_eps[:, 1:2], LN_EPS), wr=[bceps])
        A("pool", lambda h: h.memset(cvec_eps[:, 2:3], 1.0), wr=[bceps])

        def proj_cm(W, bW, col0, cg, Tt):
            pb, bpb = bank()
            for i in range(4):
                c0 = col0 + (cg * 4 + i) * 128
                for dt_ in range(8):
                    A("pe", lambda h, pb=pb, i=i, c0=c0, dt_=dt_: h.matmul(
                        pb[:, i * Tt:(i + 1) * Tt], lhsT=W[:, dt_, c0:c0 + 128], rhs=xnT[:, dt_, :Tt],
                        start=(dt_ == 0), stop=(dt_ == 7)), rd=[bW, bxnTs[dt_]], wr=[bpb])
            return pb, bpb

        def out_proj(W, bW, Tt, par, outT):
            X, bX = xt[par], bxt[par]
            for half in range(2):
                pb, bpb = bank()
                for ct in range(16):
                    A("pe", lambda h, pb=pb, ct=ct, half=half: h.matmul(
                        pb[:Tt, :], lhsT=outT[:, ct, :Tt], rhs=W[:, ct, half * 512:(half + 1) * 512],
                        start=(ct == 0), stop=(ct == 15)), rd=[boutT, bW], wr=[bpb])
                A("dve", lambda h, pb=pb, half=half: h.tensor_tensor(
                    out=X[:Tt, half * 512:(half + 1) * 512], in0=pb[:Tt, :], in1=X[:Tt, half * 512:(half + 1) * 512],
                    op=ALU.add), rd=[bpb, bX], wr=[bX])

        def layer0_tile(src_ap, dst_row, Tt, par, sample, pre=False, nxt=None):
            X, bX = xt[par], bxt[par]
            outT, tmpZ, tmpS = outT_1, tmpZ_1, tmpS_1
            S0 = v_f32[:, 0:1024]
            if not pre:
                rms_pre(src_ap, Tt, par, (), S0, bv_f32)
            rms_post(Tt, NG0, S0, bv_f32)
            Wm_, bWm_ = (WmTs, bWmTs) if sample else (WmT, bWmT)
            bt_, bbt_ = (biasTs, bbiasTs) if sample else (biasT, bbiasT)
            for cb in range(4):
                pb, bpb = bank()
                for dt_ in range(8):
                    A("pe", lambda h, pb=pb, cb=cb, dt_=dt_: h.matmul(
                        pb[:Tt, :], lhsT=xnT[:, dt_, :Tt], rhs=Win0[:, dt_, E + cb * 512:E + (cb + 1) * 512],
                        start=(dt_ == 0), stop=(dt_ == 7)), rd=[bxnTs[dt_], bWin0], wr=[bpb])
                A("act", lambda h, pb=pb, cb=cb: h.activation(out=v_f32[:Tt, cb * 512:(cb + 1) * 512], in_=pb[:Tt, :],
                                                              func=AF.Gelu_apprx_tanh), rd=[bpb], wr=[bv_f32])
            for cb in range(4):
                A("dve", lambda h, cb=cb: h.bn_stats(out=stats[:Tt, cb, :], in_=v_f32[:Tt, cb * 512:(cb + 1) * 512]),
                  rd=[bv_f32], wr=[bstats])
            A("dve", lambda h: h.bn_aggr(out=sm[:Tt, 82:84], in_=stats[:Tt, :, :].rearrange("p a c -> p (a c)")), rd=[bstats], wr=[bsm])
            A("dve", lambda h: h.tensor_copy(out=sm[:Tt, 84:85], in_=sm[:Tt, 83:84]), rd=[bsm], wr=[bsm])
            A("act", lambda h: h.activation(out=sm[:Tt, 4:5], in_=sm[:Tt, 84:85], func=AF.Sqrt, bias=cvec_eps[:Tt, 1:2]),
              rd=[bsm, bceps], wr=[bsm])
            A("dve", lambda h: h.reciprocal(out=sm[:Tt, 5:6], in_=sm[:Tt, 4:5]), rd=[bsm], wr=[bsm])
            A("dve", lambda h: h.tensor_scalar(out=vhat[:Tt, :], in0=v_f32[:Tt, :], scalar1=sm[:Tt, 82:83],
                                               scalar2=sm[:Tt, 5:6], op0=ALU.subtract, op1=ALU.mult),
              rd=[bv_f32, bsm], wr=[bvhat])
            if sample:
                A("dve", lambda h: h.tensor_scalar(out=v_f32[:Tt, :], in0=v_f32[:Tt, :], scalar1=sm[:Tt, 82:83],
                                                   scalar2=sm[:Tt, 5:6], op0=ALU.subtract, op1=ALU.mult),
                  rd=[bsm], wr=[bv_f32])
                A("sp", lambda h: h.dma_start(out=hn[:Tt, :], in_=a_lng_row.partition_broadcast(Tt)), wr=[bhn], dma=1, key="lnrow")
                A("dve", lambda h: h.tensor_tensor(out=v_f32[:Tt, :], in0=v_f32[:Tt, :], in1=hn[:Tt, :], op=ALU.mult),
                  rd=[bhn, bv_f32], wr=[bv_f32])
                A("sp", lambda h: h.dma_start(out=hn[:Tt, :], in_=a_lnb_row.partition_broadcast(Tt)), wr=[bhn], dma=1, key="lnrow")
                A("dve", lambda h: h.tensor_tensor(out=v_f32[:Tt, :], in0=v_f32[:Tt, :], in1=hn[:Tt, :], op=ALU.add),
                  rd=[bhn, bv_f32], wr=[bv_f32])
                A("sp", lambda h: h.dma_start(out=sguv[:, :], in_=v_f32[:Tt, :]), rd=[bv_f32], dma=1, key="sguv")
            if nxt is not None:
                rms_pre(nxt[0], nxt[1], 1 - par, (), S0, bv_f32)
            for cg in range(4):
                pu, bpu = proj_cm(Win0, bWin0u, 0, cg, Tt)
                pz, bpz = proj_cm(Win0, bWin0z, 2 * E, cg, Tt)
                pS, bpS = bank()
                for i in range(4):
                    ct = cg * 4 + i
                    A("pe", lambda h, pS=pS, i=i, ct=ct: h.matmul(
                        pS[:, i * Tt:(i + 1) * Tt], lhsT=vhat[:Tt, ct * 128:(ct + 1) * 128], rhs=Wm_[:Tt, ct // 2, :Tt],
                        start=True, stop=True), rd=[bvhat, bWm_], wr=[bpS])
                n4 = 4 * Tt
                A("act", lambda h, pu=pu: h.activation(out=tmpU[:, :n4], in_=pu[:, :n4], func=AF.Gelu_apprx_tanh),
                  rd=[bpu], wr=[btmpU])
                A("act", lambda h, pz=pz: h.activation(out=tmpZ[:, :n4], in_=pz[:, :n4], func=AF.Silu),
                  rd=[bpz], wr=[btmpZ])
                for i in range(4):
                    ct = cg * 4 + i
                    A("dve", lambda h, pS=pS, i=i, ct=ct: h.scalar_tensor_tensor(
                        out=tmpS[:, i * Tt:(i + 1) * Tt], in0=pS[:, i * Tt:(i + 1) * Tt],
                        scalar=cvec[:, LNG + ct:LNG + ct + 1], in1=bt_[:, ct, :Tt], op0=ALU.mult, op1=ALU.add),
                      rd=[bpS, bcvec, bbt_], wr=[btmpS])
                A("dve", lambda h: h.tensor_tensor(out=tmpS[:, :n4], in0=tmpS[:, :n4], in1=tmpU[:, :n4], op=ALU.mult),
                  rd=[btmpU, btmpS], wr=[btmpS])
                A("dve", lambda h, cg=cg: h.tensor_tensor(
                    out=outT[:, cg * 4:(cg + 1) * 4, :Tt], in0=tmpS[:, :n4].rearrange("p (a t) -> p a t", a=4),
                    in1=tmpZ[:, :n4].rearrange("p (a t) -> p a t", a=4), op=ALU.mult),
                  rd=[btmpS, btmpZ], wr=[boutT])
            out_proj(Wout0, bWout0, Tt, par, outT)
            A("sp", lambda h: h.dma_start(out=x1s[dst_row:dst_row + Tt, :], in_=X[:Tt, :]), rd=[bX], wr=[bx1s],
              dma=1, key=f"x1st{par}")

        bx1s = P.buf("x1s")

        def l1_front(src_row, Tt, par, nstr, L, need_q, halo_mode, conv_out=None, pre=None, nxt=None, halo_only=False):
            if pre is None:
                rms_front(x1s[src_row:src_row + Tt, :], Tt, par, NG1, rd=[bx1s])
            elif pre == "B0":
                rms_pre(x1s[src_row:src_row + Tt, :], Tt, par, [bx1s], hn[:, 0:1024], bxpad)
                rms_post(Tt, NG1, hn[:, 0:1024], bxpad)
            elif pre == "B1":
                rms_pre_b(Tt, par, hn[:, 0:1024], bxpad)
                rms_post(Tt, NG1, hn[:, 0:1024], bxpad)
            else:
                if not pre:
                    rms_pre(x1s[src_row:src_row + Tt, :], Tt, par, [bx1s], hn[:, 0:1024], bxpad)
                rms_post(Tt, NG1, hn[:, 0:1024], bxpad)
            A("dve", lambda h: h.memset(dummy[:, 0:1], 0.0), wr=ALLX + [bdummy])
            xp4 = xpad[:, :, 0:nstr * (L + 3)].rearrange("p a (j t) -> p a j t", j=nstr)
            if halo_mode == "zero":
                A("pool", lambda h: h.memset(xp4[:, :, :, 0:3], 0.0), wr=[bxpad])
            elif halo_mode == "copy":
                A("act", lambda h: h.activation(out=xp4[:, :, :, 0:3], in_=halo[:, :, 0:3].unsqueeze(2), func=AF.Copy), rd=[bhalo], wr=[bxpad])
            elif halo_mode == "flag":
                A("act", lambda h: h.activation(out=xp4[:, :, :, 0:3], in_=halo[:, :, 0:3].unsqueeze(2), func=AF.Copy,
                                                scale=flg[:, 0:1]), rd=[bhalo, bflg], wr=[bxpad])
            elif halo_mode == "sample":
                A("sp", lambda h: h.dma_start(out=crow[:12, :], in_=stconv[:, :]), wr=[bcrow], dma=1, key="crow")
                pb, bpb = bank()
                for ct in range(16):
                    A("pe", lambda h, pb=pb, ct=ct: h.matmul(pb[:, ct * 12:(ct + 1) * 12], lhsT=crow[:12, ct * 128:(ct + 1) * 128],
                                                             rhs=identf[:12, :12], start=True, stop=True),
                      rd=[bcrow, bident], wr=[bpb])
                A("act", lambda h, pb=pb: h.activation(out=xp4[:, :, :, 0:3],
                                                       in_=pb[:, 0:192].rearrange("p (a j t) -> p a j t", a=16, j=4),
                                                       func=AF.Copy), rd=[bpb], wr=[bxpad])
            for cg in range(4):
                pb, bpb = proj_cm(Win1, bWin1, 0, cg, Tt)
                A("act", lambda h, pb=pb, cg=cg: h.activation(
                    out=xp4[:, cg * 4:(cg + 1) * 4, :, 3:3 + L],
                    in_=pb[:, :4 * Tt].rearrange("p (a j t) -> p a j t", a=4, j=nstr), func=AF.Copy), rd=[bpb], wr=[bxpad])
                A("act", lambda h, cg=cg: h.activation(
                    out=xmT[:, cg * 4:(cg + 1) * 4, :Tt].rearrange("p a (j t) -> p a j t", j=nstr),
                    in_=xp4[:, cg * 4:(cg + 1) * 4, :, 3:3 + L], func=AF.Copy), rd=[bxpad], wr=[bxmTs[cg]])
            A("act", lambda h: h.activation(out=halo[:, :, 0:3 * nstr].rearrange("p a (j t) -> p a j t", j=nstr),
                                            in_=xp4[:, :, :, L:L + 3], func=AF.Copy), rd=[bxpad], wr=[bhalo])
            if conv_out is not None:
                conv_tail_out(nstr, 3 * nstr, conv_out[0], conv_out[1])
            if halo_only:
                A("act", lambda h: h.activation(out=halo0[:, :, :], in_=halo[:, :, 0:3], func=AF.Copy), rd=[bhalo], wr=[bhalo0])
                if nxt is not None:
                    rms_pre(x1s[nxt:nxt + 128, :], 128, 1 - par, [bx1s], hn[:, 0:1024], bxpad)
                return
            xc4 = xcf[:, :, :Tt].rearrange("p a (j t) -> p a j t", j=nstr)
            A("dve", lambda h: h.memset(dummy[:, 1:2], 0.0), wr=ALLX + [bdummy])
            for ct in range(16):
                A("act", lambda h, ct=ct: h.activation(
                    out=xc4[:, ct], in_=xp4[:, ct, :, 0:L], func=AF.Identity, scale=cvec[:, CW + ct * 4:CW + ct * 4 + 1],
                    bias=cvec[:, CB + ct:CB + ct + 1]), rd=[bxpad, bcvec], wr=[bxcfc[ct]])
            for ct_, j in [(c_, j_) for hf in range(2) for j_ in range(1, 4) for c_ in range(hf * 8, hf * 8 + 8)]:
                for ct in (ct_,):
                    A("dve", lambda h, ct=ct, j=j: h.scalar_tensor_tensor(
                        out=xc4[:, ct], in0=xp4[:, ct, :, j:j + L], scalar=cvec[:, CW + ct * 4 + j:CW + ct * 4 + j + 1],
                        in1=xc4[:, ct], op0=ALU.mult, op1=ALU.add), rd=[bxpad, bcvec, bxcfc[ct]], wr=[bxcfc[ct]])
            for cg in range(4):
                A("act", lambda h, cg=cg: h.activation(out=xcf[:, cg * 4:(cg + 1) * 4, :Tt], in_=xcf[:, cg * 4:(cg + 1) * 4, :Tt],
                                                       func=AF.Silu), rd=bxcfc[cg * 4:(cg + 1) * 4], wr=bxcfc[cg * 4:(cg + 1) * 4])
            for cg in range(4):
                A("act", lambda h, cg=cg: h.activation(out=xcT[:, cg * 4:(cg + 1) * 4, :Tt], in_=xcf[:, cg * 4:(cg + 1) * 4, :Tt],
                                                       func=AF.Copy), rd=bxcfc[cg * 4:(cg + 1) * 4], wr=[bxcTs[cg]])
            if nxt is not None and pre in ("B0", "B1"):
                rms_pre_a(x1s[nxt:nxt + 128, :], 128, 1 - par, [bx1s], JUNK, [btmpZ, btmpS])
            elif nxt is not None:
                rms_pre(x1s[nxt:nxt + 128, :], 128, 1 - par, [bx1s], hn[:, 0:1024], bxpad)
            if need_q:
                for qi, (dst, bdst, scl) in enumerate([(qT, bqT, 1.0), (kT, bkT, KSCALE)]):
                    for cg in range(4):
                        pb, bpb = bank()
                        for i in range(4):
                            ct = cg * 4 + i
                            A("pe", lambda h, pb=pb, i=i, ct=ct, qi=qi: h.matmul(
                                pb[:, i * Tt:(i + 1) * Tt], lhsT=Wbd[:, qi * 16 + ct, :], rhs=xcT[:, ct, :Tt],
                                start=True, stop=True), rd=[bWbd, bxcTs[ct // 4]], wr=[bpb])
                        A("act", lambda h, pb=pb, cg=cg, dst=dst, scl=scl: h.activation(
                            out=dst[:, cg * 4:(cg + 1) * 4, :Tt], in_=pb[:, :4 * Tt].rearrange("p (a t) -> p a t", a=4),
                            func=AF.Copy, scale=scl), rd=[bpb], wr=[bdst])
                A("dve", lambda h: h.tensor_tensor(out=xcf[:, :, :Tt], in0=xcf[:, :, :Tt],
                                                    in1=cvec[:, SKIP:SKIP + 16].unsqueeze(2).to_broadcast([128, 16, Tt]),
                                                    op=ALU.mult), rd=bxcfc + [bcvec], wr=bxcfc)

        cbpar = [0]

        def scan_chunk(off, L, full, Cview, nview, bCl, bnl, mview, bml, acc_B=False):
            mview = mview[:, :]
            pg, bpg = bank()
            for ct in range(16):
                A("pe", lambda h, pg=pg, ct=ct: h.matmul(pg[:L, 0:8], lhsT=xcT[:, ct, off:off + L], rhs=Wg[:, ct, :],
                                                         start=(ct == 0), stop=False), rd=[bxcTs[ct // 4], bWg], wr=[bpg])
            for ct in range(16):
                A("pe", lambda h, pg=pg, ct=ct: h.matmul(pg[:L, 0:8], lhsT=xmT[:, ct, off:off + L], rhs=Wg[:, 16 + ct, :],
                                                         start=False, stop=(ct == 15)), rd=[bxmTs[ct // 4], bWg], wr=[bpg])
            A("dve", lambda h, pg=pg: h.tensor_tensor(out=sm[:L, 8:16], in0=pg[:L, 0:8], in1=bgb[:L, :], op=ALU.add),
              rd=[bpg, bbgb], wr=[bsm])
            A("act", lambda h: h.activation(out=sm[:L, 16:20], in_=sm[:L, 12:16], func=AF.Abs), rd=[bsm], wr=[bsm])
            A("act", lambda h: h.activation(out=sm[:L, 20:24], in_=sm[:L, 16:20], func=AF.Exp, scale=-1.0), rd=[bsm], wr=[bsm])
            A("act", lambda h: h.activation(out=sm[:L, 24:28], in_=sm[:L, 20:24], func=AF.Ln, bias=cvec_eps[:L, 2:3]),
              rd=[bsm, bceps], wr=[bsm])
            A("dve", lambda h: h.tensor_single_scalar(out=sm[:L, 28:32], in_=sm[:L, 12:16], scalar=0.0, op=ALU.min),
              rd=[bsm], wr=[bsm])
            A("dve", lambda h: h.tensor_tensor(out=sm[:L, 28:32], in0=sm[:L, 28:32], in1=sm[:L, 24:28], op=ALU.subtract),
              rd=[bsm], wr=[bsm])
            pc, bpc = bank()
            A("pe", lambda h, pc=pc: h.matmul(pc[:L, 0:4], lhsT=maskT[:L, :L], rhs=sm[:L, 28:32], start=True, stop=True),
              rd=[bmaskT, bsm], wr=[bpc])
            A("pe", lambda h, pc=pc: h.matmul(pc[:, 8:12], lhsT=onesf[:L, :], rhs=sm[:L, 28:32], start=True, stop=True),
              rd=[bones, bsm], wr=[bpc])
            A("dve", lambda h, pc=pc: h.tensor_copy(out=sm[:L, 72:76], in_=pc[:L, 0:4]), rd=[bpc], wr=[bsm])
            A("dve", lambda h: h.tensor_tensor(out=sm[:L, 32:36], in0=sm[:L, 8:12], in1=sm[:L, 72:76], op=ALU.subtract),
              rd=[bsm], wr=[bsm])
            A("dve", lambda h: h.tensor_tensor(
                out=diagA[:L, :4 * L].rearrange("p (a t) -> p a t", a=4),
                in0=identf[:L, :L].unsqueeze(1).to_broadcast([L, 4, L]),
                in1=sm[:L, 32:36].unsqueeze(2).to_broadcast([L, 4, L]), op=ALU.mult), rd=[bident, bsm], wr=[bdiagA])
            pa, bpa = bank()
            A("pe", lambda h, pa=pa: h.matmul(pa[:, :4 * L], lhsT=onesf[:L, :], rhs=diagA[:L, :4 * L], start=True, stop=True),
              rd=[bones, bdiagA], wr=[bpa])
            A("dve", lambda h, pa=pa: h.tensor_reduce(out=pmisc[:, 0:4], in_=pa[:, :4 * L].rearrange("p (a t) -> p a t", a=4),
                                                      axis=AX.X, op=ALU.max), rd=[bpa], wr=[bpmisc])
            A("dve", lambda h: h.tensor_tensor(out=pmisc[:, 4:8], in0=pmisc[:, 0:4], in1=mview, op=ALU.max),
              rd=[bpmisc, bml], wr=[bpmisc])
            A("dve", lambda h: h.tensor_tensor(out=pmisc[:, 12:16], in0=mview, in1=pmisc[:, 4:8], op=ALU.subtract),
              rd=[bpmisc, bml], wr=[bpmisc])
            A("act", lambda h: h.activation(out=pmisc[:, 8:12], in_=pmisc[:, 12:16], func=AF.Exp), rd=[bpmisc], wr=[bpmisc])
            A("dve", lambda h, pc=pc: h.tensor_copy(out=pmisc[:, 16:20], in_=pc[:, 8:12]), rd=[bpc], wr=[bpmisc])
            A("dve", lambda h: h.tensor_tensor(out=mview, in0=pmisc[:, 16:20], in1=pmisc[:, 4:8], op=ALU.add),
              rd=[bpmisc], wr=[bml])
            if acc_B:
                A("dve", lambda h: h.tensor_tensor(out=Bacc[:, :], in0=Bacc[:, :], in1=pmisc[:, 16:20], op=ALU.add),
                  rd=[bpmisc, bBacc], wr=[bBacc])
            A("dve", lambda h: h.tensor_tensor(out=sm[:L, 64:68], in0=sm[:L, 32:36], in1=pmisc[:L, 4:8], op=ALU.subtract),
              rd=[bsm, bpmisc], wr=[bsm])
            A("act", lambda h: h.activation(out=sm[:L, 44:48], in_=sm[:L, 64:68], func=AF.Exp), rd=[bsm], wr=[bsm])
            A("dve", lambda h: h.tensor_single_scalar(out=sm[:L, 52:56], in_=sm[:L, 44:48], scalar=KSCALE, op=ALU.mult),
              rd=[bsm], wr=[bsm])
            if full:
                A("dve", lambda h: h.tensor_tensor(out=sm[:L, 64:68], in0=sm[:L, 72:76], in1=pmisc[:L, 4:8], op=ALU.add),
                  rd=[bsm, bpmisc], wr=[bsm])
                A("act", lambda h: h.activation(out=sm[:L, 48:52], in_=sm[:L, 64:68], func=AF.Exp, scale=-1.0), rd=[bsm], wr=[bsm])
            for hh in range(4):
                pk, bpk = bank()
                for i in range(4):
                    ct = hh * 4 + i
                    A("pe", lambda h, pk=pk, i=i, ct=ct: h.matmul(pk[:L, i * 128:(i + 1) * 128], lhsT=xcT[:, ct, off:off + L],
                                                                  rhs=Wbd[:, 16 + ct, :], start=True, stop=True),
                      rd=[bxcTs[ct // 4], bWbd], wr=[bpk])
                A("act", lambda h, pk=pk, hh=hh: h.activation(out=kw[:L, hh * 512:(hh + 1) * 512], in_=pk[:L, :], func=AF.Copy,
                                                              scale=sm[:L, 52 + hh:53 + hh]), rd=[bpk, bsm], wr=[bkw])
                pv, bpv = bank()
                for i in range(4):
                    ct = hh * 4 + i
                    A("pe", lambda h, pv=pv, i=i, ct=ct: h.matmul(pv[:L, i * 128:(i + 1) * 128], lhsT=xmT[:, ct, off:off + L],
                                                                  rhs=Wbd[:, 32 + ct, :], start=True, stop=True),
                      rd=[bxmTs[ct // 4], bWbd], wr=[bpv])
                A("dve", lambda h, pv=pv, hh=hh: h.tensor_copy(out=vtok[:L, hh * 512:(hh + 1) * 512], in_=pv[:L, :]),
                  rd=[bpv], wr=[bvtoks[hh]])
            pden, bpden = (None, None)
            if full:
                pden, bpden = bank(reserve=True)
            pn, bpn = bank(reserve=True)
            for hh in range(4):
                c0col = pmisc[:, 8 + hh:9 + hh]
                if full:
                    par = cbpar[0] % 2
                    cbpar[0] += 1
                    CB_, bCB_ = Cb[par], bCb[par]
                    for dt_ in range(4):
                        A("act", lambda h, hh=hh, dt_=dt_, CB_=CB_, c0col=c0col: h.activation(
                            out=CB_[:, dt_, :], in_=Cview(hh, dt_), func=AF.Copy, scale=c0col),
                          rd=[bCl[hh * 4 + dt_], bpmisc], wr=[bCB_])
                    A("dve", lambda h, hh=hh, c0col=c0col: h.tensor_scalar(out=nbb[:, :], in0=nview[:, hh * 4:(hh + 1) * 4],
                                                                           scalar1=c0col, scalar2=None, op0=ALU.mult),
                      rd=[bnl, bpmisc], wr=[bnbb])
                    pst, bpst = bank()
                    for dt_ in range(4):
                        ct = hh * 4 + dt_
                        A("pe", lambda h, pst=pst, ct=ct, dt_=dt_: h.matmul(
                            pst[:L, :L], lhsT=kT[:, ct, off:off + L], rhs=qT[:, ct, off:off + L],
                            start=(dt_ == 0), stop=(dt_ == 3)), rd=[bkT, bqT], wr=[bpst])
                    A("dve", lambda h, pst=pst, hh=hh: h.scalar_tensor_tensor(
                        out=Sw[:L, :L], in0=pst[:L, :L], scalar=sm[:L, 44 + hh:45 + hh], in1=maskT[:L, :L],
                        op0=ALU.mult, op1=ALU.mult), rd=[bpst, bsm, bmaskT], wr=[bSw])
                    pnum, bpnum = bank()
                    A("pe", lambda h, pnum=pnum, hh=hh: h.matmul(pnum[:L, :], lhsT=Sw[:L, :L], rhs=vtok[:L, hh * 512:(hh + 1) * 512],
                                                                 start=True, stop=False), rd=[bSw, bvtoks[hh]], wr=[bpnum])
                    for dt_ in range(4):
                        ct = hh * 4 + dt_
                        A("pe", lambda h, pnum=pnum, ct=ct, dt_=dt_, CB_=CB_: h.matmul(
                            pnum[:L, :], lhsT=qT[:, ct, off:off + L], rhs=CB_[:, dt_, :], start=False, stop=(dt_ == 3)),
                          rd=[bqT, bCB_], wr=[bpnum])
                    A("pe", lambda h, hh=hh: h.matmul(pden[:L, hh:hh + 1], lhsT=Sw[:L, :L], rhs=onesb[:L, 0:1],
                                                      start=True, stop=False), rd=[bSw, bonesb], wr=[bpden])
                    for dt_ in range(4):
                        ct = hh * 4 + dt_
                        A("pe", lambda h, hh=hh, ct=ct, dt_=dt_: h.matmul(
                            pden[:L, hh:hh + 1], lhsT=qT[:, ct, off:off + L], rhs=nbb[:, dt_:dt_ + 1],
                            start=False, stop=(dt_ == 3)), rd=[bqT, bnbb], wr=[bpden])
                    A("act", lambda h, pnum=pnum, hh=hh: h.activation(out=hn[:L, hh * 512:(hh + 1) * 512], in_=pnum[:L, :],
                                                                      func=AF.Copy), rd=[bpnum], wr=[bhn])
                for dt_ in range(4):
                    ct = hh * 4 + dt_
                    pu_, bpu_ = bank()
                    A("pe", lambda h, pu_=pu_, ct=ct, hh=hh: h.matmul(
                        pu_[:, :], lhsT=kw[:L, ct * 128:(ct + 1) * 128], rhs=vtok[:L, hh * 512:(hh + 1) * 512],
                        start=True, stop=True), rd=[bkw, bvtoks[hh]], wr=[bpu_])
                    A("dve", lambda h, pu_=pu_, hh=hh, dt_=dt_, c0col=c0col: h.scalar_tensor_tensor(
                        out=Cview(hh, dt_), in0=Cview(hh, dt_), scalar=c0col, in1=pu_[:, :], op0=ALU.mult, op1=ALU.add),
                      rd=[bpu_, bpmisc, bCl[hh * 4 + dt_]], wr=[bCl[hh * 4 + dt_]])
                    A("pe", lambda h, ct=ct: h.matmul(pn[:, ct:ct + 1], lhsT=kw[:L, ct * 128:(ct + 1) * 128], rhs=onesb[:L, 0:1],
                                                      start=True, stop=True), rd=[bkw, bonesb], wr=[bpn])
                A("dve", lambda h, hh=hh, c0col=c0col: h.scalar_tensor_tensor(
                    out=nview[:, hh * 4:(hh + 1) * 4], in0=nview[:, hh * 4:(hh + 1) * 4], scalar=c0col,
                    in1=pn[:, hh * 4:(hh + 1) * 4], op0=ALU.mult, op1=ALU.add), rd=[bpn, bpmisc, bnl], wr=[bnl])
            release(pn)
            if full:
                release(pden)
                A("act", lambda h: h.activation(out=sm[:L, 56:60], in_=pden[:L, 0:4], func=AF.Abs), rd=[bpden], wr=[bsm])
                A("dve", lambda h: h.tensor_tensor(out=sm[:L, 56:60], in0=sm[:L, 56:60], in1=sm[:L, 48:52], op=ALU.max),
                  rd=[bsm], wr=[bsm])
                A("dve", lambda h: h.reciprocal(out=sm[:L, 60:64], in_=sm[:L, 56:60]), rd=[bsm], wr=[bsm])
                for hh in range(4):
                    A("dve", lambda h, hh=hh: h.bn_stats(out=stats[:L, hh, :], in_=hn[:L, hh * 512:(hh + 1) * 512]),
                      rd=[bhn], wr=[bstats])
                    A("dve", lambda h, hh=hh: h.bn_aggr(out=mv[:L, hh, :], in_=stats[:L, hh, :]), rd=[bstats], wr=[bmv])
                A("dve", lambda h: h.tensor_tensor(out=sm[:L, 64:68], in0=sm[:L, 60:64], in1=sm[:L, 60:64], op=ALU.mult),
                  rd=[bsm], wr=[bsm])
                A("dve", lambda h: h.tensor_tensor(out=sm[:L, 64:68], in0=sm[:L, 64:68], in1=mv[:L, :, 1], op=ALU.mult),
                  rd=[bsm, bmv], wr=[bsm])
                A("act", lambda h: h.activation(out=sm[:L, 64:68], in_=sm[:L, 64:68], func=AF.Sqrt, bias=cvec_eps[:L, 1:2]),
                  rd=[bsm, bceps], wr=[bsm])
                A("dve", lambda h: h.reciprocal(out=sm[:L, 68:72], in_=sm[:L, 64:68]), rd=[bsm], wr=[bsm])
                A("dve", lambda h: h.tensor_tensor(out=sm[:L, 68:72], in0=sm[:L, 68:72], in1=sm[:L, 60:64], op=ALU.mult),
                  rd=[bsm], wr=[bsm])
                for hh in range(4):
                    A("dve", lambda h, hh=hh: h.tensor_scalar(
                        out=hn[:L, hh * 512:(hh + 1) * 512], in0=hn[:L, hh * 512:(hh + 1) * 512], scalar1=mv[:L, hh, 0:1],
                        scalar2=sm[:L, 68 + hh:69 + hh], op0=ALU.subtract, op1=ALU.mult), rd=[bhn, bmv, bsm], wr=[bhn])

        def hn_transpose(banks, off, L, Tt):
            for cg in range(4):
                pb, bpb = banks[cg]
                for i in range(4):
                    ct = cg * 4 + i
                    A("pe", lambda h, pb=pb, i=i, ct=ct: h.matmul(pb[:, i * Tt + off:i * Tt + off + L],
                                                                  lhsT=hn[:L, ct * 128:(ct + 1) * 128], rhs=identf[:L, :L],
                                                                  start=True, stop=True), rd=[bhn, bident], wr=[bpb])

        def l1_tail(banks, Tt, par, out_ap, key):
            X, bX = xt[par], bxt[par]
            xsf, bxsf = xt[1 - par], bxt[1 - par]
            outT, tmpZ, tmpS = outT_2, tmpZ_2, tmpS_2
            n4 = 4 * Tt
            for cg in range(4):
                pz, bpz = proj_cm(Win1, bWin1z, E, cg, Tt)
                A("act", lambda h, pz=pz: h.activation(out=tmpZ[:, :n4], in_=pz[:, :n4], func=AF.Silu), rd=[bpz], wr=[btmpZ])
                pb, bpb = banks[cg]
                for i in range(4):
                    ct = cg * 4 + i
                    A("dve", lambda h, pb=pb, i=i, ct=ct: h.scalar_tensor_tensor(
                        out=tmpS[:, i * Tt:(i + 1) * Tt], in0=pb[:, i * Tt:(i + 1) * Tt], scalar=cvec[:, HG + ct:HG + ct + 1],
                        in1=xcf[:, ct, :Tt], op0=ALU.mult, op1=ALU.add), rd=[bpb, bcvec, bxcfc[ct]], wr=[btmpS])
                A("dve", lambda h, cg=cg: h.tensor_tensor(
                    out=outT[:, cg * 4:(cg + 1) * 4, :Tt], in0=tmpS[:, :n4].rearrange("p (a t) -> p a t", a=4),
                    in1=tmpZ[:, :n4].rearrange("p (a t) -> p a t", a=4), op=ALU.mult), rd=[btmpS, btmpZ], wr=[boutT])
            out_proj(Wout1, bWout1, Tt, par, outT)
            A("act", lambda h: h.activation(out=JUNK[:Tt, :], in_=X[:Tt, :], func=AF.Square, accum_out=sm[:Tt, 86:87]),
              rd=[bX], wr=[btmpZ, btmpS, bsm])
            A("act", lambda h: h.activation(out=sm[:Tt, 87:88], in_=sm[:Tt, 86:87], func=AF.Sqrt, scale=1.0 / DM, bias=cvec_eps[:Tt, 0:1]),
              rd=[bsm, bceps], wr=[bsm])
            A("dve", lambda h: h.reciprocal(out=sm[:Tt, 88:89], in_=sm[:Tt, 87:88]), rd=[bsm], wr=[bsm])
            A("dve", lambda h: h.scalar_tensor_tensor(out=X[:Tt, :], in0=X[:Tt, :], scalar=sm[:Tt, 88:89], in1=FGb[:Tt, :],
                                                      op0=ALU.mult, op1=ALU.mult), rd=[bX, bsm, bFGb], wr=[bX])
            A("sp", lambda h: h.dma_start(out=out_ap, in_=X[:Tt, :]), rd=[bX], dma=1, key=key)

        def conv_tail_out(nstr, nrows, out_ap, key):
            for q in range(4):
                pb, bpb = bank()
                for i in range(4):
                    ct = q * 4 + i
                    A("pe", lambda h, pb=pb, i=i, ct=ct: h.matmul(pb[:nrows, i * 128:(i + 1) * 128], lhsT=halo[:, ct, 0:nrows],
                                                                  rhs=identf[:, :], start=True, stop=True),
                      rd=[bhalo, bident], wr=[bpb])
                A("dve", lambda h, pb=pb, q=q: h.tensor_copy(out=crow[:nrows, q * 512:(q + 1) * 512], in_=pb[:nrows, :]),
                  rd=[bpb], wr=[bcrow])
            A("sp", lambda h: h.dma_start(out=out_ap, in_=crow[:nrows, :]), rd=[bcrow], dma=1, key=key)

        layer0_tile(xs[:, :], (NT + 1) * 128, 64, 0, True, pre=False, nxt=(xp[0:128, :], 128))
        for i in range(NT + 1):
            nx = (xp[(i + 1) * 128:(i + 2) * 128, :], 128) if i < NT else None
            layer0_tile(xp[i * 128:(i + 1) * 128, :], i * 128, 128, (i + 1) % 2, False, pre=True, nxt=nx)

        def zero_state(extra=()):
            for hh in range(4):
                A("pool", lambda h, hh=hh: h.memset(Cst[:, hh * 4:(hh + 1) * 4, :], 0.0), wr=bC[hh * 4:(hh + 1) * 4] + list(extra))
            A("pool", lambda h: h.memset(nst[:], 0.0), wr=[bnst])
            A("pool", lambda h: h.memset(mst[:], 0.0), wr=[bmst])

        def Cv(hh, dt_):
            return Cst[:, hh * 4 + dt_, :]

        bsin = P.buf("summ_in")
        bsout = P.buf("summ_out")
        SROW = (NT + 1) * 128

        def phase_w1():
            W0ALL = [bWin0, bWin0u, bWin0z, bWout0]
            f_ = ld_w(Win1, b_w_in, 8, (0, E)); A("pool", f_, wr=[bWin1] + W0ALL, dma=f_.n, key="Win1")
            A("pool", lambda h: [h.dma_start(out=Wbd_t[:, i * 2048:(i + 1) * 2048], in_=wbd[:, i * 2048:(i + 1) * 2048]) for i in range(3)], wr=[bWbd, bvhat, btmpZ, btmpS], dma=3, key="Wbd")

            zero_state(extra=[bv_f32, bbiasT, bbiasTs, btmpU, btmpZ, btmpS, bWout0, *bxcTs, *bxmTs, bqT, bkT, bkw, *bvtoks, bCb[0], bCb[1]])
            A("pool", lambda h: h.memset(Bacc[:], 0.0), wr=[bBacc])
            A("sp", lambda h: h.dma_start(out=FGb[:, :], in_=fng.partition_broadcast(128)),
              wr=[bFGb, bWmT, bWmTs, bwtmp, bwtmp2], dma=1, key="FGb")


        def phase_A():
            import os
            KA = int(os.environ.get("KA", str(NT)))
            KSCAN = int(os.environ.get("KSCAN", "1"))
            l1_front(0, 128, 0, 1, 128, False, "zero", pre=False, nxt=128, halo_only=True)
            for i in range(KA):
                l1_front((i + 1) * 128, 128, (i + 1) % 2, 1, 128, False, "flag" if i == 0 else "copy",
                         pre=True, nxt=((i + 2) * 128 if i < KA - 1 else None))
                if KSCAN:
                    scan_chunk(0, 128, False, Cv, nst, bC, bnst, mst, bmst, acc_B=True)
                if i == 0:
                    W0ALL = [bWin0, bWin0u, bWin0z, bWout0]
                    f_ = ld_w(Win1, b_w_in, 8, (E, 2 * E)); A("pool", f_, wr=[bWin1z] + W0ALL, dma=f_.n, key="Win1z")
                    f_ = ld_w(Wout1, b_w_out, 4); A("pool", f_, wr=[bWout1] + W0ALL, dma=f_.n, key="Wout1")

        def phase_X():
            A("dve", lambda h: h.memset(misc[:], 0.0), wr=[bmisc])
            A("dve", lambda h: h.tensor_copy(out=misc[:, 0:16], in_=nst[:, :]), rd=[bnst], wr=[bmisc])
            A("dve", lambda h: h.tensor_copy(out=misc[:, 16:20], in_=mst[:, :]), rd=[bmst], wr=[bmisc])
            A("dve", lambda h: h.tensor_copy(out=misc[:, 20:24], in_=Bacc[:, :]), rd=[bBacc], wr=[bmisc])
            A("sp", lambda h: [h.dma_start(out=summ_in[hh][:, :].rearrange("(a p) e -> p a e", p=128), in_=Cst[:, hh * 4:(hh + 1) * 4, :])
                               for hh in range(4)], rd=bC, wr=[bsin], dma=4, key="summC")
            A("sp", lambda h: h.dma_start(out=summ_in_m[:, :], in_=misc[:, :]), rd=[bmisc], wr=[bsin], dma=1, key="summM")
            RG = [[0, 1, 2, 3], [4, 5, 6, 7]]
            for hh in range(4):
                A("pool", lambda h, hh=hh: h.collective_compute("AllGather", ALU.bypass, replica_groups=RG,
                                                                ins=[summ_in[hh].ap().opt()], outs=[summ_out[hh].ap().opt()]),
                  rd=[bsin], wr=[bsout], dma="cc", key="cc")
            A("pool", lambda h: h.collective_compute("AllGather", ALU.bypass, replica_groups=RG,
                                                     ins=[summ_in_m.ap().opt()], outs=[summ_out_m.ap().opt()]),
              rd=[bsin], wr=[bsout], dma="cc", key="cc")

        def phase_S():
            l1_front(SROW, 64, 0, 4, 16, True, "sample", conv_out=(convs[:, :], "convs"))
            sbanks = [bank(reserve=True) for _ in range(4)]
            for j in range(4):
                for hh in range(4):
                    A("sp", lambda h, j=j, hh=hh: h.dma_start(
                        out=Cst[:, hh * 4:(hh + 1) * 4, :],
                        in_=stC[j * 2048 + hh * 512:j * 2048 + (hh + 1) * 512, :].rearrange("(a p) e -> p a e", p=128)),
                      wr=bC[hh * 4:(hh + 1) * 4], dma=1, key=f"Cld{hh}")
                A("sp", lambda h, j=j: h.dma_start(out=nst[:, :], in_=stn[j, :, :]), wr=[bnst], dma=1, key="nld")
                A("sp", lambda h, j=j: h.dma_start(out=mst[:, :], in_=stm[j:j + 1, :].partition_broadcast(128)),
                  wr=[bmst], dma=1, key="mld")
                scan_chunk(16 * j, 16, True, Cv, nst, bC, bnst, mst, bmst)
                hn_transpose(sbanks, 16 * j, 16, 64)
                for hh in range(4):
                    A("sp", lambda h, j=j, hh=hh: h.dma_start(
                        out=Cs[j * 2048 + hh * 512:j * 2048 + (hh + 1) * 512, :].rearrange("(a p) e -> p a e", p=128),
                        in_=Cst[:, hh * 4:(hh + 1) * 4, :]), rd=bC[hh * 4:(hh + 1) * 4], dma=1, key=f"Cst_out{hh}")
                A("sp", lambda h, j=j: h.dma_start(out=ns_o[j, :, :], in_=nst[:, :]), rd=[bnst], dma=1, key="nst_out")
                A("sp", lambda h, j=j: h.dma_start(out=ms_o[j:j + 1, :], in_=mst[0:1, :]), rd=[bmst], dma=1, key="mst_out")
            l1_tail(sbanks, 64, 0, ys[:, :], "ys")
            for pb, _ in sbanks:
                release(pb)

        def phase_C():
            zero_state()
            for r in range(3):
                A("sp", lambda h, r=r: h.dma_start(out=misc[:, :], in_=summ_out_m[r * 128:(r + 1) * 128, :]),
                  rd=[bsout], wr=[bmisc], dma=1, key="miscld")
                pm = flg[:, 1 + r:2 + r]
                A("dve", lambda h, pm=pm: h.tensor_scalar(out=pmisc[:, 20:24], in0=misc[:, 20:24], scalar1=pm, scalar2=None, op0=ALU.mult),
                  rd=[bmisc, bflg], wr=[bpmisc])
                A("dve", lambda h, pm=pm: h.tensor_scalar(out=pmisc[:, 28:29], in0=pm, scalar1=-1.0, scalar2=1e30, op0=ALU.add, op1=ALU.mult),
                  rd=[bflg], wr=[bpmisc])
                A("dve", lambda h, pm=pm: h.tensor_scalar(out=pmisc[:, 24:28], in0=misc[:, 16:20], scalar1=pm, scalar2=pmisc[:, 28:29],
                                                          op0=ALU.mult, op1=ALU.add), rd=[bmisc, bflg, bpmisc], wr=[bpmisc])
                A("dve", lambda h: h.tensor_tensor(out=pmisc[:, 44:48], in0=pmisc[:, 20:24], in1=mst[:, :], op=ALU.add),
                  rd=[bpmisc, bmst], wr=[bpmisc])
                A("dve", lambda h: h.tensor_tensor(out=pmisc[:, 32:36], in0=pmisc[:, 44:48], in1=pmisc[:, 24:28], op=ALU.max),
                  rd=[bpmisc], wr=[bpmisc])
                A("dve", lambda h: h.tensor_tensor(out=pmisc[:, 44:48], in0=pmisc[:, 44:48], in1=pmisc[:, 32:36], op=ALU.subtract),
                  rd=[bpmisc], wr=[bpmisc])
                A("act", lambda h: h.activation(out=pmisc[:, 36:40], in_=pmisc[:, 44:48], func=AF.Exp), rd=[bpmisc], wr=[bpmisc])
                A("dve", lambda h: h.tensor_tensor(out=pmisc[:, 44:48], in0=pmisc[:, 24:28], in1=pmisc[:, 32:36], op=ALU.subtract),
                  rd=[bpmisc], wr=[bpmisc])
                A("act", lambda h: h.activation(out=pmisc[:, 40:44], in_=pmisc[:, 44:48], func=AF.Exp), rd=[bpmisc], wr=[bpmisc])
                A("dve", lambda h: h.tensor_copy(out=mst[:, :], in_=pmisc[:, 32:36]), rd=[bpmisc], wr=[bmst])
                for hh in range(4):
                    A("sp", lambda h, r=r, hh=hh: h.dma_start(
                        out=hn[:, :].rearrange("p (a e) -> p a e", a=4),
                        in_=summ_out[hh][r * 512:(r + 1) * 512, :].rearrange("(a p) e -> p a e", p=128)),
                      rd=[bsout], wr=[bhn], dma=1, key="Crld")
                    for dt_ in range(4):
                        A("act", lambda h, hh=hh, dt_=dt_: h.activation(out=Cv(hh, dt_), in_=Cv(hh, dt_), func=AF.Copy,
                                                                        scale=pmisc[:, 36 + hh:37 + hh]), rd=[bpmisc, bC[hh * 4 + dt_]], wr=[bC[hh * 4 + dt_]])
                        A("dve", lambda h, hh=hh, dt_=dt_: h.scalar_tensor_tensor(
                            out=Cv(hh, dt_), in0=hn[:, dt_ * 512:(dt_ + 1) * 512], scalar=pmisc[:, 40 + hh:41 + hh], in1=Cv(hh, dt_),
                            op0=ALU.mult, op1=ALU.add), rd=[bhn, bpmisc, bC[hh * 4 + dt_]], wr=[bC[hh * 4 + dt_]])
                    A("dve", lambda h, hh=hh: h.tensor_scalar(out=nst[:, hh * 4:(hh + 1) * 4], in0=nst[:, hh * 4:(hh + 1) * 4],
                                                              scalar1=pmisc[:, 36 + hh:37 + hh], scalar2=None, op0=ALU.mult),
                      rd=[bpmisc, bnst], wr=[bnst])
                    A("dve", lambda h, hh=hh: h.scalar_tensor_tensor(
                        out=nst[:, hh * 4:(hh + 1) * 4], in0=misc[:, hh * 4:(hh + 1) * 4], scalar=pmisc[:, 40 + hh:41 + hh],
                        in1=nst[:, hh * 4:(hh + 1) * 4], op0=ALU.mult, op1=ALU.add), rd=[bmisc, bpmisc, bnst], wr=[bnst])

        def phase_B():
            A("act", lambda h: h.activation(out=halo[:, :, 0:3], in_=halo0[:, :, :], func=AF.Copy), rd=[bhalo0], wr=[bhalo])
            for i in range(NT):
                pr = i % 2
                l1_front((i + 1) * 128, 128, pr, 1, 128, True, "flag" if i == 0 else "copy",
                         conv_out=(convp[:, :], "convp") if i == NT - 1 else None,
                         pre=("B0" if i == 0 else "B1"), nxt=((i + 2) * 128 if i < NT - 1 else None))
                scan_chunk(0, 128, True, Cv, nst, bC, bnst, mst, bmst)
                banks = [bank(reserve=True) for _ in range(4)]
                hn_transpose(banks, 0, 128, 128)
                l1_tail(banks, 128, pr, yp[i * 128:(i + 1) * 128, :], "yp")
                for pb, _ in banks:
                    release(pb)
            A("sp", lambda h: h.dma_start(out=Cp[:, :].rearrange("(a p) e -> p a e", p=128), in_=Cst[:, :, :]), rd=bC, dma=1, key="Cp")
            A("sp", lambda h: h.dma_start(out=np_o[:, :], in_=nst[:, :]), rd=[bnst], dma=1, key="np")
            A("sp", lambda h: h.dma_start(out=mp_o[:, :], in_=mst[0:1, :]), rd=[bmst], dma=1, key="mp")

        import os
        KSTOP = int(os.environ.get('KSTOP', '9'))
        for _k, _f in enumerate([phase_w1, phase_A, phase_X, phase_S, phase_C, phase_B]):
            if KSTOP > _k:
                _f()
        P.emit(nc, st)
    return nc


_NC_CACHE = {}


def kernel(**inputs):
    f = lambda k: np.ascontiguousarray(np.asarray(inputs[k], dtype=np.float32))
    x_prompt, x_sample = f("x_prompt"), f("x_sample")
    stC_, stn_, stm_, stcv_ = f("state_mlstm_C"), f("state_mlstm_n"), f("state_mlstm_m"), f("state_mlstm_conv")

    def cols(v, n):
        return np.ascontiguousarray(v.reshape(n, 128).T)

    ngc = np.concatenate([cols(f("norm_g")[0], 8), cols(f("norm_g")[1], 8)], axis=1)
    cw = f("b_conv_w")[0]
    b_cw = np.ascontiguousarray(cw.reshape(4, 16, 128).transpose(2, 1, 0).reshape(128, 64))

    def bdiag(w):
        out = np.zeros((16, 128, 128), np.float32)
        wr = w.reshape(16, 32, 4, 4)
        for n in range(32):
            out[:, 4 * n:4 * n + 4, 4 * n:4 * n + 4] = wr[:, n]
        return out
    bds = [bdiag(f(k)[0]) for k in ("b_wq", "b_wk", "b_wv")]
    wbd = np.ascontiguousarray(np.concatenate(bds, 0).transpose(1, 0, 2).reshape(128, 48 * 128))
    wbdT = np.ascontiguousarray(np.concatenate(bds, 0).transpose(2, 0, 1).reshape(128, 48 * 128))
    wg = f("b_w_gates")[0]
    b_wg = np.ascontiguousarray(wg.reshape(48, 128, 8).transpose(1, 0, 2).reshape(128, 48 * 8))
    shared = {
        "ngc": ngc, "fng": f("final_norm_g").reshape(1, DM),
        "a_w_in": f("a_w_in")[0], "a_lng": cols(f("a_ln_g")[0], 16), "a_lnb": cols(f("a_ln_b")[0], 16),
        "a_lng_row": f("a_ln_g")[0].reshape(1, E), "a_lnb_row": f("a_ln_b")[0].reshape(1, E),
        "a_ws": f("a_w_s")[0].reshape(8 * 128, 128), "a_bs": f("a_b_s")[0].reshape(1, 8 * 128),
        "a_w_out": f("a_w_out")[0], "b_w_in": f("b_w_in")[0], "b_cw": b_cw, "b_cb": cols(f("b_conv_b")[0], 16),
        "wbd": wbd, "wbdT": wbdT, "b_wg": b_wg, "b_bg": f("b_b_gates")[0].reshape(1, 8),
        "b_hg": cols(f("b_hnorm_g")[0], 16), "b_skip": cols(f("b_skip")[0], 16), "b_w_out": f("b_w_out")[0],
    }
    in_maps = []
    for c in range(8):
        b, g = c // 4, c % 4
        xpc = np.zeros(((NT + 1) * 128, DM), np.float32)
        lo = g * 2048 - 128
        if g == 0:
            xpc[128:] = x_prompt[b, 0:2048]
        else:
            xpc[:] = x_prompt[b, lo:lo + (NT + 1) * 128]
        fl = np.zeros((1, 8), np.float32)
        fl[0, 0] = 0.0 if g == 0 else 1.0
        for r in range(3):
            fl[0, 1 + r] = 1.0 if r < g else 0.0
        sl = slice(4 * c, 4 * c + 4)
        m = dict(shared)
        m.update({
            "xp": xpc, "xs": np.ascontiguousarray(x_sample[sl].reshape(64, DM)),
            "stC": np.ascontiguousarray(stC_[0, sl].reshape(16 * 512, 512)),
            "stn": np.ascontiguousarray(stn_[0, sl].reshape(4, 4, 4, 128).transpose(0, 3, 1, 2).reshape(4, 128, 16)),
            "stm": np.ascontiguousarray(stm_[0, sl].reshape(4, 4)),
            "stconv": np.ascontiguousarray(stcv_[0, sl].reshape(12, E)),
            "flags": fl,
        })
        in_maps.append(m)
    if "nc" not in _NC_CACHE:
        _NC_CACHE["nc"] = build_program()
    res = run_bass_kernel_spmd(_NC_CACHE["nc"], in_maps, core_ids=list(range(8)))
    R = res.results
    y_prompt = np.stack([np.concatenate([R[b * 4 + g]["yp"] for g in range(4)], 0) for b in range(2)]).astype(np.float32)
    y_sample = np.concatenate([R[c]["ys"].reshape(4, 16, DM) for c in range(8)], 0).astype(np.float32)
    sgu_v = np.concatenate([R[c]["sguv"].reshape(4, 16, E) for c in range(8)], 0)[None].astype(np.float32)

    def n_from(a):
        return a.reshape(128, 4, 4).transpose(1, 2, 0).reshape(4, 512)
    C_prompt = np.stack([R[b * 4 + 3]["Cp"].reshape(4, 512, 512) for b in range(2)])[None].astype(np.float32)
    n_prompt = np.stack([n_from(R[b * 4 + 3]["np_o"]) for b in range(2)])[None].astype(np.float32)
    m_prompt = np.stack([R[b * 4 + 3]["mp_o"].reshape(4) for b in range(2)])[None].astype(np.float32)
    conv_prompt = np.stack([R[b * 4 + 3]["convp"].reshape(3, E) for b in range(2)])[None].astype(np.float32)
    C_sample = np.concatenate([R[c]["Cs"].reshape(4, 4, 512, 512) for c in range(8)], 0)[None].astype(np.float32)
    n_sample = np.concatenate([np.stack([n_from(R[c]["ns_o"][j]) for j in range(4)]) for c in range(8)], 0)[None].astype(np.float32)
    m_sample = np.concatenate([R[c]["ms_o"].reshape(4, 4) for c in range(8)], 0)[None].astype(np.float32)
    conv_sample = np.concatenate([R[c]["convs"].reshape(4, 3, E) for c in range(8)], 0)[None].astype(np.float32)
    return (y_prompt, y_sample, sgu_v, C_prompt, n_prompt, m_prompt, conv_prompt,
            C_sample, n_sample, m_sample, conv_sample)
```
